# Optimizing a Trainium2 kernel written in Bass

```python
import math
import jax, jax.numpy as jnp
from jax import lax
import numpy as np

D_MODEL = 2048
BATCH = 1
SEQ = 8192
DEPTH = 4

S5_WIDTH = D_MODEL // 2
S5_GROUP = 16
S5_GROUPS = S5_WIDTH // S5_GROUP
S5_STATE = 64
S5_DT_MIN = 0.001
S5_DT_MAX = 0.1
DN_HEADS = 8
DN_DK = 128
DN_DV = 128
DN_QK_WIDTH = DN_HEADS * DN_DK
DN_V_WIDTH = DN_HEADS * DN_DV
DN_CONV = 4
DN_CHUNK = 64
DN_DT_MIN = 0.001
DN_DT_MAX = 0.1
FFN_DIM = 5632
FFN_CONV = 3
NORM_EPS = 1e-6

OFF_U = S5_WIDTH
OFF_QKV = OFF_U + 2 * DN_QK_WIDTH + DN_V_WIDTH
OFF_Z = OFF_QKV + DN_V_WIDTH
OFF_BETA = OFF_Z + DN_HEADS
OFF_ALPHA = OFF_BETA + DN_HEADS
OFF_GS = OFF_ALPHA + D_MODEL
N_IN = OFF_GS + D_MODEL
SPLITS = [OFF_U, OFF_QKV, OFF_Z, OFF_BETA, OFF_ALPHA, OFF_GS]

kernel_name = "hybrid_s5_gdn_convffn"


def rmsnorm(x, w):
    xf = x.astype(jnp.float32)
    y = xf * lax.rsqrt(jnp.mean(xf * xf, axis=-1, keepdims=True) + NORM_EPS) * w.astype(jnp.float32)
    return y.astype(x.dtype)


def l2norm(x):
    return x * lax.rsqrt(jnp.sum(x * x, axis=-1, keepdims=True) + NORM_EPS)


def causal_dwconv(x, w):
    K, C = w.shape
    return lax.conv_general_dilated(
        x, w[:, None, :].astype(x.dtype), window_strides=(1,), padding=[(K - 1, 0)],
        dimension_numbers=("NWC", "WIO", "NWC"), feature_group_count=C)


def s5_branch(u, log_dt, a_re, a_im, b_re, b_im, c_re, c_im, d):
    Bn, L, _ = u.shape
    f32 = jnp.float32
    uf = u.astype(f32)
    ug = uf.reshape(Bn, L, S5_GROUPS, S5_GROUP)
    lr, li = a_re.astype(f32), a_im.astype(f32)
    dt = jnp.exp(log_dt.astype(f32))[:, None]
    mag = jnp.exp(lr * dt)
    abar_re, abar_im = mag * jnp.cos(li * dt), mag * jnp.sin(li * dt)
    den = lr * lr + li * li
    nr, ni = abar_re - 1.0, abar_im
    coef_re = (nr * lr + ni * li) / den
    coef_im = (ni * lr - nr * li) / den
    br, bi = b_re.astype(f32), b_im.astype(f32)
    bbar_re = coef_re[..., None] * br - coef_im[..., None] * bi
    bbar_im = coef_re[..., None] * bi + coef_im[..., None] * br
    bu_re = jnp.einsum("gph,blgh->blgp", bbar_re, ug)
    bu_im = jnp.einsum("gph,blgh->blgp", bbar_im, ug)
    ar_t = jnp.broadcast_to(abar_re, bu_re.shape)
    ai_t = jnp.broadcast_to(abar_im, bu_re.shape)

    def combine(e1, e2):
        a1r, a1i, b1r, b1i = e1
        a2r, a2i, b2r, b2i = e2
        return (a2r * a1r - a2i * a1i,
                a2r * a1i + a2i * a1r,
                a2r * b1r - a2i * b1i + b2r,
                a2r * b1i + a2i * b1r + b2i)

    _, _, xr, xi = lax.associative_scan(combine, (ar_t, ai_t, bu_re, bu_im), axis=1)
    y = (jnp.einsum("ghp,blgp->blgh", c_re.astype(f32), xr)
         - jnp.einsum("ghp,blgp->blgh", c_im.astype(f32), xi))
    y = y.reshape(Bn, L, S5_WIDTH) + d.astype(f32) * uf
    return jax.nn.gelu(y, approximate=False).astype(u.dtype)


def chunk_gated_delta_rule(q, k, v, g, beta):
    Bn, L, H, DK = q.shape
    DV = v.shape[-1]
    N, C = L // DN_CHUNK, DN_CHUNK

    def chunks(t):
        return t.reshape(Bn, N, C, H, -1).transpose(1, 0, 3, 2, 4)

    qc, kc, vc = chunks(q), chunks(k), chunks(v)
    bc = beta.reshape(Bn, N, C, H).transpose(1, 0, 3, 2)
    gc = jnp.cumsum(g.reshape(Bn, N, C, H).transpose(1, 0, 3, 2), axis=-1)
    tril = jnp.tril(jnp.ones((C, C), dtype=bool))
    strict = jnp.tril(jnp.ones((C, C), dtype=bool), -1)
    decay = jnp.exp(jnp.where(tril, gc[..., :, None] - gc[..., None, :], -jnp.inf))
    k_beta = kc * bc[..., None]
    v_beta = vc * bc[..., None]
    lmat = jnp.where(strict, jnp.einsum("nbhcd,nbhsd->nbhcs", k_beta, kc) * decay, 0.0)
    eye = jnp.eye(C, dtype=q.dtype)
    rhs = jnp.concatenate([v_beta, k_beta * jnp.exp(gc)[..., None]], axis=-1)
    sol = lax.linalg.triangular_solve(eye + lmat, rhs, left_side=True, lower=True, unit_diagonal=True)
    u_c, w_c = sol[..., :DV], sol[..., DV:]
    attn = jnp.where(tril, jnp.einsum("nbhcd,nbhsd->nbhcs", qc, kc) * decay, 0.0)

    def step(S, inp):
        q_i, k_i, u_i, w_i, g_i, a_i = inp
        v_new = u_i - jnp.einsum("bhcd,bhdv->bhcv", w_i, S)
        o_i = (jnp.einsum("bhcd,bhdv->bhcv", q_i * jnp.exp(g_i)[..., None], S)
               + jnp.einsum("bhcs,bhsv->bhcv", a_i, v_new))
        g_last = g_i[..., -1]
        S = (S * jnp.exp(g_last)[..., None, None]
             + jnp.einsum("bhcd,bhcv->bhdv", k_i * jnp.exp(g_last[..., None] - g_i)[..., None], v_new))
        return S, o_i

    S0 = jnp.zeros((Bn, H, DK, DV), q.dtype)
    _, o = lax.scan(step, S0, (qc, kc, u_c, w_c, gc, attn))
    return o.transpose(1, 0, 3, 2, 4).reshape(Bn, L, H, DV)


def deltanet_branch(qkv, z, b_raw, a_raw, conv_w, a_log, dt_bias, norm_w):
    Bn, L, _ = qkv.shape
    f32 = jnp.float32
    qkv = jax.nn.silu(causal_dwconv(qkv, conv_w))
    q, k, v = jnp.split(qkv, [DN_QK_WIDTH, 2 * DN_QK_WIDTH], axis=-1)
    q = l2norm(q.reshape(Bn, L, DN_HEADS, DN_DK).astype(f32)) * (DN_DK ** -0.5)
    k = l2norm(k.reshape(Bn, L, DN_HEADS, DN_DK).astype(f32))
    v = v.reshape(Bn, L, DN_HEADS, DN_DV).astype(f32)
    beta = jax.nn.sigmoid(b_raw.astype(f32))
    g = -jnp.exp(a_log.astype(f32)) * jax.nn.softplus(a_raw.astype(f32) + dt_bias.astype(f32))
    o = chunk_gated_delta_rule(q, k, v, g, beta)
    o = rmsnorm(o, norm_w) * jax.nn.silu(z.reshape(Bn, L, DN_HEADS, DN_DV).astype(f32))
    return o.reshape(Bn, L, DN_V_WIDTH).astype(qkv.dtype)


def setup_inputs(seed: int = 0) -> dict:
    key = jax.random.key(seed)
    ks = jax.random.split(key, 26)
    f32 = jnp.float32

    def nrm(k, shape, scale):
        return jax.random.normal(k, shape, f32) * scale

    G, P, HG = S5_GROUPS, S5_STATE, S5_GROUP
    x = nrm(ks[0], (BATCH, SEQ, D_MODEL), 1.0)
    mix_norm_w = 1.0 + nrm(ks[1], (DEPTH, D_MODEL), 0.02)
    w_in = nrm(ks[2], (DEPTH, D_MODEL, N_IN), D_MODEL ** -0.5)
    s5_log_dt = jax.random.uniform(ks[3], (DEPTH, G), f32, math.log(S5_DT_MIN), math.log(S5_DT_MAX))
    s5_a_re = -0.5 + nrm(ks[4], (DEPTH, G, P), 0.01)
    s5_a_im = math.pi * jnp.arange(P, dtype=f32) + nrm(ks[5], (DEPTH, G, P), 0.01)
    s5_b_re = nrm(ks[6], (DEPTH, G, P, HG), (2 * HG) ** -0.5)
    s5_b_im = nrm(ks[7], (DEPTH, G, P, HG), (2 * HG) ** -0.5)
    s5_c_re = nrm(ks[8], (DEPTH, G, HG, P), (2 * P) ** -0.5)
    s5_c_im = nrm(ks[9], (DEPTH, G, HG, P), (2 * P) ** -0.5)
    s5_d = nrm(ks[10], (DEPTH, S5_WIDTH), 1.0)
    s5_glu_w = nrm(ks[11], (DEPTH, S5_WIDTH, 2 * D_MODEL), S5_WIDTH ** -0.5)
    dn_conv_w = nrm(ks[12], (DEPTH, DN_CONV, 2 * DN_QK_WIDTH + DN_V_WIDTH), DN_CONV ** -0.5)
    dn_a_log = jnp.log(jax.random.uniform(ks[13], (DEPTH, DN_HEADS), f32, 1.0, 16.0))
    dn_dt = jnp.exp(jax.random.uniform(ks[14], (DEPTH, DN_HEADS), f32, math.log(DN_DT_MIN), math.log(DN_DT_MAX)))
    dn_dt_bias = dn_dt + jnp.log(-jnp.expm1(-dn_dt))
    dn_norm_w = 1.0 + nrm(ks[15], (DEPTH, DN_DV), 0.02)
    dn_proj_w = nrm(ks[16], (DEPTH, DN_V_WIDTH, D_MODEL), DN_V_WIDTH ** -0.5)
    w_out = nrm(ks[17], (DEPTH, D_MODEL, D_MODEL), D_MODEL ** -0.5)
    ffn_norm_w = 1.0 + nrm(ks[18], (DEPTH, D_MODEL), 0.02)
    ffn_up = nrm(ks[19], (DEPTH, D_MODEL, 2 * FFN_DIM), D_MODEL ** -0.5)
    ffn_conv_w = nrm(ks[20], (DEPTH, FFN_CONV, 2 * FFN_DIM), FFN_CONV ** -0.5)
    ffn_down = nrm(ks[21], (DEPTH, FFN_DIM, D_MODEL), FFN_DIM ** -0.5)
    final_norm_w = 1.0 + nrm(ks[22], (D_MODEL,), 0.02)
    return {"x": x, "mix_norm_w": mix_norm_w, "w_in": w_in, "s5_log_dt": s5_log_dt,
            "s5_a_re": s5_a_re, "s5_a_im": s5_a_im, "s5_b_re": s5_b_re, "s5_b_im": s5_b_im,
            "s5_c_re": s5_c_re, "s5_c_im": s5_c_im, "s5_d": s5_d, "s5_glu_w": s5_glu_w,
            "dn_conv_w": dn_conv_w, "dn_a_log": dn_a_log, "dn_dt_bias": dn_dt_bias,
            "dn_norm_w": dn_norm_w, "dn_proj_w": dn_proj_w, "w_out": w_out,
            "ffn_norm_w": ffn_norm_w, "ffn_up": ffn_up, "ffn_conv_w": ffn_conv_w,
            "ffn_down": ffn_down, "final_norm_w": final_norm_w}


def reference(x, mix_norm_w, w_in, s5_log_dt, s5_a_re, s5_a_im, s5_b_re, s5_b_im,
              s5_c_re, s5_c_im, s5_d, s5_glu_w, dn_conv_w, dn_a_log, dn_dt_bias,
              dn_norm_w, dn_proj_w, w_out, ffn_norm_w, ffn_up, ffn_conv_w, ffn_down,
              final_norm_w):
    for l in range(DEPTH):
        h = rmsnorm(x, mix_norm_w[l])
        proj = jnp.einsum("bld,de->ble", h, w_in[l])
        u, qkv, z, b_raw, a_raw, g_s5, g_dn = jnp.split(proj, SPLITS, axis=-1)
        y_s5 = s5_branch(u, s5_log_dt[l], s5_a_re[l], s5_a_im[l], s5_b_re[l], s5_b_im[l],
                         s5_c_re[l], s5_c_im[l], s5_d[l])
        glu_a, glu_b = jnp.split(jnp.einsum("blc,ce->ble", y_s5, s5_glu_w[l]), 2, axis=-1)
        br_s5 = glu_a * jax.nn.sigmoid(glu_b)
        y_dn = deltanet_branch(qkv, z, b_raw, a_raw, dn_conv_w[l], dn_a_log[l], dn_dt_bias[l], dn_norm_w[l])
        br_dn = jnp.einsum("blc,cd->bld", y_dn, dn_proj_w[l])
        merged = jax.nn.sigmoid(g_s5) * br_s5 + jax.nn.sigmoid(g_dn) * br_dn
        x = x + jnp.einsum("bld,de->ble", merged, w_out[l])
        h = rmsnorm(x, ffn_norm_w[l])
        up = causal_dwconv(jnp.einsum("bld,df->blf", h, ffn_up[l]), ffn_conv_w[l])
        act, val = jnp.split(up, 2, axis=-1)
        x = x + jnp.einsum("blf,fd->bld", jax.nn.silu(act) * val, ffn_down[l])
    return rmsnorm(x, final_norm_w)
```

```python
import math
import numpy as np
import concourse.bass as bass
import concourse.mybir as mybir
from concourse.bass_utils import run_bass_kernel_spmd
from contextlib import ExitStack

F32 = mybir.dt.float32
BF16 = mybir.dt.bfloat16
AF = mybir.ActivationFunctionType
ALU = mybir.AluOpType


class Buf:
    __slots__ = ("name", "w", "r")

    def __init__(self, name):
        self.name = name
        self.w = None
        self.r = {}


class Trk:
    SEM_ROLL = 20000

    def __init__(self, nc, stack, n_dma_sems=12):
        self.nc, self.stack = nc, stack
        self.eng = {"pe": nc.tensor, "act": nc.scalar, "dve": nc.vector,
                    "pool": nc.gpsimd, "sp": nc.sync}
        self.sem, self.cnt, self.seen = {}, {}, {}
        self.nsem = 0
        for e in self.eng:
            self.seen[e] = {}
        for e in ("pe", "act", "dve", "pool"):
            self._newsem(e)
        self.dma = {}
        for q in ("sp", "pool"):
            self.dma[q] = [[self._alloc(f"d_{q}{i}"), 0] for i in range(n_dma_sems)]
        self.dma_i = {"sp": 0, "pool": 0}
        self.ninst = 0

    def _alloc(self, name):
        self.nsem += 1
        return self.stack.enter_context(self.nc.semaphore(f"{name}_{self.nsem}"))

    def _newsem(self, e):
        self.sem[e] = self._alloc(f"c_{e}")
        self.cnt[e] = 0

    def _wait(self, e, deps, skip_same_pe=False):
        eng = self.eng[e]
        best = {}
        for d in deps:
            if d is None:
                continue
            sem, val, src = d
            if skip_same_pe and src == "pe" and e == "pe":
                continue
            k = id(sem)
            if k not in best or best[k][1] < val:
                best[k] = d
        for k, (sem, val, src) in best.items():
            if self.seen[e].get(k, 0) >= val:
                continue
            eng.wait_ge(sem, val)
            self.seen[e][k] = val

    def _deps(self, reads, writes):
        deps = []
        for b in reads:
            deps.append(b.w)
        for b in writes:
            deps.append(b.w)
            deps.extend(b.r.values())
        return deps

    def _record(self, dep, reads, writes):
        k = id(dep[0])
        for b in reads:
            b.r[k] = dep
        for b in writes:
            b.w = dep
            b.r = {}

    def op(self, e, fn, reads=(), writes=(), acc=False):
        self._wait(e, self._deps(reads, writes), skip_same_pe=acc)
        inst = fn(self.eng[e])
        if self.cnt[e] >= self.SEM_ROLL:
            self._newsem(e)
        self.cnt[e] += 1
        inst.then_inc(self.sem[e], 1)
        dep = (self.sem[e], self.cnt[e], e)
        self._record(dep, reads, writes)
        self.ninst += 1
        return dep

    def dma_op(self, q, out, in_, reads=(), writes=(), **kw):
        slots = self.dma[q]
        i = self.dma_i[q]
        self.dma_i[q] = (i + 1) % len(slots)
        slot = slots[i]
        deps = self._deps(reads, writes)
        if slot[1] > 0:
            deps.append((slot[0], slot[1], "dma"))
        self._wait(q, deps)
        slot[1] += 16
        self.eng[q].dma_start(out=out, in_=in_, **kw).then_inc(slot[0], 16)
        dep = (slot[0], slot[1], "dma")
        self._record(dep, reads, writes)
        self.ninst += 1
        return dep

    def finish(self, bufs):
        deps = [b.w for b in bufs]
        for e in ("pe", "act", "dve", "pool"):
            if self.cnt[e]:
                deps.append((self.sem[e], self.cnt[e], e))
        for q in self.dma:
            for s, v in self.dma[q]:
                if v:
                    deps.append((s, v, "dma"))
        self._wait("sp", deps)


D = 2048
KC = D // 128
EPS = 1e-6


def emit_rmsnorm(t, nc, st, xT, bx, nw, bnw, hT, bh, ones_bf, bones, L, tag):
    sq = st.enter_context(nc.sbuf_tensor(f"sq_{tag}", [128, 2, 512], BF16))
    rs = st.enter_context(nc.sbuf_tensor(f"rs_{tag}", [128, 512], F32))
    ps = st.enter_context(nc.psum_tensor(f"psn_{tag}", [128, 512], F32))
    bsq = [Buf("sq0"), Buf("sq1")]
    brs, bps = Buf("rs"), Buf("psn")
    for tb in range(L // 512):
        ts = slice(tb * 512, (tb + 1) * 512)
        for kc in range(KC):
            j = kc % 2
            t.op("act", lambda e, kc=kc, j=j: e.activation(out=sq[:, j, :], in_=xT[:, kc, ts], func=AF.Square),
                 reads=[bx], writes=[bsq[j]])
            t.op("pe", lambda e, kc=kc, j=j: e.matmul(ps[:], lhsT=ones_bf[:], rhs=sq[:, j, :],
                                                      start=(kc == 0), stop=(kc == KC - 1)),
                 reads=[bsq[j], bones], writes=[bps], acc=(kc > 0))
        t.op("dve", lambda e: e.tensor_scalar(out=rs[:], in0=ps[:], scalar1=1.0 / D, scalar2=EPS,
                                              op0=ALU.mult, op1=ALU.add), reads=[bps], writes=[brs])
        t.op("act", lambda e: e.activation(out=rs[:], in_=rs[:], func=AF.Sqrt), reads=[brs], writes=[brs])
        t.op("dve", lambda e: e.reciprocal(out=rs[:], in_=rs[:]), reads=[brs], writes=[brs])
        for kc in range(KC):
            t.op("dve", lambda e, kc=kc: e.scalar_tensor_tensor(out=hT[:, kc, ts], in0=xT[:, kc, ts],
                                                                scalar=nw[:, kc:kc + 1], in1=rs[:],
                                                                op0=ALU.mult, op1=ALU.mult),
                 reads=[bx, bnw, brs], writes=[bh])


I32 = mybir.dt.int32
T = 16
TWO_PI = 2.0 * math.pi


def build_S5(L):
    NCH = L // T
    NJ = int(math.log2(NCH))
    NS = [float(n) for n in range(17)] + [float(16 * 2 ** j) for j in range(1, NJ)]
    NP = len(NS)
    pidx = {int(n): i for i, n in enumerate(NS)}
    nc = bass.Bass("TRN2", target_bir_lowering=False)
    din = lambda n, s, d=F32: nc.dram_tensor(n, s, d, kind="ExternalInput").ap()
    d_u = din("u", [128, L], BF16)
    d_lr, d_li, d_ldt = din("lr", [128, 4]), din("li", [128, 4]), din("ldt", [128, 4])
    d_zb = [din("zb_re", [128, 4, 128]), din("zb_im", [128, 4, 128])]
    d_zc = [din("zc_re", [128, 4, 128]), din("zc_im", [128, 4, 128])]
    d_d = din("dskip", [128, 1])
    d_ns = din("ns", [128, NP])
    d_id = din("ident", [128, 128])
    d_y = nc.dram_tensor("y", [128, L], BF16, kind="ExternalOutput").ap()
    with ExitStack() as st:
        t = Trk(nc, st)
        B = {}

        def sb(n, s, d=F32):
            B[n] = Buf(n)
            return st.enter_context(nc.sbuf_tensor("s_" + n, s, d))

        def ps(n, s):
            B[n] = Buf(n)
            full = st.enter_context(nc.psum_tensor("p_" + n, [128, 512], F32))
            return full[:, :s[1]] if s[1] != 512 else full

        def dve(fn, r, w): t.op("dve", fn, reads=[B[x] for x in r], writes=[B[x] for x in w])
        def act(fn, r, w): t.op("act", fn, reads=[B[x] for x in r], writes=[B[x] for x in w])
        def pool(fn, r, w): t.op("pool", fn, reads=[B[x] for x in r], writes=[B[x] for x in w])
        def mm(o, on, l, ln, r, rn, start=True, stop=True, acc=False):
            t.op("pe", lambda e: e.matmul(o, lhsT=l, rhs=r, start=start, stop=stop), reads=[B[ln], B[rn]], writes=[B[on]], acc=acc)

        u32 = sb("u32", [128, L]); ub = sb("ub", [128, L], BF16)
        lr, li, ldt = sb("lr", [128, 4]), sb("li", [128, 4]), sb("ldt", [128, 4])
        zb = [sb("zb0", [128, 4, 128]), sb("zb1", [128, 4, 128])]
        zc = [sb("zc0", [128, 4, 128]), sb("zc1", [128, 4, 128])]
        dsk = sb("dsk", [128, 1]); ns = sb("ns", [128, NP]); ident = sb("ident", [128, 128])
        for tl, d, n in [(ub, d_u, "ub"), (lr, d_lr, "lr"), (li, d_li, "li"), (ldt, d_ldt, "ldt"), (zb[0], d_zb[0], "zb0"), (zb[1], d_zb[1], "zb1"),
                         (zc[0], d_zc[0], "zc0"), (zc[1], d_zc[1], "zc1"), (dsk, d_d, "dsk"), (ns, d_ns, "ns"), (ident, d_id, "ident")]:
            t.dma_op("sp", tl[:], d, writes=[B[n]])
        dve(lambda e: e.tensor_copy(out=u32[:], in_=ub[:]), ["ub"], ["u32"])

        dt_ = sb("dt", [128, 4]); lrdt = sb("lrdt", [128, 4]); th = sb("th", [128, 4])
        act(lambda e: e.activation(out=dt_[:], in_=ldt[:], func=AF.Exp), ["ldt"], ["dt"])
        dve(lambda e: e.tensor_tensor(out=lrdt[:], in0=lr[:], in1=dt_[:], op=ALU.mult), ["lr", "dt"], ["lrdt"])
        dve(lambda e: e.tensor_tensor(out=th[:], in0=li[:], in1=dt_[:], op=ALU.mult), ["li", "dt"], ["th"])
        pw = [sb("pw_re", [128, 4, NP]), sb("pw_im", [128, 4, NP])]
        mag = sb("mag", [128, 4, NP]); ph = sb("ph", [128, 4, NP]); kq = sb("kq", [128, 4, NP]); ki = sb("ki", [128, 4, NP], I32)
        rr = sb("rr", [128, 4, NP]); sn = sb("sn", [128, 4, NP]); cs = sb("cs", [128, 4, NP])
        for pr in range(4):
            dve(lambda e, pr=pr: e.tensor_scalar(out=mag[:, pr, :], in0=ns[:], scalar1=lrdt[:, pr:pr + 1], scalar2=None, op0=ALU.mult), ["ns", "lrdt", "mag"], ["mag"])
            dve(lambda e, pr=pr: e.tensor_scalar(out=ph[:, pr, :], in0=ns[:], scalar1=th[:, pr:pr + 1], scalar2=None, op0=ALU.mult), ["ns", "th", "ph"], ["ph"])
        act(lambda e: e.activation(out=mag[:], in_=mag[:], func=AF.Exp), ["mag"], ["mag"])
        dve(lambda e: e.tensor_scalar(out=kq[:], in0=ph[:], scalar1=1.0 / TWO_PI, scalar2=None, op0=ALU.mult), ["ph"], ["kq"])
        dve(lambda e: e.tensor_copy(out=ki[:], in_=kq[:]), ["kq"], ["ki"])
        dve(lambda e: e.tensor_copy(out=kq[:], in_=ki[:]), ["ki"], ["kq"])
        dve(lambda e: e.scalar_tensor_tensor(out=rr[:], in0=kq[:], scalar=-TWO_PI, in1=ph[:], op0=ALU.mult, op1=ALU.add), ["kq", "ph"], ["rr"])
        msk = sb("msk", [128, 4, NP])

        def wrap(dst, dn, src, srcn, shift):
            dve(lambda e: e.tensor_scalar(out=dst[:], in0=src[:], scalar1=shift, scalar2=None, op0=ALU.add), [srcn], [dn])
            dve(lambda e: e.tensor_scalar(out=msk[:], in0=dst[:], scalar1=math.pi, scalar2=None, op0=ALU.is_gt), [dn], ["msk"])
            dve(lambda e: e.scalar_tensor_tensor(out=dst[:], in0=msk[:], scalar=-TWO_PI, in1=dst[:], op0=ALU.mult, op1=ALU.add), ["msk", dn], [dn])
            dve(lambda e: e.tensor_scalar(out=msk[:], in0=dst[:], scalar1=-math.pi, scalar2=None, op0=ALU.is_lt), [dn], ["msk"])
            dve(lambda e: e.scalar_tensor_tensor(out=dst[:], in0=msk[:], scalar=TWO_PI, in1=dst[:], op0=ALU.mult, op1=ALU.add), ["msk", dn], [dn])

        wrap(sn, "sn", rr, "rr", 0.0)
        wrap(cs, "cs", rr, "rr", math.pi / 2)
        act(lambda e: e.activation(out=sn[:], in_=sn[:], func=AF.Sin), ["sn"], ["sn"])
        act(lambda e: e.activation(out=cs[:], in_=cs[:], func=AF.Sin), ["cs"], ["cs"])
        dve(lambda e: e.tensor_tensor(out=pw[0][:], in0=mag[:], in1=cs[:], op=ALU.mult), ["mag", "cs"], ["pw_re"])
        dve(lambda e: e.tensor_tensor(out=pw[1][:], in0=mag[:], in1=sn[:], op=ALU.mult), ["mag", "sn"], ["pw_im"])
        i1 = pidx[1]
        nr = sb("nr", [128, 4]); den = sb("den", [128, 4]); tmp = sb("tmp", [128, 4]); cre = sb("cre", [128, 4]); cim = sb("cim", [128, 4]); ncim = sb("ncim", [128, 4])
        dve(lambda e: e.tensor_scalar(out=nr[:], in0=pw[0][:, :, i1], scalar1=-1.0, scalar2=None, op0=ALU.add), ["pw_re"], ["nr"])
        dve(lambda e: e.tensor_tensor(out=den[:], in0=lr[:], in1=lr[:], op=ALU.mult), ["lr"], ["den"])
        dve(lambda e: e.tensor_tensor(out=tmp[:], in0=li[:], in1=li[:], op=ALU.mult), ["li"], ["tmp"])
        dve(lambda e: e.tensor_tensor(out=den[:], in0=den[:], in1=tmp[:], op=ALU.add), ["den", "tmp"], ["den"])
        dve(lambda e: e.reciprocal(out=den[:], in_=den[:]), ["den"], ["den"])
        dve(lambda e: e.tensor_tensor(out=cre[:], in0=nr[:], in1=lr[:], op=ALU.mult), ["nr", "lr"], ["cre"])
        dve(lambda e: e.tensor_tensor(out=tmp[:], in0=pw[1][:, :, i1], in1=li[:], op=ALU.mult), ["pw_im", "li"], ["tmp"])
        dve(lambda e: e.tensor_tensor(out=cre[:], in0=cre[:], in1=tmp[:], op=ALU.add), ["cre", "tmp"], ["cre"])
        dve(lambda e: e.tensor_tensor(out=cre[:], in0=cre[:], in1=den[:], op=ALU.mult), ["cre", "den"], ["cre"])
        dve(lambda e: e.tensor_tensor(out=cim[:], in0=pw[1][:, :, i1], in1=lr[:], op=ALU.mult), ["pw_im", "lr"], ["cim"])
        dve(lambda e: e.tensor_tensor(out=tmp[:], in0=nr[:], in1=li[:], op=ALU.mult), ["nr", "li"], ["tmp"])
        dve(lambda e: e.tensor_tensor(out=cim[:], in0=cim[:], in1=tmp[:], op=ALU.subtract), ["cim", "tmp"], ["cim"])
        dve(lambda e: e.tensor_tensor(out=cim[:], in0=cim[:], in1=den[:], op=ALU.mult), ["cim", "den"], ["cim"])
        dve(lambda e: e.tensor_scalar(out=ncim[:], in0=cim[:], scalar1=-1.0, scalar2=None, op0=ALU.mult), ["cim"], ["ncim"])
        bb = [sb("bb0", [128, 4, 128]), sb("bb1", [128, 4, 128])]
        for pr in range(4):
            s_ = slice(pr, pr + 1)
            dve(lambda e, pr=pr, s_=s_: e.tensor_scalar(out=bb[0][:, pr, :], in0=zb[0][:, pr, :], scalar1=cre[:, s_], scalar2=None, op0=ALU.mult), ["zb0", "cre", "bb0"], ["bb0"])
            dve(lambda e, pr=pr, s_=s_: e.scalar_tensor_tensor(out=bb[0][:, pr, :], in0=zb[1][:, pr, :], scalar=ncim[:, s_], in1=bb[0][:, pr, :], op0=ALU.mult, op1=ALU.add), ["zb1", "ncim", "bb0"], ["bb0"])
            dve(lambda e, pr=pr, s_=s_: e.tensor_scalar(out=bb[1][:, pr, :], in0=zb[1][:, pr, :], scalar1=cre[:, s_], scalar2=None, op0=ALU.mult), ["zb1", "cre", "bb1"], ["bb1"])
            dve(lambda e, pr=pr, s_=s_: e.scalar_tensor_tensor(out=bb[1][:, pr, :], in0=zb[0][:, pr, :], scalar=cim[:, s_], in1=bb[1][:, pr, :], op0=ALU.mult, op1=ALU.add), ["zb0", "cim", "bb1"], ["bb1"])
        npw_im = sb("npw_im", [128, 4, NP])
        dve(lambda e: e.tensor_scalar(out=npw_im[:], in0=pw[1][:], scalar1=-1.0, scalar2=None, op0=ALU.mult), ["pw_im"], ["npw_im"])
        nzc1 = sb("nzc1", [128, 4, 128])
        dve(lambda e: e.tensor_scalar(out=nzc1[:], in0=zc[1][:], scalar1=-1.0, scalar2=None, op0=ALU.mult), ["zc1"], ["nzc1"])

        Kt = sb("Kt", [128, T, 128], BF16)
        Wm = sb("Wm", [128, T, 4, 2, 128], BF16)
        Cm = sb("Cm", [128, T, 4, 2, 128], BF16)
        en = [sb("en0", [128, 4, 128]), sb("en1", [128, 4, 128])]
        pk = ps("pk", [128, 128]); ptr = [ps("ptr0", [128, 128]), ps("ptr1", [128, 128])]
        for n in range(T):
            ix = pidx[n]
            for pr in range(4):
                a_re, a_im, na_im = pw[0][:, pr, ix:ix + 1], pw[1][:, pr, ix:ix + 1], npw_im[:, pr, ix:ix + 1]
                dve(lambda e, pr=pr, a_re=a_re: e.tensor_scalar(out=en[0][:, pr, :], in0=bb[0][:, pr, :], scalar1=a_re, scalar2=None, op0=ALU.mult), ["bb0", "pw_re", "en0"], ["en0"])
                dve(lambda e, pr=pr, na_im=na_im: e.scalar_tensor_tensor(out=en[0][:, pr, :], in0=bb[1][:, pr, :], scalar=na_im, in1=en[0][:, pr, :], op0=ALU.mult, op1=ALU.add), ["bb1", "npw_im", "en0"], ["en0"])
                dve(lambda e, pr=pr, a_re=a_re: e.tensor_scalar(out=en[1][:, pr, :], in0=bb[1][:, pr, :], scalar1=a_re, scalar2=None, op0=ALU.mult), ["bb1", "pw_re", "en1"], ["en1"])
                dve(lambda e, pr=pr, a_im=a_im: e.scalar_tensor_tensor(out=en[1][:, pr, :], in0=bb[0][:, pr, :], scalar=a_im, in1=en[1][:, pr, :], op0=ALU.mult, op1=ALU.add), ["bb0", "pw_im", "en1"], ["en1"])
            for pr in range(4):
                mm(pk, "pk", en[0][:, pr, :], "en0", zc[0][:, pr, :], "zc0", start=(pr == 0), stop=False, acc=(pr > 0))
                mm(pk, "pk", en[1][:, pr, :], "en1", nzc1[:, pr, :], "nzc1", start=False, stop=(pr == 3), acc=True)
            act(lambda e, n=n: e.copy(out=Kt[:, n, :], in_=pk), ["pk", "Kt"], ["Kt"])
            for pr in range(4):
                for c in range(2):
                    pn = "ptr%d" % c
                    t.op("pe", lambda e, pr=pr, c=c: e.transpose(ptr[c], en[c][:, pr, :], ident[:]), reads=[B["en%d" % c], B["ident"]], writes=[B[pn]])
                    if c == 0:
                        act(lambda e, n=n, pr=pr: e.copy(out=Wm[:, n, pr, 0, :], in_=ptr[0]), [pn, "Wm"], ["Wm"])
                    else:
                        dve(lambda e, n=n, pr=pr: e.tensor_copy(out=Wm[:, n, pr, 1, :], in_=ptr[1]), [pn, "Wm"], ["Wm"])
        cn = sb("cn", [128, 128])
        for tt in range(T):
            ix = pidx[tt + 1]
            for pr in range(4):
                a_re, a_im, na_im = pw[0][:, pr, ix:ix + 1], pw[1][:, pr, ix:ix + 1], npw_im[:, pr, ix:ix + 1]
                dve(lambda e, pr=pr, a_re=a_re: e.tensor_scalar(out=cn[:], in0=zc[0][:, pr, :], scalar1=a_re, scalar2=None, op0=ALU.mult), ["zc0", "pw_re", "cn"], ["cn"])
                dve(lambda e, pr=pr, tt=tt, na_im=na_im: e.scalar_tensor_tensor(out=Cm[:, tt, pr, 0, :], in0=zc[1][:, pr, :], scalar=na_im, in1=cn[:], op0=ALU.mult, op1=ALU.add), ["zc1", "npw_im", "cn", "Cm"], ["Cm"])
                dve(lambda e, pr=pr, a_re=a_re: e.tensor_scalar(out=cn[:], in0=nzc1[:, pr, :], scalar1=a_re, scalar2=None, op0=ALU.mult), ["nzc1", "pw_re", "cn"], ["cn"])
                dve(lambda e, pr=pr, tt=tt, na_im=na_im: e.scalar_tensor_tensor(out=Cm[:, tt, pr, 1, :], in0=zc[0][:, pr, :], scalar=na_im, in1=cn[:], op0=ALU.mult, op1=ALU.add), ["zc0", "npw_im", "cn", "Cm"], ["Cm"])

        X = [[sb("x%d_%d" % (k, c), [128, 4, NCH]) for c in range(2)] for k in range(2)]
        pw_ = [ps("pw0", [128, 512]), ps("pw1", [128, 512])]
        uv = ub[:].rearrange("p (n s) -> p n s", s=T)
        k = 0
        for pr in range(4):
            for c in range(2):
                for c0 in range(0, NCH, 512):
                    cw = min(512, NCH - c0)
                    pn = "pw%d" % (k % 2)
                    for s in range(T):
                        mm(pw_[k % 2][:, :cw], pn, Wm[:, T - 1 - s, pr, c, :], "Wm", uv[:, c0:c0 + cw, s], "ub", start=(s == 0), stop=(s == T - 1), acc=(s > 0))
                    if k % 2 == 0:
                        act(lambda e, pr=pr, c=c, c0=c0, cw=cw, k=k: e.copy(out=X[0][c][:, pr, c0:c0 + cw], in_=pw_[k % 2][:, :cw]), [pn, "x0_%d" % c], ["x0_%d" % c])
                    else:
                        dve(lambda e, pr=pr, c=c, c0=c0, cw=cw, k=k: e.tensor_copy(out=X[0][c][:, pr, c0:c0 + cw], in_=pw_[k % 2][:, :cw]), [pn, "x0_%d" % c], ["x0_%d" % c])
                    k += 1
        cur = 0
        for j in range(NJ):
            sh = 2 ** j
            ix = pidx[16 * sh]
            o, n_ = X[cur], X[1 - cur]
            on, nn = ["x%d_0" % cur, "x%d_1" % cur], ["x%d_0" % (1 - cur), "x%d_1" % (1 - cur)]
            for pr in range(4):
                a_re, a_im, na_im = pw[0][:, pr, ix:ix + 1], pw[1][:, pr, ix:ix + 1], npw_im[:, pr, ix:ix + 1]
                for c in range(2):
                    eng = dve if c == 0 else dve
                    eng(lambda e, pr=pr, c=c: e.tensor_copy(out=n_[c][:, pr, :sh], in_=o[c][:, pr, :sh]), [on[c], nn[c]], [nn[c]])
                dve(lambda e, pr=pr, a_re=a_re: e.scalar_tensor_tensor(out=n_[0][:, pr, sh:], in0=o[0][:, pr, :NCH - sh], scalar=a_re, in1=o[0][:, pr, sh:], op0=ALU.mult, op1=ALU.add), [on[0], "pw_re", nn[0]], [nn[0]])
                dve(lambda e, pr=pr, na_im=na_im: e.scalar_tensor_tensor(out=n_[0][:, pr, sh:], in0=o[1][:, pr, :NCH - sh], scalar=na_im, in1=n_[0][:, pr, sh:], op0=ALU.mult, op1=ALU.add), [on[1], "npw_im", nn[0]], [nn[0]])
                dve(lambda e, pr=pr, a_re=a_re: e.scalar_tensor_tensor(out=n_[1][:, pr, sh:], in0=o[1][:, pr, :NCH - sh], scalar=a_re, in1=o[1][:, pr, sh:], op0=ALU.mult, op1=ALU.add), [on[1], "pw_re", nn[1]], [nn[1]])
                dve(lambda e, pr=pr, a_im=a_im: e.scalar_tensor_tensor(out=n_[1][:, pr, sh:], in0=o[0][:, pr, :NCH - sh], scalar=a_im, in1=n_[1][:, pr, sh:], op0=ALU.mult, op1=ALU.add), [on[0], "pw_im", nn[1]], [nn[1]])
            cur = 1 - cur
        xb = [sb("xb0", [128, 4, NCH + 1], BF16), sb("xb1", [128, 4, NCH + 1], BF16)]
        for c in range(2):
            dve(lambda e, c=c: e.memset(xb[c][:, :, 0:1], 0.0), ["xb%d" % c], ["xb%d" % c])
            dve(lambda e, c=c: e.tensor_copy(out=xb[c][:, :, 1:], in_=X[cur][c][:]), ["x%d_%d" % (cur, c), "xb%d" % c], ["xb%d" % c])

        py = [ps("py0", [128, 512]), ps("py1", [128, 512])]
        ypre = sb("ypre", [128, 2, 512]); yo = sb("yo", [128, 2, 512], BF16)
        B["out"] = Buf("out")
        CPB = 512 // T
        for blk in range(L // 512):
            pi = blk % 2
            pn = "py%d" % pi
            pv = py[pi][:].rearrange("p (n s) -> p n s", s=T)
            ch0 = blk * CPB
            for tau in range(T):
                mm(pv[:, :, tau:T], pn, Kt[:, tau, :], "Kt", uv[:, ch0:ch0 + CPB, 0:T - tau], "ub", start=(tau == 0), stop=False, acc=(tau > 0))
            for tt in range(T):
                for pr in range(4):
                    for c in range(2):
                        last = (tt == T - 1 and pr == 3 and c == 1)
                        mm(pv[:, :, tt], pn, Cm[:, tt, pr, c, :], "Cm", xb[c][:, pr, ch0:ch0 + CPB], "xb%d" % c, start=False, stop=last, acc=True)
            ts = slice(blk * 512, (blk + 1) * 512)
            dve(lambda e, pi=pi, ts=ts: e.scalar_tensor_tensor(out=ypre[:, pi, :], in0=u32[:, ts], scalar=dsk[:, 0:1], in1=py[pi][:], op0=ALU.mult, op1=ALU.add), ["u32", "dsk", pn, "ypre"], ["ypre"])
            act(lambda e, pi=pi: e.activation(out=yo[:, pi, :], in_=ypre[:, pi, :], func=AF.Gelu), ["ypre", "yo"], ["yo"])
            t.dma_op("sp", d_y[:, ts], yo[:, pi, :], reads=[B["yo"]], writes=[B["out"]])
        t.finish([B["out"]])
    return nc, t, NS


class Ctx:
    def __init__(self):
        self.nc = bass.Bass("TRN2", target_bir_lowering=False)
        self.st = ExitStack()
        self.t = Trk(self.nc, self.st)
        self.B = {}

    def din(self, n, s, d=F32):
        return self.nc.dram_tensor(n, s, d, kind="ExternalInput").ap()

    def dout(self, n, s, d=F32):
        self.B["o_" + n] = Buf("o_" + n)
        return self.nc.dram_tensor(n, s, d, kind="ExternalOutput").ap()

    def sb(self, n, s, d=F32):
        self.B[n] = Buf(n)
        return self.st.enter_context(self.nc.sbuf_tensor("s_" + n, s, d))

    def ps(self, n, w=512):
        self.B[n] = Buf(n)
        full = self.st.enter_context(self.nc.psum_tensor("p_" + n, [128, 512], F32))
        return full[:] if w == 512 else full[:, :w]

    def _b(self, names):
        return [self.B[x] for x in names]

    def dve(self, fn, r, w): self.t.op("dve", fn, reads=self._b(r), writes=self._b(w))
    def act(self, fn, r, w): self.t.op("act", fn, reads=self._b(r), writes=self._b(w))

    def mm(self, o, on, l, ln, r, rn, start=True, stop=True, acc=False):
        self.t.op("pe", lambda e: e.matmul(o, lhsT=l, rhs=r, start=start, stop=stop), reads=self._b([ln, rn]), writes=self._b([on]), acc=acc)

    def load(self, q, tl, n, src, **kw):
        self.t.dma_op(q, tl, src, writes=self._b([n]), **kw)

    def store(self, dst, on, tl, n):
        self.t.dma_op("sp", dst, tl, reads=self._b([n]), writes=self._b(["o_" + on]))

    def done(self, outs):
        self.t.finish(self._b(["o_" + o for o in outs]))
        self.st.close()
        return self.nc


def col_groups(n):
    g, c = [], 0
    while c < n:
        w = 512 if n - c >= 512 else n - c
        assert w % 128 == 0
        g.append((c, w)); c += w
    return g


def emit_proj(cx, w_dram, K, NOUT, rhs_fn, rhs_names, L, evac, tag, wq="pool"):
    KCn = K // 128
    wb = cx.sb("wb_" + tag, [128, 2, KCn, 512], BF16)
    cx.B["wb0_" + tag], cx.B["wb1_" + tag] = Buf("wb0"), Buf("wb1")
    pss = [cx.ps("pp%d_%s" % (i, tag)) for i in range(3)]
    wv = w_dram.rearrange("(kc p) n -> p kc n", p=128)
    groups = col_groups(NOUT)

    def load_w(gi):
        c0, w = groups[gi]
        cx.t.dma_op(wq, wb[:, gi % 2, :, :w], wv[:, :, c0:c0 + w], writes=[cx.B["wb%d_%s" % (gi % 2, tag)]])

    load_w(0)
    k = 0
    for gi, (c0, w) in enumerate(groups):
        if gi + 1 < len(groups):
            load_w(gi + 1)
        for m in range(w // 128):
            for tb in range(L // 512):
                ts = slice(tb * 512, (tb + 1) * 512)
                pi = k % 3
                pn = "pp%d_%s" % (pi, tag)
                for kc in range(KCn):
                    cx.mm(pss[pi], pn, wb[:, gi % 2, kc, m * 128:(m + 1) * 128], "wb%d_%s" % (gi % 2, tag), rhs_fn(kc, ts), rhs_names[0] if len(rhs_names) == 1 else rhs_names[kc],
                          start=(kc == 0), stop=(kc == KCn - 1), acc=(kc > 0))
                evac(pss[pi], pn, c0 + m * 128, ts, k)
                k += 1


def build_A(L, NOUT):
    cx = Ctx()
    x = cx.din("xT", [D, L]); nwd = cx.din("nw", [128, KC]); w = cx.din("w", [D, NOUT])
    p = cx.dout("pT", [NOUT, L], BF16)
    xT = cx.sb("x", [128, KC, L]); hT = cx.sb("h", [128, KC, L], BF16); nw = cx.sb("nw", [128, KC]); ones = cx.sb("ones", [128, 128], BF16)
    ob = cx.sb("ob", [128, 3, 512], BF16)
    cx.load("sp", xT[:], "x", x.rearrange("(kc p) l -> p kc l", p=128))
    cx.load("sp", nw[:], "nw", nwd)
    cx.dve(lambda e: e.memset(ones[:], 1.0), [], ["ones"])
    emit_rmsnorm(cx.t, cx.nc, cx.st, xT, cx.B["x"], nw, cx.B["nw"], hT, cx.B["h"], ones, cx.B["ones"], L, "a")
    obn = ["ob0", "ob1", "ob2"]
    for n in obn:
        cx.B[n] = Buf(n)

    def evac(pst, pn, row0, ts, k):
        j = k % 3
        if k % 2 == 0:
            cx.act(lambda e: e.copy(out=ob[:, j, :], in_=pst), [pn], [obn[j]])
        else:
            cx.dve(lambda e: e.tensor_copy(out=ob[:, j, :], in_=pst), [pn], [obn[j]])
        cx.store(p[row0:row0 + 128, ts], "pT", ob[:, j, :], obn[j])

    emit_proj(cx, w, D, NOUT, lambda kc, ts: hT[:, kc, ts], ["h"], L, evac, "a")
    return cx.done(["pT"])


def build_DNF(L):
    NB = L // 128
    cx = Ctx()
    dq, dk, dv, dz = (cx.din(n, [128, L], BF16) for n in ("q", "k", "v", "z"))
    dbr, dar = cx.din("b_raw_bc", [128, L], BF16), cx.din("a_raw_bc", [128, L], BF16)
    darc = cx.din("a_raw_col", [128, NB], BF16)
    dcw = cx.din("convw", [128, 3, 4])
    dsc = cx.din("scal", [128, 2])
    oq, ok, ov = (cx.dout(n, [128, L]) for n in ("qn", "kn", "vs"))
    og, obt = cx.dout("g_bc", [128, L]), cx.dout("beta_bc", [128, L])
    ogc = cx.dout("g_col", [128, NB]); osz = cx.dout("sz", [128, L], BF16)
    raw = cx.sb("raw", [128, L], BF16); acc = cx.sb("acc", [128, L]); res = cx.sb("res", [128, L])
    cw = cx.sb("cw", [128, 3, 4]); sc = cx.sb("sc", [128, 2]); nea = cx.sb("nea", [128, 1])
    ones = cx.sb("ones", [128, 128]); sq = cx.sb("sq", [128, 512]); rs = cx.sb("rs", [128, 512])
    pn_ = cx.ps("pn")
    cx.load("sp", cw[:], "cw", dcw); cx.load("sp", sc[:], "sc", dsc)
    cx.dve(lambda e: e.memset(ones[:], 1.0), [], ["ones"])
    cx.act(lambda e: e.activation(out=nea[:], in_=sc[:, 0:1], func=AF.Exp), ["sc"], ["nea"])
    cx.dve(lambda e: e.tensor_scalar(out=nea[:], in0=nea[:], scalar1=-1.0, scalar2=None, op0=ALU.mult), ["nea"], ["nea"])
    for i, (src, dst, on) in enumerate([(dq, oq, "qn"), (dk, ok, "kn"), (dv, ov, "vs")]):
        cx.load("sp", raw[:], "raw", src)
        cx.dve(lambda e, i=i: e.tensor_scalar(out=acc[:], in0=raw[:], scalar1=cw[:, i, 3:4], scalar2=None, op0=ALU.mult), ["raw", "cw"], ["acc"])
        for s in (1, 2, 3):
            cx.dve(lambda e, i=i, s=s: e.scalar_tensor_tensor(out=acc[:, s:], in0=raw[:, :L - s], scalar=cw[:, i, 3 - s:4 - s], in1=acc[:, s:], op0=ALU.mult, op1=ALU.add), ["raw", "cw", "acc"], ["acc"])
        cx.act(lambda e: e.activation(out=res[:], in_=acc[:], func=AF.Silu), ["acc"], ["res"])
        if i < 2:
            for tb in range(L // 512):
                ts = slice(tb * 512, (tb + 1) * 512)
                cx.act(lambda e, ts=ts: e.activation(out=sq[:], in_=res[:, ts], func=AF.Square), ["res"], ["sq"])
                cx.mm(pn_, "pn", ones[:], "ones", sq[:], "sq")
                cx.dve(lambda e: e.tensor_scalar(out=rs[:], in0=pn_, scalar1=EPS, scalar2=None, op0=ALU.add), ["pn"], ["rs"])
                cx.act(lambda e: e.activation(out=rs[:], in_=rs[:], func=AF.Sqrt), ["rs"], ["rs"])
                cx.dve(lambda e: e.reciprocal(out=rs[:], in_=rs[:]), ["rs"], ["rs"])
                if i == 0:
                    cx.dve(lambda e, ts=ts: e.scalar_tensor_tensor(out=res[:, ts], in0=res[:, ts], scalar=128.0 ** -0.5, in1=rs[:], op0=ALU.mult, op1=ALU.mult), ["res", "rs"], ["res"])
                else:
                    cx.dve(lambda e, ts=ts: e.tensor_tensor(out=res[:, ts], in0=res[:, ts], in1=rs[:], op=ALU.mult), ["res", "rs"], ["res"])
        cx.store(dst, on, res[:], "res")
    szt = cx.sb("szt", [128, L], BF16)
    cx.load("sp", raw[:], "raw", dz)
    cx.act(lambda e: e.activation(out=szt[:], in_=raw[:], func=AF.Silu), ["raw"], ["szt"])
    cx.store(osz, "sz", szt[:], "szt")
    cx.load("sp", raw[:], "raw", dbr)
    cx.act(lambda e: e.activation(out=res[:], in_=raw[:], func=AF.Sigmoid), ["raw"], ["res"])
    cx.store(obt, "beta_bc", res[:], "res")
    cx.load("sp", raw[:], "raw", dar)
    cx.act(lambda e: e.activation(out=acc[:], in_=raw[:], func=AF.Exp, bias=sc[:, 1:2]), ["raw", "sc"], ["acc"])
    cx.act(lambda e: e.activation(out=acc[:], in_=acc[:], func=AF.Ln, bias=1.0), ["acc"], ["acc"])
    cx.dve(lambda e: e.tensor_scalar(out=res[:], in0=acc[:], scalar1=nea[:, 0:1], scalar2=None, op0=ALU.mult), ["acc", "nea"], ["res"])
    cx.store(og, "g_bc", res[:], "res")
    rc = cx.sb("rc", [128, NB], BF16); gcl = cx.sb("gcl", [128, NB])
    cx.load("sp", rc[:], "rc", darc)
    cx.act(lambda e: e.activation(out=gcl[:], in_=rc[:], func=AF.Exp, bias=sc[:, 1:2]), ["rc", "sc"], ["gcl"])
    cx.act(lambda e: e.activation(out=gcl[:], in_=gcl[:], func=AF.Ln, bias=1.0), ["gcl"], ["gcl"])
    cx.dve(lambda e: e.tensor_scalar(out=gcl[:], in0=gcl[:], scalar1=nea[:, 0:1], scalar2=None, op0=ALU.mult), ["gcl", "nea"], ["gcl"])
    cx.store(ogc, "g_col", gcl[:], "gcl")
    return cx.done(["qn", "kn", "vs", "g_bc", "beta_bc", "g_col", "sz"])


NEG = -30000.0


def build_DNC(L):
    NB = L // 128
    cx = Ctx()
    dq, dk, dv, dg, db = (cx.din(n, [128, L]) for n in ("qn", "kn", "vs", "g_bc", "beta_bc"))
    dgc = cx.din("g_col", [128, NB]); dsz = cx.din("sz", [128, L], BF16); dnw = cx.din("dn_nw", [128, 1])
    dmu, dmui, dml, dtri, did = (cx.din(n, [128, 128]) for n in ("maskU", "maskUi", "maskL", "tri", "ident"))
    dy = cx.dout("y_dn", [128, L], BF16)
    gcol = cx.sb("gcol", [128, NB]); gccol = cx.sb("gccol", [128, NB]); ngccol = cx.sb("ngccol", [128, NB]); nw = cx.sb("nw", [128, 1])
    mU, mUi, mL, tri, ident = (cx.sb(n, [128, 128]) for n in ("mU", "mUi", "mL", "tri", "id"))
    ones1 = cx.sb("ones1", [128, 128])
    for tl, d, n in [(gcol, dgc, "gcol"), (nw, dnw, "nw"), (mU, dmu, "mU"), (mUi, dmui, "mUi"), (mL, dml, "mL"), (tri, dtri, "tri"), (ident, did, "id")]:
        cx.load("sp", tl[:], n, d)
    cx.dve(lambda e: e.memset(ones1[:], 1.0), [], ["ones1"])
    pcol = cx.ps("pcol", NB)
    cx.mm(pcol, "pcol", tri[:], "tri", gcol[:], "gcol")
    cx.dve(lambda e: e.tensor_copy(out=gccol[:], in_=pcol), ["pcol"], ["gccol"])
    cx.dve(lambda e: e.tensor_scalar(out=ngccol[:], in0=pcol, scalar1=-1.0, scalar2=None, op0=ALU.mult), ["pcol"], ["ngccol"])
    pA, pB, pC, pD = (cx.ps(n, 128) for n in ("pA", "pB", "pC", "pD"))
    inb = [[cx.sb("in%d_%d" % (p, i), [128, 128]) for i in range(5)] for p in range(2)]
    szb = [cx.sb("sz%d" % p, [128, 128], BF16) for p in range(2)]
    names = ["arg", "DT", "D", "M", "Lm", "M2", "L2", "R", "Rt", "attnT", "vb", "kbg", "wT", "vnew", "kdec", "S", "egl", "bcol", "kb", "gc", "egc", "qg", "osb", "osq", "rs"]
    T_ = {n: cx.sb("t_" + n, [128, 128]) for n in names}
    yb = [cx.sb("yb%d" % p, [128, 128], BF16) for p in range(2)]
    S = T_["S"]
    cx.dve(lambda e: e.memset(S[:], 0.0), [], ["t_S"])
    srcs = [dq, dk, dv, dg, db]

    def load_blk(b):
        p = b % 2
        bs = slice(b * 128, (b + 1) * 128)
        for i in range(5):
            cx.load("sp", inb[p][i][:], "in%d_%d" % (p, i), srcs[i][:, bs])
        cx.load("sp", szb[p][:], "sz%d" % p, dsz[:, bs])

    load_blk(0)
    for b in range(NB):
        p = b % 2
        if b + 1 < NB:
            load_blk(b + 1)
        qT, kT, vT, gb, bb = (inb[p][i] for i in range(5))
        nq, nk, nv, ng, nb_ = ("in%d_%d" % (p, i) for i in range(5))
        gcc, ngcc = gccol[:, b:b + 1], ngccol[:, b:b + 1]
        kb, gc, egc, qg = T_["kb"], T_["gc"], T_["egc"], T_["qg"]
        cx.dve(lambda e: e.tensor_tensor(out=kb[:], in0=kT[:], in1=bb[:], op=ALU.mult), [nk, nb_], ["t_kb"])
        cx.dve(lambda e: e.tensor_tensor_scan(out=gc[:], data0=ones1[:], data1=gb[:], initial=0.0, op0=ALU.mult, op1=ALU.add), [ng, "ones1"], ["t_gc"])
        cx.act(lambda e: e.activation(out=egc[:], in_=gc[:], func=AF.Exp), ["t_gc"], ["t_egc"])
        cx.dve(lambda e: e.tensor_tensor(out=qg[:], in0=qT[:], in1=egc[:], op=ALU.mult), [nq, "t_egc"], ["t_qg"])
        cx.mm(pA, "pA", kT[:], nk, kb[:], "t_kb")
        cx.dve(lambda e: e.tensor_tensor(out=T_["arg"][:], in0=gc[:], in1=mU[:], op=ALU.add), ["t_gc", "mU"], ["t_arg"])
        cx.act(lambda e: e.activation(out=T_["DT"][:], in_=T_["arg"][:], func=AF.Exp, bias=ngcc), ["t_arg", "ngccol"], ["t_DT"])
        cx.dve(lambda e: e.tensor_tensor(out=T_["M"][:], in0=pA, in1=T_["DT"][:], op=ALU.mult), ["pA", "t_DT"], ["t_M"])
        cx.mm(pB, "pB", kb[:], "t_kb", kT[:], nk)
        cx.dve(lambda e: e.scalar_tensor_tensor(out=T_["arg"][:], in0=gc[:], scalar=-1.0, in1=mL[:], op0=ALU.mult, op1=ALU.add), ["t_gc", "mL"], ["t_arg"])
        cx.act(lambda e: e.activation(out=T_["D"][:], in_=T_["arg"][:], func=AF.Exp, bias=gcc), ["t_arg", "gccol"], ["t_D"])
        cx.dve(lambda e: e.tensor_tensor(out=T_["Lm"][:], in0=pB, in1=T_["D"][:], op=ALU.mult), ["pB", "t_D"], ["t_Lm"])
        cx.mm(pC, "pC", kT[:], nk, qT[:], nq)
        cx.dve(lambda e: e.tensor_tensor(out=T_["arg"][:], in0=gc[:], in1=mUi[:], op=ALU.add), ["t_gc", "mUi"], ["t_arg"])
        cx.act(lambda e: e.activation(out=T_["DT"][:], in_=T_["arg"][:], func=AF.Exp, bias=ngcc), ["t_arg", "ngccol"], ["t_DT"])
        cx.dve(lambda e: e.tensor_tensor(out=T_["attnT"][:], in0=pC, in1=T_["DT"][:], op=ALU.mult), ["pC", "t_DT"], ["t_attnT"])
        cx.dve(lambda e: e.tensor_tensor(out=T_["R"][:], in0=ident[:], in1=T_["M"][:], op=ALU.subtract), ["id", "t_M"], ["t_R"])
        cx.dve(lambda e: e.tensor_tensor(out=T_["Rt"][:], in0=ident[:], in1=T_["Lm"][:], op=ALU.subtract), ["id", "t_Lm"], ["t_Rt"])
        Mp, Lp, Mq, Lq = "M", "Lm", "M2", "L2"
        for it in range(6):
            cx.mm(pA, "pA", T_[Lp][:], "t_" + Lp, T_[Mp][:], "t_" + Mp)
            cx.mm(pB, "pB", T_[Mp][:], "t_" + Mp, T_[Lp][:], "t_" + Lp)
            cx.act(lambda e, Mq=Mq: e.copy(out=T_[Mq][:], in_=pA), ["pA"], ["t_" + Mq])
            cx.dve(lambda e, Lq=Lq: e.tensor_copy(out=T_[Lq][:], in_=pB), ["pB"], ["t_" + Lq])
            cx.mm(pC, "pC", T_["Rt"][:], "t_Rt", T_[Mq][:], "t_" + Mq)
            cx.mm(pD, "pD", T_["R"][:], "t_R", T_[Lq][:], "t_" + Lq)
            cx.dve(lambda e: e.tensor_tensor(out=T_["R"][:], in0=T_["R"][:], in1=pC, op=ALU.add), ["t_R", "pC"], ["t_R"])
            cx.dve(lambda e: e.tensor_tensor(out=T_["Rt"][:], in0=T_["Rt"][:], in1=pD, op=ALU.add), ["t_Rt", "pD"], ["t_Rt"])
            Mp, Lp, Mq, Lq = Mq, Lq, Mp, Lp
        cx.t.op("pe", lambda e: e.transpose(pA, vT[:], ident[:]), reads=cx._b([nv, "id"]), writes=cx._b(["pA"]))
        cx.t.op("pe", lambda e: e.transpose(pB, kT[:], ident[:]), reads=cx._b([nk, "id"]), writes=cx._b(["pB"]))
        cx.t.op("pe", lambda e: e.transpose(pC, bb[:], ident[:]), reads=cx._b([nb_, "id"]), writes=cx._b(["pC"]))
        cx.dve(lambda e: e.tensor_copy(out=T_["bcol"][:], in_=pC), ["pC"], ["t_bcol"])
        cx.dve(lambda e: e.tensor_tensor(out=T_["vb"][:], in0=pA, in1=T_["bcol"][:], op=ALU.mult), ["pA", "t_bcol"], ["t_vb"])
        cx.act(lambda e: e.activation(out=T_["egl"][:, 0:1], in_=gcc, func=AF.Exp), ["gccol", "t_egl"], ["t_egl"])
        cx.dve(lambda e: e.tensor_tensor(out=T_["kbg"][:], in0=pB, in1=T_["bcol"][:], op=ALU.mult), ["pB", "t_bcol"], ["t_kbg"])
        cx.dve(lambda e: e.tensor_scalar(out=T_["kbg"][:], in0=T_["kbg"][:], scalar1=T_["egl"][:, 0:1], scalar2=None, op0=ALU.mult), ["t_kbg", "t_egl"], ["t_kbg"])
        cx.dve(lambda e: e.tensor_tensor(out=T_["egl"][:, 1:2], in0=gc[:, 127:128], in1=gcc, op=ALU.subtract), ["t_gc", "gccol", "t_egl"], ["t_egl"])
        cx.act(lambda e: e.activation(out=T_["egl"][:, 1:2], in_=T_["egl"][:, 1:2], func=AF.Exp), ["t_egl"], ["t_egl"])
        cx.act(lambda e: e.activation(out=T_["egl"][:, 2:3], in_=gc[:, 127:128], func=AF.Exp), ["t_gc", "t_egl"], ["t_egl"])
        cx.dve(lambda e: e.tensor_scalar(out=T_["kdec"][:], in0=pB, scalar1=T_["egl"][:, 1:2], scalar2=None, op0=ALU.mult), ["pB", "t_egl"], ["t_kdec"])
        cx.mm(pD, "pD", T_["kbg"][:], "t_kbg", T_["R"][:], "t_R")
        cx.dve(lambda e: e.tensor_scalar(out=T_["wT"][:], in0=pD, scalar1=-1.0, scalar2=None, op0=ALU.mult), ["pD"], ["t_wT"])
        cx.mm(pA, "pA", T_["R"][:], "t_R", T_["vb"][:], "t_vb", start=True, stop=False)
        cx.mm(pA, "pA", T_["wT"][:], "t_wT", S[:], "t_S", start=False, stop=True, acc=True)
        cx.act(lambda e: e.copy(out=T_["vnew"][:], in_=pA), ["pA"], ["t_vnew"])
        cx.mm(pC, "pC", S[:], "t_S", qg[:], "t_qg", start=True, stop=False)
        cx.mm(pC, "pC", T_["vnew"][:], "t_vnew", T_["attnT"][:], "t_attnT", start=False, stop=True, acc=True)
        cx.dve(lambda e: e.tensor_copy(out=T_["osb"][:], in_=pC), ["pC"], ["t_osb"])
        cx.mm(pD, "pD", T_["kdec"][:], "t_kdec", T_["vnew"][:], "t_vnew")
        cx.dve(lambda e: e.scalar_tensor_tensor(out=S[:], in0=S[:], scalar=T_["egl"][:, 2:3], in1=pD, op0=ALU.mult, op1=ALU.add), ["t_S", "t_egl", "pD"], ["t_S"])
        cx.act(lambda e: e.activation(out=T_["osq"][:], in_=T_["osb"][:], func=AF.Square), ["t_osb"], ["t_osq"])
        cx.mm(pA, "pA", ones1[:], "ones1", T_["osq"][:], "t_osq")
        cx.dve(lambda e: e.tensor_scalar(out=T_["rs"][:], in0=pA, scalar1=1.0 / 128, scalar2=EPS, op0=ALU.mult, op1=ALU.add), ["pA"], ["t_rs"])
        cx.act(lambda e: e.activation(out=T_["rs"][:], in_=T_["rs"][:], func=AF.Sqrt), ["t_rs"], ["t_rs"])
        cx.dve(lambda e: e.reciprocal(out=T_["rs"][:], in_=T_["rs"][:]), ["t_rs"], ["t_rs"])
        cx.dve(lambda e: e.scalar_tensor_tensor(out=T_["osb"][:], in0=T_["osb"][:], scalar=nw[:, 0:1], in1=T_["rs"][:], op0=ALU.mult, op1=ALU.mult), ["t_osb", "nw", "t_rs"], ["t_osb"])
        cx.dve(lambda e, p=p: e.tensor_tensor(out=yb[p][:], in0=T_["osb"][:], in1=szb[p][:], op=ALU.mult), ["t_osb", "sz%d" % p, "yb%d" % p], ["yb%d" % p])
        cx.store(dy[:, b * 128:(b + 1) * 128], "y_dn", yb[p][:], "yb%d" % p)
    return cx.done(["y_dn"])


def dn_consts():
    i = np.arange(128)
    f = lambda a: np.ascontiguousarray(a, dtype=np.float32)
    return {"maskU": f(np.where(i[:, None] < i[None, :], 0.0, NEG)), "maskUi": f(np.where(i[:, None] <= i[None, :], 0.0, NEG)),
            "maskL": f(np.where(i[None, :] < i[:, None], 0.0, NEG)), "tri": f(i[:, None] <= i[None, :]), "ident": f(np.eye(128))}


def emit_proj2(cx, w_dram, K, NOUT, rhs_fn, rhs_name, tslices, evac, tag, gw=512):
    KCn = K // 128
    wb = cx.sb("wb_" + tag, [128, 2, KCn, gw], BF16)
    cx.B["wb0_" + tag], cx.B["wb1_" + tag] = Buf("wb0"), Buf("wb1")
    pss = [cx.ps("pp%d_%s" % (i, tag)) for i in range(2)]
    wv = w_dram.rearrange("(kc p) n -> p kc n", p=128)
    groups = [(c, gw) for c in range(0, NOUT, gw)]
    assert NOUT % gw == 0

    def load_w(gi):
        c0, w = groups[gi]
        cx.t.dma_op("pool", wb[:, gi % 2, :, :], wv[:, :, c0:c0 + w], writes=[cx.B["wb%d_%s" % (gi % 2, tag)]])

    load_w(0)
    k = 0
    for gi, (c0, w) in enumerate(groups):
        if gi + 1 < len(groups):
            load_w(gi + 1)
        for m in range(w // 128):
            for (a, b_) in tslices:
                pi = k % 2
                pn = "pp%d_%s" % (pi, tag)
                for kc in range(KCn):
                    cx.mm(pss[pi][:, :b_ - a], pn, wb[:, gi % 2, kc, m * 128:(m + 1) * 128], "wb%d_%s" % (gi % 2, tag), rhs_fn(kc, slice(a, b_)), rhs_name,
                          start=(kc == 0), stop=(kc == KCn - 1), acc=(kc > 0))
                evac(pss[pi][:, :b_ - a], pn, (c0 + m * 128) // 128, (a, b_), k)
                k += 1


def build_C1(Lh):
    cx = Ctx()
    dx = cx.din("xT", [D, Lh]); dys = cx.din("ys", [1024, Lh], BF16); dyd = cx.din("yd", [1024, Lh], BF16)
    dgs = cx.din("gs", [D, Lh], BF16); dgd = cx.din("gd", [D, Lh], BF16)
    dglu = cx.din("glu_w", [1024, 4096]); ddp = cx.din("dn_proj", [1024, D]); dwo = cx.din("w_out", [D, D])
    ox = cx.dout("xo", [D, Lh])
    x = cx.sb("x", [128, KC, Lh]); ys = cx.sb("ys", [128, 8, Lh], BF16); yd = cx.sb("yd", [128, 8, Lh], BF16)
    gs = cx.sb("gs", [128, KC, Lh], BF16); gd = cx.sb("gd", [128, KC, Lh], BF16)
    sigb = cx.sb("sigb", [128, KC, Lh], BF16); mg = cx.sb("mg", [128, KC, Lh]); mgb = cx.sb("mgb", [128, KC, Lh], BF16); tmp = cx.sb("tmp", [128, Lh])
    cx.load("sp", x[:], "x", dx.rearrange("(kc p) l -> p kc l", p=128))
    cx.load("sp", ys[:], "ys", dys.rearrange("(kc p) l -> p kc l", p=128)); cx.load("sp", yd[:], "yd", dyd.rearrange("(kc p) l -> p kc l", p=128))
    cx.load("sp", gs[:], "gs", dgs.rearrange("(kc p) l -> p kc l", p=128)); cx.load("sp", gd[:], "gd", dgd.rearrange("(kc p) l -> p kc l", p=128))
    cx.act(lambda e: e.activation(out=gs[:], in_=gs[:], func=AF.Sigmoid), ["gs"], ["gs"])
    cx.act(lambda e: e.activation(out=gd[:], in_=gd[:], func=AF.Sigmoid), ["gd"], ["gd"])
    ts = [(0, Lh)]

    def ev_glu(pst, pn, mt, tsl, k):
        if mt < 16:
            cx.act(lambda e: e.activation(out=sigb[:, mt, :], in_=pst, func=AF.Sigmoid), [pn, "sigb"], ["sigb"])
        else:
            j = mt - 16
            cx.dve(lambda e: e.tensor_tensor(out=tmp[:], in0=pst, in1=sigb[:, j, :], op=ALU.mult), [pn, "sigb"], ["tmp"])
            cx.dve(lambda e: e.tensor_tensor(out=mg[:, j, :], in0=tmp[:], in1=gs[:, j, :], op=ALU.mult), ["tmp", "gs", "mg"], ["mg"])

    emit_proj2(cx, dglu, 1024, 4096, lambda kc, s: ys[:, kc, s], "ys", ts, ev_glu, "glu")

    def ev_dn(pst, pn, mt, tsl, k):
        cx.dve(lambda e: e.tensor_tensor(out=tmp[:], in0=pst, in1=gd[:, mt, :], op=ALU.mult), [pn, "gd"], ["tmp"])
        cx.dve(lambda e: e.tensor_tensor(out=mgb[:, mt, :], in0=tmp[:], in1=mg[:, mt, :], op=ALU.add), ["tmp", "mg", "mgb"], ["mgb"])

    emit_proj2(cx, ddp, 1024, D, lambda kc, s: yd[:, kc, s], "yd", ts, ev_dn, "dnp")

    def ev_out(pst, pn, mt, tsl, k):
        cx.dve(lambda e: e.tensor_tensor(out=x[:, mt, :], in0=x[:, mt, :], in1=pst, op=ALU.add), [pn, "x"], ["x"])
        cx.store(ox[mt * 128:(mt + 1) * 128, :], "xo", x[:, mt, :], "x")

    emit_proj2(cx, dwo, D, D, lambda kc, s: mgb[:, kc, s], "mgb", ts, ev_out, "wo", gw=256)
    return cx.done(["xo"])


FF = 5632


def build_C2(Lh, final):
    W = Lh + 2
    cx = Ctx()
    dx = cx.din("xT", [D, W]); dnw = cx.din("nw", [128, KC]); dup = cx.din("ffn_up", [D, 2 * FF]); dcw = cx.din("convw", [128, 88, 3]); ddn = cx.din("ffn_down", [FF, D])
    dfw = cx.din("fnw", [128, KC])
    ox = cx.dout("xo", [D, Lh])
    x = cx.sb("x", [128, KC, W]); h = cx.sb("h", [128, KC, W], BF16); nw = cx.sb("nw", [128, KC]); fw = cx.sb("fw", [128, KC]); cw = cx.sb("cw", [128, 88, 3])
    ones = cx.sb("ones", [128, 128], BF16); sq = cx.sb("sq", [128, W], BF16); rs = cx.sb("rs", [128, W])
    inter = cx.sb("inter", [128, 44, Lh], BF16); actb = cx.sb("actb", [128, 44, Lh], BF16); upp = cx.sb("upp", [128, W]); cv = cx.sb("cv", [128, Lh])
    cx.load("sp", x[:], "x", dx.rearrange("(kc p) l -> p kc l", p=128)); cx.load("sp", nw[:], "nw", dnw); cx.load("sp", fw[:], "fw", dfw); cx.load("sp", cw[:], "cw", dcw)
    cx.dve(lambda e: e.memset(ones[:], 1.0), [], ["ones"])
    pn_ = cx.ps("pnrm")
    tsl = [(0, 2), (2, W)]

    def rmsnorm(src, sname, wt, wname, dst, dname, slices):
        for (a, b_) in slices:
            for kc in range(KC):
                cx.act(lambda e, kc=kc: e.activation(out=sq[:, a:b_], in_=src[:, kc, a:b_], func=AF.Square), [sname, "sq"], ["sq"])
                cx.mm(pn_[:, :b_ - a], "pnrm", ones[:], "ones", sq[:, a:b_], "sq", start=(kc == 0), stop=(kc == KC - 1), acc=False)
            cx.dve(lambda e: e.tensor_scalar(out=rs[:, a:b_], in0=pn_[:, :b_ - a], scalar1=1.0 / D, scalar2=EPS, op0=ALU.mult, op1=ALU.add), ["pnrm", "rs"], ["rs"])
            cx.act(lambda e: e.activation(out=rs[:, a:b_], in_=rs[:, a:b_], func=AF.Sqrt), ["rs"], ["rs"])
            cx.dve(lambda e: e.reciprocal(out=rs[:, a:b_], in_=rs[:, a:b_]), ["rs"], ["rs"])
            for kc in range(KC):
                cx.dve(lambda e, kc=kc: e.scalar_tensor_tensor(out=dst[:, kc, a:b_], in0=src[:, kc, a:b_], scalar=wt[:, kc:kc + 1], in1=rs[:, a:b_], op0=ALU.mult, op1=ALU.mult), [sname, wname, "rs", dname], [dname])

    rmsnorm(x, "x", nw, "nw", h, "h", tsl)

    def ev_up(pst, pn, mt, sl, k):
        a, b_ = sl
        if a == 0:
            cx.act(lambda e: e.copy(out=upp[:, 0:2], in_=pst), [pn, "upp"], ["upp"])
            return
        cx.act(lambda e: e.copy(out=upp[:, 2:W], in_=pst), [pn, "upp"], ["upp"])
        cx.dve(lambda e: e.tensor_scalar(out=cv[:], in0=upp[:, 2:W], scalar1=cw[:, mt, 2:3], scalar2=None, op0=ALU.mult), ["upp", "cw"], ["cv"])
        cx.dve(lambda e: e.scalar_tensor_tensor(out=cv[:], in0=upp[:, 1:W - 1], scalar=cw[:, mt, 1:2], in1=cv[:], op0=ALU.mult, op1=ALU.add), ["upp", "cw", "cv"], ["cv"])
        cx.dve(lambda e: e.scalar_tensor_tensor(out=cv[:], in0=upp[:, 0:W - 2], scalar=cw[:, mt, 0:1], in1=cv[:], op0=ALU.mult, op1=ALU.add), ["upp", "cw", "cv"], ["cv"])
        if mt < 44:
            cx.act(lambda e: e.activation(out=actb[:, mt, :], in_=cv[:], func=AF.Silu), ["cv", "actb"], ["actb"])
        else:
            cx.dve(lambda e: e.tensor_tensor(out=inter[:, mt - 44, :], in0=cv[:], in1=actb[:, mt - 44, :], op=ALU.mult), ["cv", "actb", "inter"], ["inter"])

    emit_proj2(cx, dup, D, 2 * FF, lambda kc, s: h[:, kc, s], "h", tsl, ev_up, "up")

    def ev_dn(pst, pn, mt, sl, k):
        cx.dve(lambda e: e.tensor_tensor(out=x[:, mt, 2:W], in0=x[:, mt, 2:W], in1=pst, op=ALU.add), [pn, "x"], ["x"])
        if not final:
            cx.store(ox[mt * 128:(mt + 1) * 128, :], "xo", x[:, mt, 2:W], "x")

    emit_proj2(cx, ddn, FF, D, lambda kc, s: inter[:, kc, s], "inter", [(0, Lh)], ev_dn, "dn", gw=128)
    if final:
        rmsnorm(x, "x", fw, "fw", x, "x", [(2, W)])
        for mt in range(KC):
            cx.store(ox[mt * 128:(mt + 1) * 128, :], "xo", x[:, mt, 2:W], "x")
    return cx.done(["xo"])


_PROG = {}
NCORE = 8
SEQ = 8192
LC = SEQ // NCORE
NA = 9216 + 128


def _prog(key, fn):
    if key not in _PROG:
        _PROG[key] = fn()
    return _PROG[key]


def _run(nc, in_maps):
    res = run_bass_kernel_spmd(nc, in_maps, core_ids=list(range(NCORE)))
    return res.results


def _c(a, dt=None):
    return np.ascontiguousarray(a if dt is None else a.astype(dt))


def _pcol(v):
    return _c(v.reshape(-1, 128).T)


def kernel(x, mix_norm_w, w_in, s5_log_dt, s5_a_re, s5_a_im, s5_b_re, s5_b_im, s5_c_re, s5_c_im, s5_d, s5_glu_w,
           dn_conv_w, dn_a_log, dn_dt_bias, dn_norm_w, dn_proj_w, w_out, ffn_norm_w, ffn_up, ffn_conv_w, ffn_down, final_norm_w):
    f32 = np.float32
    XT = _c(np.asarray(x, f32)[0].T)
    depth = w_in.shape[0]
    cst = dn_consts()
    ncA = _prog("A", lambda: build_A(LC, NA))
    s5b = _prog("S5", lambda: build_S5(SEQ))
    ncS5, NS = s5b[0], s5b[2]
    ncF = _prog("DNF", lambda: build_DNF(SEQ)); ncDC = _prog("DNC", lambda: build_DNC(SEQ))
    ncC1 = _prog("C1", lambda: build_C1(512))
    ns_t = _c(np.tile(np.array(NS, f32), (128, 1))); ident = cst["ident"]
    for l in range(depth):
        w = np.asarray(w_in[l], f32)
        wre = np.zeros((D, NA), f32)
        wre[:, 0:5120] = w[:, 0:5120]; wre[:, 5120:9216] = w[:, 5136:9232]; wre[:, 9216:9232] = w[:, 5120:5136]
        nw = _pcol(np.asarray(mix_norm_w[l], f32))
        r = _run(ncA, [{"xT": _c(XT[:, c * LC:(c + 1) * LC]), "nw": nw, "w": wre} for c in range(NCORE)])
        P = np.concatenate([np.asarray(r[c]["pT"]) for c in range(NCORE)], axis=1)
        ins = []
        for c in range(NCORE):
            g0 = 8 * c
            lay = lambda a: _c(np.asarray(a, f32)[g0:g0 + 8].reshape(4, 128).T)
            def zpad(mm_):
                z = np.zeros((128, 4, 128), f32)
                for g in range(8):
                    z[(g % 2) * 64:(g % 2) * 64 + 64, g // 2, g * 16:(g + 1) * 16] = mm_[g]
                return z
            ins.append({"u": _c(P[128 * c:128 * c + 128]), "lr": lay(s5_a_re[l]), "li": lay(s5_a_im[l]),
                        "ldt": lay(np.repeat(np.asarray(s5_log_dt[l], f32)[:, None], 64, 1)),
                        "zb_re": zpad(np.asarray(s5_b_re[l], f32)[g0:g0 + 8]), "zb_im": zpad(np.asarray(s5_b_im[l], f32)[g0:g0 + 8]),
                        "zc_re": zpad(np.asarray(s5_c_re[l], f32)[g0:g0 + 8].transpose(0, 2, 1)), "zc_im": zpad(np.asarray(s5_c_im[l], f32)[g0:g0 + 8].transpose(0, 2, 1)),
                        "dskip": _c(np.asarray(s5_d[l], f32)[128 * c:128 * c + 128, None]), "ns": ns_t, "ident": ident})
        r = _run(ncS5, ins)
        YS = np.concatenate([np.asarray(r[c]["y"]) for c in range(NCORE)], axis=0)
        cwl = np.asarray(dn_conv_w[l], f32)
        ins = []
        for c in range(NCORE):
            b_raw, a_raw = P[9216 + c], P[9216 + 8 + c]
            cw3 = np.stack([cwl[:, 128 * c:128 * c + 128], cwl[:, 1024 + 128 * c:1024 + 128 * c + 128], cwl[:, 2048 + 128 * c:2048 + 128 * c + 128]], 1)
            ins.append({"q": _c(P[1024 + 128 * c:1152 + 128 * c]), "k": _c(P[2048 + 128 * c:2176 + 128 * c]), "v": _c(P[3072 + 128 * c:3200 + 128 * c]),
                        "z": _c(P[4096 + 128 * c:4224 + 128 * c]), "b_raw_bc": _c(np.tile(b_raw, (128, 1))), "a_raw_bc": _c(np.tile(a_raw, (128, 1))),
                        "a_raw_col": _c(a_raw.reshape(SEQ // 128, 128).T), "convw": _c(cw3.transpose(2, 1, 0)),
                        "scal": _c(np.tile(np.array([dn_a_log[l][c], dn_dt_bias[l][c]], f32), (128, 1)))})
        r1 = _run(ncF, ins)
        ins = []
        for c in range(NCORE):
            d = {n: np.asarray(r1[c][n]) for n in ("qn", "kn", "vs", "g_bc", "beta_bc", "g_col", "sz")}
            d["dn_nw"] = _c(np.asarray(dn_norm_w[l], f32)[:, None]); d.update(cst); ins.append(d)
        r = _run(ncDC, ins)
        YD = np.concatenate([np.asarray(r[c]["y_dn"]) for c in range(NCORE)], axis=0)
        glu = np.asarray(s5_glu_w[l], f32)
        glu_ba = _c(np.concatenate([glu[:, 2048:], glu[:, :2048]], 1))
        dpw, wo = _c(np.asarray(dn_proj_w[l], f32)), _c(np.asarray(w_out[l], f32))
        Xmid = np.empty_like(XT)
        for hh in range(2):
            sl = [slice(c * LC + hh * 512, c * LC + hh * 512 + 512) for c in range(NCORE)]
            r = _run(ncC1, [{"xT": _c(XT[:, s]), "ys": _c(YS[:, s]), "yd": _c(YD[:, s]), "gs": _c(P[5120:7168, s]), "gd": _c(P[7168:9216, s]),
                             "glu_w": glu_ba, "dn_proj": dpw, "w_out": wo} for s in sl])
            for c, s in enumerate(sl):
                Xmid[:, s] = np.asarray(r[c]["xo"])
        final = (l == depth - 1)
        ncC2 = _prog("C2f" if final else "C2", lambda: build_C2(512, final))
        fcw = np.asarray(ffn_conv_w[l], f32)
        cw = _c(fcw.reshape(3, 88, 128).transpose(2, 1, 0))
        upw, dnw_ = _c(np.asarray(ffn_up[l], f32)), _c(np.asarray(ffn_down[l], f32))
        fnw, nw2 = _pcol(np.asarray(final_norm_w, f32)), _pcol(np.asarray(ffn_norm_w[l], f32))
        Xn = np.empty_like(XT)
        for hh in range(2):
            ins, sl = [], []
            for c in range(NCORE):
                t0 = c * LC + hh * 512
                xh = np.zeros((D, 514), f32)
                if t0 >= 2:
                    xh[:, :] = Xmid[:, t0 - 2:t0 + 512]
                else:
                    xh[:, 2:] = Xmid[:, t0:t0 + 512]
                ins.append({"xT": xh, "nw": nw2, "ffn_up": upw, "convw": cw, "ffn_down": dnw_, "fnw": fnw}); sl.append(slice(t0, t0 + 512))
            r = _run(ncC2, ins)
            for c, s in enumerate(sl):
                Xn[:, s] = np.asarray(r[c]["xo"])
        XT = Xn
    return _c(XT.T[None].astype(f32))
```

```python
import math
import numpy as np
import concourse.bass as bass
import concourse.mybir as mybir
from concourse.bass_utils import run_bass_kernel_spmd
from contextlib import ExitStack

F32 = mybir.dt.float32
BF16 = mybir.dt.bfloat16
AF = mybir.ActivationFunctionType
ALU = mybir.AluOpType


class Buf:
    __slots__ = ("name", "w", "r")

    def __init__(self, name):
        self.name = name
        self.w = None
        self.r = {}


class Trk:
    SEM_ROLL = 20000

    def __init__(self, nc, stack, n_dma_sems=12):
        self.nc, self.stack = nc, stack
        self.eng = {"pe": nc.tensor, "act": nc.scalar, "dve": nc.vector,
                    "pool": nc.gpsimd, "sp": nc.sync}
        self.sem, self.cnt, self.seen = {}, {}, {}
        self.nsem = 0
        for e in self.eng:
            self.seen[e] = {}
        for e in ("pe", "act", "dve", "pool"):
            self._newsem(e)
        self.dma = {}
        for q in ("sp", "pool"):
            self.dma[q] = [[self._alloc(f"d_{q}{i}"), 0] for i in range(n_dma_sems)]
        self.dma_i = {"sp": 0, "pool": 0}
        self.ninst = 0

    def _alloc(self, name):
        self.nsem += 1
        return self.stack.enter_context(self.nc.semaphore(f"{name}_{self.nsem}"))

    def _newsem(self, e):
        self.sem[e] = self._alloc(f"c_{e}")
        self.cnt[e] = 0

    def _wait(self, e, deps, skip_same_pe=False):
        eng = self.eng[e]
        best = {}
        for d in deps:
            if d is None:
                continue
            sem, val, src = d
            if skip_same_pe and src == "pe" and e == "pe":
                continue
            k = id(sem)
            if k not in best or best[k][1] < val:
                best[k] = d
        for k, (sem, val, src) in best.items():
            if self.seen[e].get(k, 0) >= val:
                continue
            eng.wait_ge(sem, val)
            self.seen[e][k] = val

    def _deps(self, reads, writes):
        deps = []
        for b in reads:
            deps.append(b.w)
        for b in writes:
            deps.append(b.w)
            deps.extend(b.r.values())
        return deps

    def _record(self, dep, reads, writes):
        k = id(dep[0])
        for b in reads:
            b.r[k] = dep
        for b in writes:
            b.w = dep
            b.r = {}

    def op(self, e, fn, reads=(), writes=(), acc=False):
        self._wait(e, self._deps(reads, writes), skip_same_pe=acc)
        inst = fn(self.eng[e])
        if self.cnt[e] >= self.SEM_ROLL:
            self._newsem(e)
        self.cnt[e] += 1
        inst.then_inc(self.sem[e], 1)
        dep = (self.sem[e], self.cnt[e], e)
        self._record(dep, reads, writes)
        self.ninst += 1
        return dep

    def dma_op(self, q, out, in_, reads=(), writes=(), **kw):
        slots = self.dma[q]
        i = self.dma_i[q]
        self.dma_i[q] = (i + 1) % len(slots)
        slot = slots[i]
        deps = self._deps(reads, writes)
        if slot[1] > 0:
            deps.append((slot[0], slot[1], "dma"))
        self._wait(q, deps)
        slot[1] += 16
        self.eng[q].dma_start(out=out, in_=in_, **kw).then_inc(slot[0], 16)
        dep = (slot[0], slot[1], "dma")
        self._record(dep, reads, writes)
        self.ninst += 1
        return dep

    def finish(self, bufs):
        deps = [b.w for b in bufs]
        for e in ("pe", "act", "dve", "pool"):
            if self.cnt[e]:
                deps.append((self.sem[e], self.cnt[e], e))
        for q in self.dma:
            for s, v in self.dma[q]:
                if v:
                    deps.append((s, v, "dma"))
        self._wait("sp", deps)


D = 2048
KC = D // 128
EPS = 1e-6


def emit_rmsnorm(t, nc, st, xT, bx, nw, bnw, hT, bh, ones_bf, bones, L, tag):
    sq = st.enter_context(nc.sbuf_tensor(f"sq_{tag}", [128, 2, 512], BF16))
    rs = st.enter_context(nc.sbuf_tensor(f"rs_{tag}", [128, 512], F32))
    ps = st.enter_context(nc.psum_tensor(f"psn_{tag}", [128, 512], F32))
    bsq = [Buf("sq0"), Buf("sq1")]
    brs, bps = Buf("rs"), Buf("psn")
    for tb in range(L // 512):
        ts = slice(tb * 512, (tb + 1) * 512)
        for kc in range(KC):
            j = kc % 2
            t.op("act", lambda e, kc=kc, j=j: e.activation(out=sq[:, j, :], in_=xT[:, kc, ts], func=AF.Square),
                 reads=[bx], writes=[bsq[j]])
            t.op("pe", lambda e, kc=kc, j=j: e.matmul(ps[:], lhsT=ones_bf[:], rhs=sq[:, j, :],
                                                      start=(kc == 0), stop=(kc == KC - 1)),
                 reads=[bsq[j], bones], writes=[bps], acc=(kc > 0))
        t.op("dve", lambda e: e.tensor_scalar(out=rs[:], in0=ps[:], scalar1=1.0 / D, scalar2=EPS,
                                              op0=ALU.mult, op1=ALU.add), reads=[bps], writes=[brs])
        t.op("act", lambda e: e.activation(out=rs[:], in_=rs[:], func=AF.Sqrt), reads=[brs], writes=[brs])
        t.op("dve", lambda e: e.reciprocal(out=rs[:], in_=rs[:]), reads=[brs], writes=[brs])
        for kc in range(KC):
            t.op("dve", lambda e, kc=kc: e.scalar_tensor_tensor(out=hT[:, kc, ts], in0=xT[:, kc, ts],
                                                                scalar=nw[:, kc:kc + 1], in1=rs[:],
                                                                op0=ALU.mult, op1=ALU.mult),
                 reads=[bx, bnw, brs], writes=[bh])


I32 = mybir.dt.int32
T = 16
TWO_PI = 2.0 * math.pi


def build_S5(L):
    NCH = L // T
    NJ = int(math.log2(NCH))
    NS = [float(n) for n in range(17)] + [float(16 * 2 ** j) for j in range(1, NJ)]
    NP = len(NS)
    pidx = {int(n): i for i, n in enumerate(NS)}
    nc = bass.Bass("TRN2", target_bir_lowering=False)
    din = lambda n, s, d=F32: nc.dram_tensor(n, s, d, kind="ExternalInput").ap()
    d_u = din("u", [128, L], BF16)
    d_lr, d_li, d_ldt = din("lr", [128, 4]), din("li", [128, 4]), din("ldt", [128, 4])
    d_zb = [din("zb_re", [128, 4, 128]), din("zb_im", [128, 4, 128])]
    d_zc = [din("zc_re", [128, 4, 128]), din("zc_im", [128, 4, 128])]
    d_d = din("dskip", [128, 1])
    d_ns = din("ns", [128, NP])
    d_id = din("ident", [128, 128])
    d_y = nc.dram_tensor("y", [128, L], BF16, kind="ExternalOutput").ap()
    with ExitStack() as st:
        t = Trk(nc, st)
        B = {}

        def sb(n, s, d=F32):
            B[n] = Buf(n)
            return st.enter_context(nc.sbuf_tensor("s_" + n, s, d))

        def ps(n, s):
            B[n] = Buf(n)
            full = st.enter_context(nc.psum_tensor("p_" + n, [128, 512], F32))
            return full[:, :s[1]] if s[1] != 512 else full

        def dve(fn, r, w): t.op("dve", fn, reads=[B[x] for x in r], writes=[B[x] for x in w])
        def act(fn, r, w): t.op("act", fn, reads=[B[x] for x in r], writes=[B[x] for x in w])
        def pool(fn, r, w): t.op("pool", fn, reads=[B[x] for x in r], writes=[B[x] for x in w])
        def mm(o, on, l, ln, r, rn, start=True, stop=True, acc=False):
            t.op("pe", lambda e: e.matmul(o, lhsT=l, rhs=r, start=start, stop=stop), reads=[B[ln], B[rn]], writes=[B[on]], acc=acc)

        u32 = sb("u32", [128, L]); ub = sb("ub", [128, L], BF16)
        lr, li, ldt = sb("lr", [128, 4]), sb("li", [128, 4]), sb("ldt", [128, 4])
        zb = [sb("zb0", [128, 4, 128]), sb("zb1", [128, 4, 128])]
        zc = [sb("zc0", [128, 4, 128]), sb("zc1", [128, 4, 128])]
        dsk = sb("dsk", [128, 1]); ns = sb("ns", [128, NP]); ident = sb("ident", [128, 128])
        for tl, d, n in [(ub, d_u, "ub"), (lr, d_lr, "lr"), (li, d_li, "li"), (ldt, d_ldt, "ldt"), (zb[0], d_zb[0], "zb0"), (zb[1], d_zb[1], "zb1"),
                         (zc[0], d_zc[0], "zc0"), (zc[1], d_zc[1], "zc1"), (dsk, d_d, "dsk"), (ns, d_ns, "ns"), (ident, d_id, "ident")]:
            t.dma_op("sp", tl[:], d, writes=[B[n]])
        dve(lambda e: e.tensor_copy(out=u32[:], in_=ub[:]), ["ub"], ["u32"])

        dt_ = sb("dt", [128, 4]); lrdt = sb("lrdt", [128, 4]); th = sb("th", [128, 4])
        act(lambda e: e.activation(out=dt_[:], in_=ldt[:], func=AF.Exp), ["ldt"], ["dt"])
        dve(lambda e: e.tensor_tensor(out=lrdt[:], in0=lr[:], in1=dt_[:], op=ALU.mult), ["lr", "dt"], ["lrdt"])
        dve(lambda e: e.tensor_tensor(out=th[:], in0=li[:], in1=dt_[:], op=ALU.mult), ["li", "dt"], ["th"])
        pw = [sb("pw_re", [128, 4, NP]), sb("pw_im", [128, 4, NP])]
        mag = sb("mag", [128, 4, NP]); ph = sb("ph", [128, 4, NP]); kq = sb("kq", [128, 4, NP]); ki = sb("ki", [128, 4, NP], I32)
        rr = sb("rr", [128, 4, NP]); sn = sb("sn", [128, 4, NP]); cs = sb("cs", [128, 4, NP])
        for pr in range(4):
            dve(lambda e, pr=pr: e.tensor_scalar(out=mag[:, pr, :], in0=ns[:], scalar1=lrdt[:, pr:pr + 1], scalar2=None, op0=ALU.mult), ["ns", "lrdt", "mag"], ["mag"])
            dve(lambda e, pr=pr: e.tensor_scalar(out=ph[:, pr, :], in0=ns[:], scalar1=th[:, pr:pr + 1], scalar2=None, op0=ALU.mult), ["ns", "th", "ph"], ["ph"])
        act(lambda e: e.activation(out=mag[:], in_=mag[:], func=AF.Exp), ["mag"], ["mag"])
        dve(lambda e: e.tensor_scalar(out=kq[:], in0=ph[:], scalar1=1.0 / TWO_PI, scalar2=None, op0=ALU.mult), ["ph"], ["kq"])
        dve(lambda e: e.tensor_copy(out=ki[:], in_=kq[:]), ["kq"], ["ki"])
        dve(lambda e: e.tensor_copy(out=kq[:], in_=ki[:]), ["ki"], ["kq"])
        dve(lambda e: e.scalar_tensor_tensor(out=rr[:], in0=kq[:], scalar=-TWO_PI, in1=ph[:], op0=ALU.mult, op1=ALU.add), ["kq", "ph"], ["rr"])
        msk = sb("msk", [128, 4, NP])

        def wrap(dst, dn, src, srcn, shift):
            dve(lambda e: e.tensor_scalar(out=dst[:], in0=src[:], scalar1=shift, scalar2=None, op0=ALU.add), [srcn], [dn])
            dve(lambda e: e.tensor_scalar(out=msk[:], in0=dst[:], scalar1=math.pi, scalar2=None, op0=ALU.is_gt), [dn], ["msk"])
            dve(lambda e: e.scalar_tensor_tensor(out=dst[:], in0=msk[:], scalar=-TWO_PI, in1=dst[:], op0=ALU.mult, op1=ALU.add), ["msk", dn], [dn])
            dve(lambda e: e.tensor_scalar(out=msk[:], in0=dst[:], scalar1=-math.pi, scalar2=None, op0=ALU.is_lt), [dn], ["msk"])
            dve(lambda e: e.scalar_tensor_tensor(out=dst[:], in0=msk[:], scalar=TWO_PI, in1=dst[:], op0=ALU.mult, op1=ALU.add), ["msk", dn], [dn])

        wrap(sn, "sn", rr, "rr", 0.0)
        wrap(cs, "cs", rr, "rr", math.pi / 2)
        act(lambda e: e.activation(out=sn[:], in_=sn[:], func=AF.Sin), ["sn"], ["sn"])
        act(lambda e: e.activation(out=cs[:], in_=cs[:], func=AF.Sin), ["cs"], ["cs"])
        dve(lambda e: e.tensor_tensor(out=pw[0][:], in0=mag[:], in1=cs[:], op=ALU.mult), ["mag", "cs"], ["pw_re"])
        dve(lambda e: e.tensor_tensor(out=pw[1][:], in0=mag[:], in1=sn[:], op=ALU.mult), ["mag", "sn"], ["pw_im"])
        i1 = pidx[1]
        nr = sb("nr", [128, 4]); den = sb("den", [128, 4]); tmp = sb("tmp", [128, 4]); cre = sb("cre", [128, 4]); cim = sb("cim", [128, 4]); ncim = sb("ncim", [128, 4])
        dve(lambda e: e.tensor_scalar(out=nr[:], in0=pw[0][:, :, i1], scalar1=-1.0, scalar2=None, op0=ALU.add), ["pw_re"], ["nr"])
        dve(lambda e: e.tensor_tensor(out=den[:], in0=lr[:], in1=lr[:], op=ALU.mult), ["lr"], ["den"])
        dve(lambda e: e.tensor_tensor(out=tmp[:], in0=li[:], in1=li[:], op=ALU.mult), ["li"], ["tmp"])
        dve(lambda e: e.tensor_tensor(out=den[:], in0=den[:], in1=tmp[:], op=ALU.add), ["den", "tmp"], ["den"])
        dve(lambda e: e.reciprocal(out=den[:], in_=den[:]), ["den"], ["den"])
        dve(lambda e: e.tensor_tensor(out=cre[:], in0=nr[:], in1=lr[:], op=ALU.mult), ["nr", "lr"], ["cre"])
        dve(lambda e: e.tensor_tensor(out=tmp[:], in0=pw[1][:, :, i1], in1=li[:], op=ALU.mult), ["pw_im", "li"], ["tmp"])
        dve(lambda e: e.tensor_tensor(out=cre[:], in0=cre[:], in1=tmp[:], op=ALU.add), ["cre", "tmp"], ["cre"])
        dve(lambda e: e.tensor_tensor(out=cre[:], in0=cre[:], in1=den[:], op=ALU.mult), ["cre", "den"], ["cre"])
        dve(lambda e: e.tensor_tensor(out=cim[:], in0=pw[1][:, :, i1], in1=lr[:], op=ALU.mult), ["pw_im", "lr"], ["cim"])
        dve(lambda e: e.tensor_tensor(out=tmp[:], in0=nr[:], in1=li[:], op=ALU.mult), ["nr", "li"], ["tmp"])
        dve(lambda e: e.tensor_tensor(out=cim[:], in0=cim[:], in1=tmp[:], op=ALU.subtract), ["cim", "tmp"], ["cim"])
        dve(lambda e: e.tensor_tensor(out=cim[:], in0=cim[:], in1=den[:], op=ALU.mult), ["cim", "den"], ["cim"])
        dve(lambda e: e.tensor_scalar(out=ncim[:], in0=cim[:], scalar1=-1.0, scalar2=None, op0=ALU.mult), ["cim"], ["ncim"])
        bb = [sb("bb0", [128, 4, 128]), sb("bb1", [128, 4, 128])]
        for pr in range(4):
            s_ = slice(pr, pr + 1)
            dve(lambda e, pr=pr, s_=s_: e.tensor_scalar(out=bb[0][:, pr, :], in0=zb[0][:, pr, :], scalar1=cre[:, s_], scalar2=None, op0=ALU.mult), ["zb0", "cre", "bb0"], ["bb0"])
            dve(lambda e, pr=pr, s_=s_: e.scalar_tensor_tensor(out=bb[0][:, pr, :], in0=zb[1][:, pr, :], scalar=ncim[:, s_], in1=bb[0][:, pr, :], op0=ALU.mult, op1=ALU.add), ["zb1", "ncim", "bb0"], ["bb0"])
            dve(lambda e, pr=pr, s_=s_: e.tensor_scalar(out=bb[1][:, pr, :], in0=zb[1][:, pr, :], scalar1=cre[:, s_], scalar2=None, op0=ALU.mult), ["zb1", "cre", "bb1"], ["bb1"])
            dve(lambda e, pr=pr, s_=s_: e.scalar_tensor_tensor(out=bb[1][:, pr, :], in0=zb[0][:, pr, :], scalar=cim[:, s_], in1=bb[1][:, pr, :], op0=ALU.mult, op1=ALU.add), ["zb0", "cim", "bb1"], ["bb1"])
        npw_im = sb("npw_im", [128, 4, NP])
        dve(lambda e: e.tensor_scalar(out=npw_im[:], in0=pw[1][:], scalar1=-1.0, scalar2=None, op0=ALU.mult), ["pw_im"], ["npw_im"])
        nzc1 = sb("nzc1", [128, 4, 128])
        dve(lambda e: e.tensor_scalar(out=nzc1[:], in0=zc[1][:], scalar1=-1.0, scalar2=None, op0=ALU.mult), ["zc1"], ["nzc1"])

        Kt = sb("Kt", [128, T, 128], BF16)
        Wm = sb("Wm", [128, T, 4, 2, 128], BF16)
        Cm = sb("Cm", [128, T, 4, 2, 128], BF16)
        en = [sb("en0", [128, 4, 128]), sb("en1", [128, 4, 128])]
        pk = ps("pk", [128, 128]); ptr = [ps("ptr0", [128, 128]), ps("ptr1", [128, 128])]
        for n in range(T):
            ix = pidx[n]
            for pr in range(4):
                a_re, a_im, na_im = pw[0][:, pr, ix:ix + 1], pw[1][:, pr, ix:ix + 1], npw_im[:, pr, ix:ix + 1]
                dve(lambda e, pr=pr, a_re=a_re: e.tensor_scalar(out=en[0][:, pr, :], in0=bb[0][:, pr, :], scalar1=a_re, scalar2=None, op0=ALU.mult), ["bb0", "pw_re", "en0"], ["en0"])
                dve(lambda e, pr=pr, na_im=na_im: e.scalar_tensor_tensor(out=en[0][:, pr, :], in0=bb[1][:, pr, :], scalar=na_im, in1=en[0][:, pr, :], op0=ALU.mult, op1=ALU.add), ["bb1", "npw_im", "en0"], ["en0"])
                dve(lambda e, pr=pr, a_re=a_re: e.tensor_scalar(out=en[1][:, pr, :], in0=bb[1][:, pr, :], scalar1=a_re, scalar2=None, op0=ALU.mult), ["bb1", "pw_re", "en1"], ["en1"])
                dve(lambda e, pr=pr, a_im=a_im: e.scalar_tensor_tensor(out=en[1][:, pr, :], in0=bb[0][:, pr, :], scalar=a_im, in1=en[1][:, pr, :], op0=ALU.mult, op1=ALU.add), ["bb0", "pw_im", "en1"], ["en1"])
            for pr in range(4):
                mm(pk, "pk", en[0][:, pr, :], "en0", zc[0][:, pr, :], "zc0", start=(pr == 0), stop=False, acc=(pr > 0))
                mm(pk, "pk", en[1][:, pr, :], "en1", nzc1[:, pr, :], "nzc1", start=False, stop=(pr == 3), acc=True)
            act(lambda e, n=n: e.copy(out=Kt[:, n, :], in_=pk), ["pk", "Kt"], ["Kt"])
            for pr in range(4):
                for c in range(2):
                    pn = "ptr%d" % c
                    t.op("pe", lambda e, pr=pr, c=c: e.transpose(ptr[c], en[c][:, pr, :], ident[:]), reads=[B["en%d" % c], B["ident"]], writes=[B[pn]])
                    if c == 0:
                        act(lambda e, n=n, pr=pr: e.copy(out=Wm[:, n, pr, 0, :], in_=ptr[0]), [pn, "Wm"], ["Wm"])
                    else:
                        dve(lambda e, n=n, pr=pr: e.tensor_copy(out=Wm[:, n, pr, 1, :], in_=ptr[1]), [pn, "Wm"], ["Wm"])
        cn = sb("cn", [128, 128])
        for tt in range(T):
            ix = pidx[tt + 1]
            for pr in range(4):
                a_re, a_im, na_im = pw[0][:, pr, ix:ix + 1], pw[1][:, pr, ix:ix + 1], npw_im[:, pr, ix:ix + 1]
                dve(lambda e, pr=pr, a_re=a_re: e.tensor_scalar(out=cn[:], in0=zc[0][:, pr, :], scalar1=a_re, scalar2=None, op0=ALU.mult), ["zc0", "pw_re", "cn"], ["cn"])
                dve(lambda e, pr=pr, tt=tt, na_im=na_im: e.scalar_tensor_tensor(out=Cm[:, tt, pr, 0, :], in0=zc[1][:, pr, :], scalar=na_im, in1=cn[:], op0=ALU.mult, op1=ALU.add), ["zc1", "npw_im", "cn", "Cm"], ["Cm"])
                dve(lambda e, pr=pr, a_re=a_re: e.tensor_scalar(out=cn[:], in0=nzc1[:, pr, :], scalar1=a_re, scalar2=None, op0=ALU.mult), ["nzc1", "pw_re", "cn"], ["cn"])
                dve(lambda e, pr=pr, tt=tt, na_im=na_im: e.scalar_tensor_tensor(out=Cm[:, tt, pr, 1, :], in0=zc[0][:, pr, :], scalar=na_im, in1=cn[:], op0=ALU.mult, op1=ALU.add), ["zc0", "npw_im", "cn", "Cm"], ["Cm"])

        X = [[sb("x%d_%d" % (k, c), [128, 4, NCH]) for c in range(2)] for k in range(2)]
        pw_ = [ps("pw0", [128, 512]), ps("pw1", [128, 512])]
        uv = ub[:].rearrange("p (n s) -> p n s", s=T)
        k = 0
        for pr in range(4):
            for c in range(2):
                for c0 in range(0, NCH, 512):
                    cw = min(512, NCH - c0)
                    pn = "pw%d" % (k % 2)
                    for s in range(T):
                        mm(pw_[k % 2][:, :cw], pn, Wm[:, T - 1 - s, pr, c, :], "Wm", uv[:, c0:c0 + cw, s], "ub", start=(s == 0), stop=(s == T - 1), acc=(s > 0))
                    if k % 2 == 0:
                        act(lambda e, pr=pr, c=c, c0=c0, cw=cw, k=k: e.copy(out=X[0][c][:, pr, c0:c0 + cw], in_=pw_[k % 2][:, :cw]), [pn, "x0_%d" % c], ["x0_%d" % c])
                    else:
                        dve(lambda e, pr=pr, c=c, c0=c0, cw=cw, k=k: e.tensor_copy(out=X[0][c][:, pr, c0:c0 + cw], in_=pw_[k % 2][:, :cw]), [pn, "x0_%d" % c], ["x0_%d" % c])
                    k += 1
        cur = 0
        for j in range(NJ):
            sh = 2 ** j
            ix = pidx[16 * sh]
            o, n_ = X[cur], X[1 - cur]
            on, nn = ["x%d_0" % cur, "x%d_1" % cur], ["x%d_0" % (1 - cur), "x%d_1" % (1 - cur)]
            for pr in range(4):
                a_re, a_im, na_im = pw[0][:, pr, ix:ix + 1], pw[1][:, pr, ix:ix + 1], npw_im[:, pr, ix:ix + 1]
                for c in range(2):
                    eng = dve if c == 0 else dve
                    eng(lambda e, pr=pr, c=c: e.tensor_copy(out=n_[c][:, pr, :sh], in_=o[c][:, pr, :sh]), [on[c], nn[c]], [nn[c]])
                dve(lambda e, pr=pr, a_re=a_re: e.scalar_tensor_tensor(out=n_[0][:, pr, sh:], in0=o[0][:, pr, :NCH - sh], scalar=a_re, in1=o[0][:, pr, sh:], op0=ALU.mult, op1=ALU.add), [on[0], "pw_re", nn[0]], [nn[0]])
                dve(lambda e, pr=pr, na_im=na_im: e.scalar_tensor_tensor(out=n_[0][:, pr, sh:], in0=o[1][:, pr, :NCH - sh], scalar=na_im, in1=n_[0][:, pr, sh:], op0=ALU.mult, op1=ALU.add), [on[1], "npw_im", nn[0]], [nn[0]])
                dve(lambda e, pr=pr, a_re=a_re: e.scalar_tensor_tensor(out=n_[1][:, pr, sh:], in0=o[1][:, pr, :NCH - sh], scalar=a_re, in1=o[1][:, pr, sh:], op0=ALU.mult, op1=ALU.add), [on[1], "pw_re", nn[1]], [nn[1]])
                dve(lambda e, pr=pr, a_im=a_im: e.scalar_tensor_tensor(out=n_[1][:, pr, sh:], in0=o[0][:, pr, :NCH - sh], scalar=a_im, in1=n_[1][:, pr, sh:], op0=ALU.mult, op1=ALU.add), [on[0], "pw_im", nn[1]], [nn[1]])
            cur = 1 - cur
        xb = [sb("xb0", [128, 4, NCH + 1], BF16), sb("xb1", [128, 4, NCH + 1], BF16)]
        for c in range(2):
            dve(lambda e, c=c: e.memset(xb[c][:, :, 0:1], 0.0), ["xb%d" % c], ["xb%d" % c])
            dve(lambda e, c=c: e.tensor_copy(out=xb[c][:, :, 1:], in_=X[cur][c][:]), ["x%d_%d" % (cur, c), "xb%d" % c], ["xb%d" % c])

        py = [ps("py0", [128, 512]), ps("py1", [128, 512])]
        ypre = sb("ypre", [128, 2, 512]); yo = sb("yo", [128, 2, 512], BF16)
        B["out"] = Buf("out")
        CPB = 512 // T
        for blk in range(L // 512):
            pi = blk % 2
            pn = "py%d" % pi
            pv = py[pi][:].rearrange("p (n s) -> p n s", s=T)
            ch0 = blk * CPB
            for tau in range(T):
                mm(pv[:, :, tau:T], pn, Kt[:, tau, :], "Kt", uv[:, ch0:ch0 + CPB, 0:T - tau], "ub", start=(tau == 0), stop=False, acc=(tau > 0))
            for tt in range(T):
                for pr in range(4):
                    for c in range(2):
                        last = (tt == T - 1 and pr == 3 and c == 1)
                        mm(pv[:, :, tt], pn, Cm[:, tt, pr, c, :], "Cm", xb[c][:, pr, ch0:ch0 + CPB], "xb%d" % c, start=False, stop=last, acc=True)
            ts = slice(blk * 512, (blk + 1) * 512)
            dve(lambda e, pi=pi, ts=ts: e.scalar_tensor_tensor(out=ypre[:, pi, :], in0=u32[:, ts], scalar=dsk[:, 0:1], in1=py[pi][:], op0=ALU.mult, op1=ALU.add), ["u32", "dsk", pn, "ypre"], ["ypre"])
            act(lambda e, pi=pi: e.activation(out=yo[:, pi, :], in_=ypre[:, pi, :], func=AF.Gelu), ["ypre", "yo"], ["yo"])
            t.dma_op("sp", d_y[:, ts], yo[:, pi, :], reads=[B["yo"]], writes=[B["out"]])
        t.finish([B["out"]])
    return nc, t, NS


class Ctx:
    def __init__(self):
        self.nc = bass.Bass("TRN2", target_bir_lowering=False)
        self.st = ExitStack()
        self.t = Trk(self.nc, self.st)
        self.B = {}

    def din(self, n, s, d=F32):
        return self.nc.dram_tensor(n, s, d, kind="ExternalInput").ap()

    def dout(self, n, s, d=F32):
        self.B["o_" + n] = Buf("o_" + n)
        return self.nc.dram_tensor(n, s, d, kind="ExternalOutput").ap()

    def sb(self, n, s, d=F32):
        self.B[n] = Buf(n)
        return self.st.enter_context(self.nc.sbuf_tensor("s_" + n, s, d))

    def ps(self, n, w=512):
        self.B[n] = Buf(n)
        full = self.st.enter_context(self.nc.psum_tensor("p_" + n, [128, 512], F32))
        return full[:] if w == 512 else full[:, :w]

    def _b(self, names):
        return [self.B[x] for x in names]

    _rec = None

    def rec_start(self):
        self._rec = []

    def rec_stop(self):
        r, self._rec = self._rec, None
        return r

    @staticmethod
    def interleave(*chains):
        for i in range(max(len(c) for c in chains)):
            for c in chains:
                if i < len(c):
                    c[i]()

    def _do(self, thunk):
        if self._rec is not None:
            self._rec.append(thunk)
        else:
            thunk()

    def dve(self, fn, r, w):
        R, W = self._b(r), self._b(w)
        self._do(lambda: self.t.op("dve", fn, reads=R, writes=W))

    def act(self, fn, r, w):
        R, W = self._b(r), self._b(w)
        self._do(lambda: self.t.op("act", fn, reads=R, writes=W))

    def pool(self, fn, r, w):
        R, W = self._b(r), self._b(w)
        self._do(lambda: self.t.op("pool", fn, reads=R, writes=W))

    def pe(self, fn, r, w):
        R, W = self._b(r), self._b(w)
        self._do(lambda: self.t.op("pe", fn, reads=R, writes=W))

    def mm(self, o, on, l, ln, r, rn, start=True, stop=True, acc=False):
        R, W = self._b([ln, rn]), self._b([on])
        self._do(lambda: self.t.op("pe", lambda e: e.matmul(o, lhsT=l, rhs=r, start=start, stop=stop), reads=R, writes=W, acc=acc))

    def load(self, q, tl, n, src, **kw):
        W = self._b([n])
        self._do(lambda: self.t.dma_op(q, tl, src, writes=W, **kw))

    def store(self, dst, on, tl, n):
        R, W = self._b([n]), self._b(["o_" + on])
        self._do(lambda: self.t.dma_op("sp", dst, tl, reads=R, writes=W))

    def done(self, outs):
        self.t.finish(self._b(["o_" + o for o in outs]))
        self.st.close()
        return self.nc


def col_groups(n):
    g, c = [], 0
    while c < n:
        w = 512 if n - c >= 512 else n - c
        assert w % 128 == 0
        g.append((c, w)); c += w
    return g


def emit_proj(cx, w_dram, K, NOUT, rhs_fn, rhs_names, L, evac, tag, wq="pool"):
    KCn = K // 128
    wb = cx.sb("wb_" + tag, [128, 2, KCn, 512], BF16)
    cx.B["wb0_" + tag], cx.B["wb1_" + tag] = Buf("wb0"), Buf("wb1")
    pss = [cx.ps("pp%d_%s" % (i, tag)) for i in range(3)]
    wv = w_dram.rearrange("(kc p) n -> p kc n", p=128)
    groups = col_groups(NOUT)

    def load_w(gi):
        c0, w = groups[gi]
        cx.t.dma_op(wq, wb[:, gi % 2, :, :w], wv[:, :, c0:c0 + w], writes=[cx.B["wb%d_%s" % (gi % 2, tag)]])

    load_w(0)
    k = 0
    for gi, (c0, w) in enumerate(groups):
        if gi + 1 < len(groups):
            load_w(gi + 1)
        for m in range(w // 128):
            for tb in range(L // 512):
                ts = slice(tb * 512, (tb + 1) * 512)
                pi = k % 3
                pn = "pp%d_%s" % (pi, tag)
                for kc in range(KCn):
                    cx.mm(pss[pi], pn, wb[:, gi % 2, kc, m * 128:(m + 1) * 128], "wb%d_%s" % (gi % 2, tag), rhs_fn(kc, ts), rhs_names[0] if len(rhs_names) == 1 else rhs_names[kc],
                          start=(kc == 0), stop=(kc == KCn - 1), acc=(kc > 0))
                evac(pss[pi], pn, c0 + m * 128, ts, k)
                k += 1


def build_A(L, NOUT):
    cx = Ctx()
    x = cx.din("xT", [D, L]); nwd = cx.din("nw", [128, KC]); w = cx.din("w", [D, NOUT])
    p = cx.dout("pT", [NOUT, L], BF16)
    xT = cx.sb("x", [128, KC, L]); hT = cx.sb("h", [128, KC, L], BF16); nw = cx.sb("nw", [128, KC]); ones = cx.sb("ones", [128, 128], BF16)
    ob = cx.sb("ob", [128, 3, 512], BF16)
    cx.load("sp", xT[:], "x", x.rearrange("(kc p) l -> p kc l", p=128))
    cx.load("sp", nw[:], "nw", nwd)
    cx.dve(lambda e: e.memset(ones[:], 1.0), [], ["ones"])
    emit_rmsnorm(cx.t, cx.nc, cx.st, xT, cx.B["x"], nw, cx.B["nw"], hT, cx.B["h"], ones, cx.B["ones"], L, "a")
    obn = ["ob0", "ob1", "ob2"]
    for n in obn:
        cx.B[n] = Buf(n)

    def evac(pst, pn, row0, ts, k):
        j = k % 3
        if k % 2 == 0:
            cx.act(lambda e: e.copy(out=ob[:, j, :], in_=pst), [pn], [obn[j]])
        else:
            cx.dve(lambda e: e.tensor_copy(out=ob[:, j, :], in_=pst), [pn], [obn[j]])
        cx.store(p[row0:row0 + 128, ts], "pT", ob[:, j, :], obn[j])

    emit_proj(cx, w, D, NOUT, lambda kc, ts: hT[:, kc, ts], ["h"], L, evac, "a")
    return cx.done(["pT"])


def build_DNF(L):
    NB = L // 128
    cx = Ctx()
    dq, dk, dv, dz = (cx.din(n, [128, L], BF16) for n in ("q", "k", "v", "z"))
    dbr, dar = cx.din("b_raw_bc", [128, L], BF16), cx.din("a_raw_bc", [128, L], BF16)
    darc = cx.din("a_raw_col", [128, NB], BF16)
    dcw = cx.din("convw", [128, 3, 4])
    dsc = cx.din("scal", [128, 2])
    oq, ok, ov = (cx.dout(n, [128, L]) for n in ("qn", "kn", "vs"))
    og, obt = cx.dout("g_bc", [128, L]), cx.dout("beta_bc", [128, L])
    ogc = cx.dout("g_col", [128, NB]); osz = cx.dout("sz", [128, L], BF16)
    raw = cx.sb("raw", [128, L], BF16); acc = cx.sb("acc", [128, L]); res = cx.sb("res", [128, L])
    cw = cx.sb("cw", [128, 3, 4]); sc = cx.sb("sc", [128, 2]); nea = cx.sb("nea", [128, 1])
    ones = cx.sb("ones", [128, 128]); sqa = cx.sb("sqa", [128, L]); rsa = cx.sb("rsa", [128, L]); epsb = cx.sb("epsb", [128, 1])
    pn2 = [cx.ps("pn0"), cx.ps("pn1")]
    cx.dve(lambda e: e.memset(epsb[:], EPS), [], ["epsb"])
    cx.load("sp", cw[:], "cw", dcw); cx.load("sp", sc[:], "sc", dsc)
    cx.dve(lambda e: e.memset(ones[:], 1.0), [], ["ones"])
    cx.act(lambda e: e.activation(out=nea[:], in_=sc[:, 0:1], func=AF.Exp), ["sc"], ["nea"])
    cx.dve(lambda e: e.tensor_scalar(out=nea[:], in0=nea[:], scalar1=-1.0, scalar2=None, op0=ALU.mult), ["nea"], ["nea"])
    for i, (src, dst, on) in enumerate([(dq, oq, "qn"), (dk, ok, "kn"), (dv, ov, "vs")]):
        cx.load("sp", raw[:], "raw", src)
        cx.dve(lambda e, i=i: e.tensor_scalar(out=acc[:], in0=raw[:], scalar1=cw[:, i, 3:4], scalar2=None, op0=ALU.mult), ["raw", "cw"], ["acc"])
        for s in (1, 2, 3):
            cx.dve(lambda e, i=i, s=s: e.scalar_tensor_tensor(out=acc[:, s:], in0=raw[:, :L - s], scalar=cw[:, i, 3 - s:4 - s], in1=acc[:, s:], op0=ALU.mult, op1=ALU.add), ["raw", "cw", "acc"], ["acc"])
        cx.act(lambda e: e.activation(out=res[:], in_=acc[:], func=AF.Silu), ["acc"], ["res"])
        if i < 2:
            cx.act(lambda e: e.activation(out=sqa[:], in_=res[:], func=AF.Square), ["res"], ["sqa"])
            for tb in range(L // 512):
                ts = slice(tb * 512, (tb + 1) * 512)
                pj = pn2[tb % 2]
                cx.mm(pj, "pn%d" % (tb % 2), ones[:], "ones", sqa[:, ts], "sqa")
                if tb % 2 == 0:
                    cx.dve(lambda e, ts=ts, pj=pj: e.tensor_scalar(out=rsa[:, ts], in0=pj, scalar1=EPS, scalar2=None, op0=ALU.add), ["pn0", "rsa"], ["rsa"])
                else:
                    cx.act(lambda e, ts=ts, pj=pj: e.activation(out=rsa[:, ts], in_=pj, func=AF.Identity, bias=epsb[:, 0:1]), ["pn1", "rsa", "epsb"], ["rsa"])
            cx.act(lambda e: e.activation(out=rsa[:], in_=rsa[:], func=AF.Sqrt), ["rsa"], ["rsa"])
            cx.dve(lambda e: e.reciprocal(out=rsa[:], in_=rsa[:]), ["rsa"], ["rsa"])
            if i == 0:
                cx.dve(lambda e: e.scalar_tensor_tensor(out=res[:], in0=res[:], scalar=128.0 ** -0.5, in1=rsa[:], op0=ALU.mult, op1=ALU.mult), ["res", "rsa"], ["res"])
            else:
                cx.dve(lambda e: e.tensor_tensor(out=res[:], in0=res[:], in1=rsa[:], op=ALU.mult), ["res", "rsa"], ["res"])
        cx.store(dst, on, res[:], "res")
    szt = cx.sb("szt", [128, L], BF16)
    cx.load("sp", raw[:], "raw", dz)
    cx.act(lambda e: e.activation(out=szt[:], in_=raw[:], func=AF.Silu), ["raw"], ["szt"])
    cx.store(osz, "sz", szt[:], "szt")
    cx.load("sp", raw[:], "raw", dbr)
    cx.act(lambda e: e.activation(out=res[:], in_=raw[:], func=AF.Sigmoid), ["raw"], ["res"])
    cx.store(obt, "beta_bc", res[:], "res")
    cx.load("sp", raw[:], "raw", dar)
    cx.act(lambda e: e.activation(out=acc[:], in_=raw[:], func=AF.Exp, bias=sc[:, 1:2]), ["raw", "sc"], ["acc"])
    cx.act(lambda e: e.activation(out=acc[:], in_=acc[:], func=AF.Ln, bias=1.0), ["acc"], ["acc"])
    cx.dve(lambda e: e.tensor_scalar(out=res[:], in0=acc[:], scalar1=nea[:, 0:1], scalar2=None, op0=ALU.mult), ["acc", "nea"], ["res"])
    cx.store(og, "g_bc", res[:], "res")
    rc = cx.sb("rc", [128, NB], BF16); gcl = cx.sb("gcl", [128, NB])
    cx.load("sp", rc[:], "rc", darc)
    cx.act(lambda e: e.activation(out=gcl[:], in_=rc[:], func=AF.Exp, bias=sc[:, 1:2]), ["rc", "sc"], ["gcl"])
    cx.act(lambda e: e.activation(out=gcl[:], in_=gcl[:], func=AF.Ln, bias=1.0), ["gcl"], ["gcl"])
    cx.dve(lambda e: e.tensor_scalar(out=gcl[:], in0=gcl[:], scalar1=nea[:, 0:1], scalar2=None, op0=ALU.mult), ["gcl", "nea"], ["gcl"])
    cx.store(ogc, "g_col", gcl[:], "gcl")
    return cx.done(["qn", "kn", "vs", "g_bc", "beta_bc", "g_col", "sz"])


NEG = -30000.0


def build_DNC(L, cx=None, io=None):
    NB = L // 128
    cx = cx or Ctx()
    if io is not None:
        cx.begin(io)
    dq, dk, dv, dg, db = (cx.din(n, [128, L]) for n in ("qn", "kn", "vs", "g_bc", "beta_bc"))
    dgc = cx.din("g_col", [128, NB]); dsz = cx.din("sz", [128, L], BF16); dnw = cx.din("dn_nw", [128, 1])
    dmu, dmui, dml, dtri, did = (cx.din(n, [128, 128]) for n in ("maskU", "maskUi", "maskL", "tri", "ident"))
    dy = cx.dout("y_dn", [128, L], BF16)
    gcol = cx.sb("gcol", [128, NB]); gccol = cx.sb("gccol", [128, NB]); ngccol = cx.sb("ngccol", [128, NB]); nw = cx.sb("nw", [128, 1])
    mU, mUi, mL, tri, ident = (cx.sb(n, [128, 128]) for n in ("mU", "mUi", "mL", "tri", "id"))
    ones1 = cx.sb("ones1", [128, 128])
    for tl, d, n in [(gcol, dgc, "gcol"), (nw, dnw, "nw"), (mU, dmu, "mU"), (mUi, dmui, "mUi"), (mL, dml, "mL"), (tri, dtri, "tri"), (ident, did, "id")]:
        cx.load("sp", tl[:], n, d)
    cx.dve(lambda e: e.memset(ones1[:], 1.0), [], ["ones1"])
    PS = [{n: cx.ps("%s%d" % (n, c), 128) for n in ("pA", "pB", "pC", "pD")} for c in range(2)]
    pcol = PS[1]["pD"][:, :NB]
    cx.mm(pcol, "pD1", tri[:], "tri", gcol[:], "gcol")
    cx.dve(lambda e: e.tensor_copy(out=gccol[:], in_=pcol), ["pD1"], ["gccol"])
    cx.dve(lambda e: e.tensor_scalar(out=ngccol[:], in0=pcol, scalar1=-1.0, scalar2=None, op0=ALU.mult), ["pD1"], ["ngccol"])
    inb = [[cx.sb("in%d_%d" % (p, i), [128, 128]) for i in range(5)] for p in range(2)]
    szb = [cx.sb("sz%d" % p, [128, 128], BF16) for p in range(2)]
    names = ["arg", "argL", "argI", "DT", "DTi", "D", "M", "Lm", "M2", "L2", "R", "Rt", "attnT", "vb", "kbg", "wT", "vnew", "kdec", "egl", "bcol", "kb", "gc", "egc", "qg", "osb", "osq", "rs"]
    TS = [{n: cx.sb("t%d_%s" % (c, n), [128, 128]) for n in names} for c in range(2)]
    yb = [cx.sb("yb%d" % p, [128, 128], BF16) for p in range(2)]
    S = cx.sb("t_S", [128, 128])
    cx.dve(lambda e: e.memset(S[:], 0.0), [], ["t_S"])
    srcs = [dq, dk, dv, dg, db]

    def load_in(b):
        p = b % 2
        bs = slice(b * 128, (b + 1) * 128)
        for i in range(5):
            cx.load("sp", inb[p][i][:], "in%d_%d" % (p, i), srcs[i][:, bs])

    def load_sz(b):
        p = b % 2
        cx.load("sp", szb[p][:], "sz%d" % p, dsz[:, b * 128:(b + 1) * 128])

    def phase_P(b):
        c = b % 2
        T_, P_ = TS[c], PS[c]
        tn = lambda n: "t%d_%s" % (c, n)
        pA, pB, pC, pD = P_["pA"], P_["pB"], P_["pC"], P_["pD"]
        nA, nB, nC, nD = "pA%d" % c, "pB%d" % c, "pC%d" % c, "pD%d" % c
        qT, kT, vT, gb, bb = (inb[c][i] for i in range(5))
        nq, nk, nv, ng, nb_ = ("in%d_%d" % (c, i) for i in range(5))
        gcc, ngcc = gccol[:, b:b + 1], ngccol[:, b:b + 1]
        kb, gc, egc, qg = T_["kb"], T_["gc"], T_["egc"], T_["qg"]
        cx.dve(lambda e: e.tensor_tensor(out=kb[:], in0=kT[:], in1=bb[:], op=ALU.mult), [nk, nb_], [tn("kb")])
        cx.dve(lambda e: e.tensor_tensor_scan(out=gc[:], data0=ones1[:], data1=gb[:], initial=0.0, op0=ALU.mult, op1=ALU.add), [ng, "ones1"], [tn("gc")])
        cx.act(lambda e: e.activation(out=egc[:], in_=gc[:], func=AF.Exp), [tn("gc")], [tn("egc")])
        cx.dve(lambda e: e.tensor_tensor(out=qg[:], in0=qT[:], in1=egc[:], op=ALU.mult), [nq, tn("egc")], [tn("qg")])
        cx.mm(pA, nA, kT[:], nk, kb[:], tn("kb"))
        cx.dve(lambda e: e.tensor_tensor(out=T_["arg"][:], in0=gc[:], in1=mU[:], op=ALU.add), [tn("gc"), "mU"], [tn("arg")])
        cx.act(lambda e: e.activation(out=T_["DT"][:], in_=T_["arg"][:], func=AF.Exp, bias=ngcc), [tn("arg"), "ngccol"], [tn("DT")])
        cx.dve(lambda e: e.tensor_tensor(out=T_["M"][:], in0=pA, in1=T_["DT"][:], op=ALU.mult), [nA, tn("DT")], [tn("M")])
        cx.mm(pB, nB, kb[:], tn("kb"), kT[:], nk)
        cx.dve(lambda e: e.tensor_tensor(out=T_["argL"][:], in0=mL[:], in1=gc[:], op=ALU.subtract), [tn("gc"), "mL"], [tn("argL")])
        cx.act(lambda e: e.activation(out=T_["D"][:], in_=T_["argL"][:], func=AF.Exp, bias=gcc), [tn("argL"), "gccol"], [tn("D")])
        cx.dve(lambda e: e.tensor_tensor(out=T_["Lm"][:], in0=pB, in1=T_["D"][:], op=ALU.mult), [nB, tn("D")], [tn("Lm")])
        cx.mm(pC, nC, kT[:], nk, qT[:], nq)
        cx.dve(lambda e: e.tensor_tensor(out=T_["argI"][:], in0=gc[:], in1=mUi[:], op=ALU.add), [tn("gc"), "mUi"], [tn("argI")])
        cx.act(lambda e: e.activation(out=T_["DTi"][:], in_=T_["argI"][:], func=AF.Exp, bias=ngcc), [tn("argI"), "ngccol"], [tn("DTi")])
        cx.dve(lambda e: e.tensor_tensor(out=T_["attnT"][:], in0=pC, in1=T_["DTi"][:], op=ALU.mult), [nC, tn("DTi")], [tn("attnT")])
        cx.dve(lambda e: e.tensor_tensor(out=T_["R"][:], in0=ident[:], in1=T_["M"][:], op=ALU.subtract), ["id", tn("M")], [tn("R")])
        cx.mm(pA, nA, T_["Lm"][:], tn("Lm"), T_["M"][:], tn("M"))
        cx.mm(pB, nB, T_["M"][:], tn("M"), T_["Lm"][:], tn("Lm"))
        cx.act(lambda e: e.copy(out=T_["M2"][:], in_=pA), [nA], [tn("M2")])
        cx.dve(lambda e: e.tensor_copy(out=T_["L2"][:], in_=pB), [nB], [tn("L2")])
        Qn, Qtn, Qo, Qto = "M2", "L2", "M", "Lm"
        for k_ in range(1, 7):
            cx.mm(pC, nC, T_[Qtn][:], tn(Qtn), T_["R"][:], tn("R"))
            if k_ < 6:
                cx.mm(pA, nA, T_[Qtn][:], tn(Qtn), T_[Qn][:], tn(Qn))
                cx.mm(pB, nB, T_[Qn][:], tn(Qn), T_[Qtn][:], tn(Qtn))
            cx.dve(lambda e: e.tensor_tensor(out=T_["R"][:], in0=T_["R"][:], in1=pC, op=ALU.add), [tn("R"), nC], [tn("R")])
            if k_ < 6:
                cx.act(lambda e, Qo=Qo: e.copy(out=T_[Qo][:], in_=pA), [nA], [tn(Qo)])
                cx.act(lambda e, Qto=Qto: e.copy(out=T_[Qto][:], in_=pB), [nB], [tn(Qto)])
                Qn, Qtn, Qo, Qto = Qo, Qto, Qn, Qtn
        cx.pe(lambda e: e.transpose(pA, vT[:], ident[:]), [nv, "id"], [nA])
        cx.pe(lambda e: e.transpose(pB, kT[:], ident[:]), [nk, "id"], [nB])
        cx.pe(lambda e: e.transpose(pC, bb[:], ident[:]), [nb_, "id"], [nC])
        cx.dve(lambda e: e.tensor_copy(out=T_["bcol"][:], in_=pC), [nC], [tn("bcol")])
        cx.dve(lambda e: e.tensor_tensor(out=T_["vb"][:], in0=pA, in1=T_["bcol"][:], op=ALU.mult), [nA, tn("bcol")], [tn("vb")])
        cx.act(lambda e: e.activation(out=T_["egl"][:, 0:1], in_=gcc, func=AF.Exp), ["gccol", tn("egl")], [tn("egl")])
        cx.dve(lambda e: e.tensor_tensor(out=T_["kbg"][:], in0=pB, in1=T_["bcol"][:], op=ALU.mult), [nB, tn("bcol")], [tn("kbg")])
        cx.dve(lambda e: e.tensor_scalar(out=T_["kbg"][:], in0=T_["kbg"][:], scalar1=T_["egl"][:, 0:1], scalar2=None, op0=ALU.mult), [tn("kbg"), tn("egl")], [tn("kbg")])
        cx.dve(lambda e: e.tensor_tensor(out=T_["egl"][:, 1:2], in0=gc[:, 127:128], in1=gcc, op=ALU.subtract), [tn("gc"), "gccol", tn("egl")], [tn("egl")])
        cx.act(lambda e: e.activation(out=T_["egl"][:, 1:2], in_=T_["egl"][:, 1:2], func=AF.Exp), [tn("egl")], [tn("egl")])
        cx.act(lambda e: e.activation(out=T_["egl"][:, 2:3], in_=gc[:, 127:128], func=AF.Exp), [tn("gc"), tn("egl")], [tn("egl")])
        cx.dve(lambda e: e.tensor_scalar(out=T_["kdec"][:], in0=pB, scalar1=T_["egl"][:, 1:2], scalar2=None, op0=ALU.mult), [nB, tn("egl")], [tn("kdec")])
        cx.mm(pD, nD, T_["kbg"][:], tn("kbg"), T_["R"][:], tn("R"))
        cx.dve(lambda e: e.tensor_scalar(out=T_["wT"][:], in0=pD, scalar1=-1.0, scalar2=None, op0=ALU.mult), [nD], [tn("wT")])

    def phase_Q(b):
        c = b % 2
        T_, P_ = TS[c], PS[c]
        tn = lambda n: "t%d_%s" % (c, n)
        pA, pC, pD = P_["pA"], P_["pC"], P_["pD"]
        nA, nC, nD = "pA%d" % c, "pC%d" % c, "pD%d" % c
        qg = T_["qg"]
        cx.mm(pA, nA, T_["R"][:], tn("R"), T_["vb"][:], tn("vb"), start=True, stop=False)
        cx.mm(pA, nA, T_["wT"][:], tn("wT"), S[:], "t_S", start=False, stop=True, acc=True)
        cx.act(lambda e: e.copy(out=T_["vnew"][:], in_=pA), [nA], [tn("vnew")])
        cx.mm(pC, nC, S[:], "t_S", qg[:], tn("qg"), start=True, stop=False)
        cx.mm(pC, nC, T_["vnew"][:], tn("vnew"), T_["attnT"][:], tn("attnT"), start=False, stop=True, acc=True)
        cx.dve(lambda e: e.tensor_copy(out=T_["osb"][:], in_=pC), [nC], [tn("osb")])
        cx.mm(pD, nD, T_["kdec"][:], tn("kdec"), T_["vnew"][:], tn("vnew"))
        cx.dve(lambda e: e.scalar_tensor_tensor(out=S[:], in0=S[:], scalar=T_["egl"][:, 2:3], in1=pD, op0=ALU.mult, op1=ALU.add), ["t_S", tn("egl"), nD], ["t_S"])
        cx.act(lambda e: e.activation(out=T_["osq"][:], in_=T_["osb"][:], func=AF.Square), [tn("osb")], [tn("osq")])
        cx.mm(pA, nA, ones1[:], "ones1", T_["osq"][:], tn("osq"))
        cx.dve(lambda e: e.tensor_scalar(out=T_["rs"][:], in0=pA, scalar1=1.0 / 128, scalar2=EPS, op0=ALU.mult, op1=ALU.add), [nA], [tn("rs")])
        cx.act(lambda e: e.activation(out=T_["rs"][:], in_=T_["rs"][:], func=AF.Sqrt), [tn("rs")], [tn("rs")])
        cx.dve(lambda e: e.reciprocal(out=T_["rs"][:], in_=T_["rs"][:]), [tn("rs")], [tn("rs")])
        cx.dve(lambda e: e.scalar_tensor_tensor(out=T_["osb"][:], in0=T_["osb"][:], scalar=nw[:, 0:1], in1=T_["rs"][:], op0=ALU.mult, op1=ALU.mult), [tn("osb"), "nw", tn("rs")], [tn("osb")])
        cx.dve(lambda e: e.tensor_tensor(out=yb[c][:], in0=T_["osb"][:], in1=szb[c][:], op=ALU.mult), [tn("osb"), "sz%d" % c, "yb%d" % c], ["yb%d" % c])
        cx.store(dy[:, b * 128:(b + 1) * 128], "y_dn", yb[c][:], "yb%d" % c)

    def rec(fn, b):
        cx.rec_start(); fn(b); return cx.rec_stop()

    assert NB % 2 == 0
    for b0 in (0, 1):
        load_in(b0); load_sz(b0)
    for b in range(0, NB, 2):
        cx.interleave(rec(phase_P, b), rec(phase_P, b + 1))
        if b + 2 < NB:
            load_in(b + 2); load_in(b + 3)
        phase_Q(b)
        phase_Q(b + 1)
        if b + 2 < NB:
            load_sz(b + 2); load_sz(b + 3)
    return cx.done(["y_dn"])


def dn_consts():
    i = np.arange(128)
    f = lambda a: np.ascontiguousarray(a, dtype=np.float32)
    return {"maskU": f(np.where(i[:, None] < i[None, :], 0.0, NEG)), "maskUi": f(np.where(i[:, None] <= i[None, :], 0.0, NEG)),
            "maskL": f(np.where(i[None, :] < i[:, None], 0.0, NEG)), "tri": f(i[:, None] <= i[None, :]), "ident": f(np.eye(128))}


def emit_proj2(cx, w_dram, K, NOUT, rhs_fn, rhs_name, tslices, evac, tag, gw=512, blocked=False):
    KCn = K // 128
    wb = cx.sb("wb_" + tag, [128, 2, KCn, gw], BF16)
    cx.B["wb0_" + tag], cx.B["wb1_" + tag] = Buf("wb0"), Buf("wb1")
    pss = [cx.ps("pp%d_%s" % (i, tag)) for i in range(2)]
    wv = None if blocked else w_dram.rearrange("(kc p) n -> p kc n", p=128)
    groups = [(c, gw) for c in range(0, NOUT, gw)]
    assert NOUT % gw == 0

    def load_w(gi):
        c0, w = groups[gi]
        src = w_dram[gi] if blocked else wv[:, :, c0:c0 + w]
        cx.t.dma_op("pool", wb[:, gi % 2, :, :], src, writes=[cx.B["wb%d_%s" % (gi % 2, tag)]])

    load_w(0)
    k = 0
    for gi, (c0, w) in enumerate(groups):
        if gi + 1 < len(groups):
            load_w(gi + 1)
        for m in range(w // 128):
            for (a, b_) in tslices:
                pi = k % 2
                pn = "pp%d_%s" % (pi, tag)
                for kc in range(KCn):
                    cx.mm(pss[pi][:, :b_ - a], pn, wb[:, gi % 2, kc, m * 128:(m + 1) * 128], "wb%d_%s" % (gi % 2, tag), rhs_fn(kc, slice(a, b_)), rhs_name,
                          start=(kc == 0), stop=(kc == KCn - 1), acc=(kc > 0))
                evac(pss[pi][:, :b_ - a], pn, (c0 + m * 128) // 128, (a, b_), k)
                k += 1


def build_C1(Lh):
    cx = Ctx()
    dx = cx.din("xT", [D, Lh]); dys = cx.din("ys", [1024, Lh], BF16); dyd = cx.din("yd", [1024, Lh], BF16)
    dgs = cx.din("gs", [D, Lh], BF16); dgd = cx.din("gd", [D, Lh], BF16)
    dglu = cx.din("glu_w", [1024, 4096]); ddp = cx.din("dn_proj", [1024, D]); dwo = cx.din("w_out", [D, D])
    ox = cx.dout("xo", [D, Lh])
    x = cx.sb("x", [128, KC, Lh]); ys = cx.sb("ys", [128, 8, Lh], BF16); yd = cx.sb("yd", [128, 8, Lh], BF16)
    gs = cx.sb("gs", [128, KC, Lh], BF16); gd = cx.sb("gd", [128, KC, Lh], BF16)
    sigb = cx.sb("sigb", [128, KC, Lh], BF16); mg = cx.sb("mg", [128, KC, Lh]); mgb = cx.sb("mgb", [128, KC, Lh], BF16); tmp = cx.sb("tmp", [128, Lh])
    cx.load("sp", x[:], "x", dx.rearrange("(kc p) l -> p kc l", p=128))
    cx.load("sp", ys[:], "ys", dys.rearrange("(kc p) l -> p kc l", p=128)); cx.load("sp", yd[:], "yd", dyd.rearrange("(kc p) l -> p kc l", p=128))
    cx.load("sp", gs[:], "gs", dgs.rearrange("(kc p) l -> p kc l", p=128)); cx.load("sp", gd[:], "gd", dgd.rearrange("(kc p) l -> p kc l", p=128))
    cx.act(lambda e: e.activation(out=gs[:], in_=gs[:], func=AF.Sigmoid), ["gs"], ["gs"])
    cx.act(lambda e: e.activation(out=gd[:], in_=gd[:], func=AF.Sigmoid), ["gd"], ["gd"])
    ts = [(0, Lh)]

    def ev_glu(pst, pn, mt, tsl, k):
        if mt < 16:
            cx.act(lambda e: e.activation(out=sigb[:, mt, :], in_=pst, func=AF.Sigmoid), [pn, "sigb"], ["sigb"])
        else:
            j = mt - 16
            cx.dve(lambda e: e.tensor_tensor(out=tmp[:], in0=pst, in1=sigb[:, j, :], op=ALU.mult), [pn, "sigb"], ["tmp"])
            cx.dve(lambda e: e.tensor_tensor(out=mg[:, j, :], in0=tmp[:], in1=gs[:, j, :], op=ALU.mult), ["tmp", "gs", "mg"], ["mg"])

    emit_proj2(cx, dglu, 1024, 4096, lambda kc, s: ys[:, kc, s], "ys", ts, ev_glu, "glu")

    def ev_dn(pst, pn, mt, tsl, k):
        cx.dve(lambda e: e.tensor_tensor(out=tmp[:], in0=pst, in1=gd[:, mt, :], op=ALU.mult), [pn, "gd"], ["tmp"])
        cx.dve(lambda e: e.tensor_tensor(out=mgb[:, mt, :], in0=tmp[:], in1=mg[:, mt, :], op=ALU.add), ["tmp", "mg", "mgb"], ["mgb"])

    emit_proj2(cx, ddp, 1024, D, lambda kc, s: yd[:, kc, s], "yd", ts, ev_dn, "dnp")

    def ev_out(pst, pn, mt, tsl, k):
        cx.dve(lambda e: e.tensor_tensor(out=x[:, mt, :], in0=x[:, mt, :], in1=pst, op=ALU.add), [pn, "x"], ["x"])
        cx.store(ox[mt * 128:(mt + 1) * 128, :], "xo", x[:, mt, :], "x")

    emit_proj2(cx, dwo, D, D, lambda kc, s: mgb[:, kc, s], "mgb", ts, ev_out, "wo", gw=256)
    return cx.done(["xo"])


FF = 5632


def build_C2(Lh, final):
    W = Lh + 2
    cx = Ctx()
    dx = cx.din("xT", [D, W]); dnw = cx.din("nw", [128, KC]); dup = cx.din("ffn_up", [D, 2 * FF]); dcw = cx.din("convw", [128, 88, 3]); ddn = cx.din("ffn_down", [D // 128, 128, FF // 128, 128])
    dfw = cx.din("fnw", [128, KC])
    ox = cx.dout("xo", [D, Lh])
    x = cx.sb("x", [128, KC, W]); h = cx.sb("h", [128, KC, W], BF16); nw = cx.sb("nw", [128, KC]); fw = cx.sb("fw", [128, KC]); cw = cx.sb("cw", [128, 88, 3])
    ones = cx.sb("ones", [128, 128], BF16); rs = cx.sb("rs", [128, W])
    inter = cx.sb("inter", [128, 44, Lh], BF16); actb = cx.sb("actb", [128, 44, Lh], BF16); upp = cx.sb("upp", [128, W]); cv = cx.sb("cv", [128, Lh])
    cx.load("sp", x[:], "x", dx.rearrange("(kc p) l -> p kc l", p=128)); cx.load("sp", nw[:], "nw", dnw); cx.load("sp", fw[:], "fw", dfw); cx.load("sp", cw[:], "cw", dcw)
    cx.dve(lambda e: e.memset(ones[:], 1.0), [], ["ones"])
    pn_ = cx.ps("pnrm")
    tsl = [(0, 2), (2, W)]

    def rmsnorm(src, sname, wt, wname, dst, dname, slices):
        for (a, b_) in slices:
            cx.act(lambda e: e.activation(out=h[:, :, a:b_], in_=src[:, :, a:b_], func=AF.Square), [sname, "h"], ["h"])
            for kc in range(KC):
                cx.mm(pn_[:, :b_ - a], "pnrm", ones[:], "ones", h[:, kc, a:b_], "h", start=(kc == 0), stop=(kc == KC - 1), acc=(kc > 0))
            cx.dve(lambda e: e.tensor_scalar(out=rs[:, a:b_], in0=pn_[:, :b_ - a], scalar1=1.0 / D, scalar2=EPS, op0=ALU.mult, op1=ALU.add), ["pnrm", "rs"], ["rs"])
            cx.act(lambda e: e.activation(out=rs[:, a:b_], in_=rs[:, a:b_], func=AF.Sqrt), ["rs"], ["rs"])
            cx.dve(lambda e: e.reciprocal(out=rs[:, a:b_], in_=rs[:, a:b_]), ["rs"], ["rs"])
            for kc in range(KC):
                cx.dve(lambda e, kc=kc: e.scalar_tensor_tensor(out=dst[:, kc, a:b_], in0=src[:, kc, a:b_], scalar=wt[:, kc:kc + 1], in1=rs[:, a:b_], op0=ALU.mult, op1=ALU.mult), [sname, wname, "rs", dname], [dname])

    rmsnorm(x, "x", nw, "nw", h, "h", tsl)

    def ev_up(pst, pn, mt, sl, k):
        a, b_ = sl
        if a == 0:
            cx.act(lambda e: e.copy(out=upp[:, 0:2], in_=pst), [pn, "upp"], ["upp"])
            return
        cx.act(lambda e: e.copy(out=upp[:, 2:W], in_=pst), [pn, "upp"], ["upp"])
        cx.dve(lambda e: e.tensor_scalar(out=cv[:], in0=upp[:, 2:W], scalar1=cw[:, mt, 2:3], scalar2=None, op0=ALU.mult), ["upp", "cw"], ["cv"])
        cx.dve(lambda e: e.scalar_tensor_tensor(out=cv[:], in0=upp[:, 1:W - 1], scalar=cw[:, mt, 1:2], in1=cv[:], op0=ALU.mult, op1=ALU.add), ["upp", "cw", "cv"], ["cv"])
        cx.dve(lambda e: e.scalar_tensor_tensor(out=cv[:], in0=upp[:, 0:W - 2], scalar=cw[:, mt, 0:1], in1=cv[:], op0=ALU.mult, op1=ALU.add), ["upp", "cw", "cv"], ["cv"])
        if mt < 44:
            cx.act(lambda e: e.activation(out=actb[:, mt, :], in_=cv[:], func=AF.Silu), ["cv", "actb"], ["actb"])
        else:
            cx.dve(lambda e: e.tensor_tensor(out=inter[:, mt - 44, :], in0=cv[:], in1=actb[:, mt - 44, :], op=ALU.mult), ["cv", "actb", "inter"], ["inter"])

    emit_proj2(cx, dup, D, 2 * FF, lambda kc, s: h[:, kc, s], "h", tsl, ev_up, "up")

    def ev_dn(pst, pn, mt, sl, k):
        cx.dve(lambda e: e.tensor_tensor(out=x[:, mt, 2:W], in0=x[:, mt, 2:W], in1=pst, op=ALU.add), [pn, "x"], ["x"])
        if not final:
            cx.store(ox[mt * 128:(mt + 1) * 128, :], "xo", x[:, mt, 2:W], "x")

    emit_proj2(cx, ddn, FF, D, lambda kc, s: inter[:, kc, s], "inter", [(0, Lh)], ev_dn, "dn", gw=128, blocked=True)
    if final:
        rmsnorm(x, "x", fw, "fw", x, "x", [(2, W)])
        for mt in range(KC):
            cx.store(ox[mt * 128:(mt + 1) * 128, :], "xo", x[:, mt, 2:W], "x")
    return cx.done(["xo"])


_PROG = {}
NCORE = 8
SEQ = 8192
LC = SEQ // NCORE
NA = 9216 + 128


def _prog(key, fn):
    if key not in _PROG:
        _PROG[key] = fn()
    return _PROG[key]


def _run(nc, in_maps):
    res = run_bass_kernel_spmd(nc, in_maps, core_ids=list(range(NCORE)))
    return res.results


def _c(a, dt=None):
    return np.ascontiguousarray(a if dt is None else a.astype(dt))


def _pcol(v):
    return _c(v.reshape(-1, 128).T)


def kernel(x, mix_norm_w, w_in, s5_log_dt, s5_a_re, s5_a_im, s5_b_re, s5_b_im, s5_c_re, s5_c_im, s5_d, s5_glu_w,
           dn_conv_w, dn_a_log, dn_dt_bias, dn_norm_w, dn_proj_w, w_out, ffn_norm_w, ffn_up, ffn_conv_w, ffn_down, final_norm_w):
    f32 = np.float32
    XT = _c(np.asarray(x, f32)[0].T)
    depth = w_in.shape[0]
    cst = dn_consts()
    ncA = _prog("A", lambda: build_A(LC, NA))
    s5b = _prog("S5", lambda: build_S5(SEQ))
    ncS5, NS = s5b[0], s5b[2]
    ncF = _prog("DNF", lambda: build_DNF(SEQ)); ncDC = _prog("DNC", lambda: build_DNC(SEQ))
    ncC1 = _prog("C1", lambda: build_C1(512))
    ns_t = _c(np.tile(np.array(NS, f32), (128, 1))); ident = cst["ident"]
    for l in range(depth):
        w = np.asarray(w_in[l], f32)
        wre = np.zeros((D, NA), f32)
        wre[:, 0:5120] = w[:, 0:5120]; wre[:, 5120:9216] = w[:, 5136:9232]; wre[:, 9216:9232] = w[:, 5120:5136]
        nw = _pcol(np.asarray(mix_norm_w[l], f32))
        r = _run(ncA, [{"xT": _c(XT[:, c * LC:(c + 1) * LC]), "nw": nw, "w": wre} for c in range(NCORE)])
        P = np.concatenate([np.asarray(r[c]["pT"]) for c in range(NCORE)], axis=1)
        ins = []
        for c in range(NCORE):
            g0 = 8 * c
            lay = lambda a: _c(np.asarray(a, f32)[g0:g0 + 8].reshape(4, 128).T)
            def zpad(mm_):
                z = np.zeros((128, 4, 128), f32)
                for g in range(8):
                    z[(g % 2) * 64:(g % 2) * 64 + 64, g // 2, g * 16:(g + 1) * 16] = mm_[g]
                return z
            ins.append({"u": _c(P[128 * c:128 * c + 128]), "lr": lay(s5_a_re[l]), "li": lay(s5_a_im[l]),
                        "ldt": lay(np.repeat(np.asarray(s5_log_dt[l], f32)[:, None], 64, 1)),
                        "zb_re": zpad(np.asarray(s5_b_re[l], f32)[g0:g0 + 8]), "zb_im": zpad(np.asarray(s5_b_im[l], f32)[g0:g0 + 8]),
                        "zc_re": zpad(np.asarray(s5_c_re[l], f32)[g0:g0 + 8].transpose(0, 2, 1)), "zc_im": zpad(np.asarray(s5_c_im[l], f32)[g0:g0 + 8].transpose(0, 2, 1)),
                        "dskip": _c(np.asarray(s5_d[l], f32)[128 * c:128 * c + 128, None]), "ns": ns_t, "ident": ident})
        r = _run(ncS5, ins)
        YS = np.concatenate([np.asarray(r[c]["y"]) for c in range(NCORE)], axis=0)
        cwl = np.asarray(dn_conv_w[l], f32)
        ins = []
        for c in range(NCORE):
            b_raw, a_raw = P[9216 + c], P[9216 + 8 + c]
            cw3 = np.stack([cwl[:, 128 * c:128 * c + 128], cwl[:, 1024 + 128 * c:1024 + 128 * c + 128], cwl[:, 2048 + 128 * c:2048 + 128 * c + 128]], 1)
            ins.append({"q": _c(P[1024 + 128 * c:1152 + 128 * c]), "k": _c(P[2048 + 128 * c:2176 + 128 * c]), "v": _c(P[3072 + 128 * c:3200 + 128 * c]),
                        "z": _c(P[4096 + 128 * c:4224 + 128 * c]), "b_raw_bc": _c(np.tile(b_raw, (128, 1))), "a_raw_bc": _c(np.tile(a_raw, (128, 1))),
                        "a_raw_col": _c(a_raw.reshape(SEQ // 128, 128).T), "convw": _c(cw3.transpose(2, 1, 0)),
                        "scal": _c(np.tile(np.array([dn_a_log[l][c], dn_dt_bias[l][c]], f32), (128, 1)))})
        r1 = _run(ncF, ins)
        ins = []
        for c in range(NCORE):
            d = {n: np.asarray(r1[c][n]) for n in ("qn", "kn", "vs", "g_bc", "beta_bc", "g_col", "sz")}
            d["dn_nw"] = _c(np.asarray(dn_norm_w[l], f32)[:, None]); d.update(cst); ins.append(d)
        r = _run(ncDC, ins)
        YD = np.concatenate([np.asarray(r[c]["y_dn"]) for c in range(NCORE)], axis=0)
        glu = np.asarray(s5_glu_w[l], f32)
        glu_ba = _c(np.concatenate([glu[:, 2048:], glu[:, :2048]], 1))
        dpw, wo = _c(np.asarray(dn_proj_w[l], f32)), _c(np.asarray(w_out[l], f32))
        Xmid = np.empty_like(XT)
        for hh in range(2):
            sl = [slice(c * LC + hh * 512, c * LC + hh * 512 + 512) for c in range(NCORE)]
            r = _run(ncC1, [{"xT": _c(XT[:, s]), "ys": _c(YS[:, s]), "yd": _c(YD[:, s]), "gs": _c(P[5120:7168, s]), "gd": _c(P[7168:9216, s]),
                             "glu_w": glu_ba, "dn_proj": dpw, "w_out": wo} for s in sl])
            for c, s in enumerate(sl):
                Xmid[:, s] = np.asarray(r[c]["xo"])
        final = (l == depth - 1)
        ncC2 = _prog("C2f" if final else "C2", lambda: build_C2(512, final))
        fcw = np.asarray(ffn_conv_w[l], f32)
        cw = _c(fcw.reshape(3, 88, 128).transpose(2, 1, 0))
        upw = _c(np.asarray(ffn_up[l], f32))
        dnw_ = _c(np.asarray(ffn_down[l], f32).reshape(FF // 128, 128, D // 128, 128).transpose(2, 1, 0, 3))
        fnw, nw2 = _pcol(np.asarray(final_norm_w, f32)), _pcol(np.asarray(ffn_norm_w[l], f32))
        Xn = np.empty_like(XT)
        for hh in range(2):
            ins, sl = [], []
            for c in range(NCORE):
                t0 = c * LC + hh * 512
                xh = np.zeros((D, 514), f32)
                if t0 >= 2:
                    xh[:, :] = Xmid[:, t0 - 2:t0 + 512]
                else:
                    xh[:, 2:] = Xmid[:, t0:t0 + 512]
                ins.append({"xT": xh, "nw": nw2, "ffn_up": upw, "convw": cw, "ffn_down": dnw_, "fnw": fnw}); sl.append(slice(t0, t0 + 512))
            r = _run(ncC2, ins)
            for c, s in enumerate(sl):
                Xn[:, s] = np.asarray(r[c]["xo"])
        XT = Xn
    return _c(XT.T[None].astype(f32))
```

```python
import math
import numpy as np
import concourse.bass as bass
import concourse.mybir as mybir
from concourse.bass_utils import run_bass_kernel_spmd
from contextlib import ExitStack

F32 = mybir.dt.float32
BF16 = mybir.dt.bfloat16
AF = mybir.ActivationFunctionType
ALU = mybir.AluOpType


class Buf:
    __slots__ = ("name", "w", "r")

    def __init__(self, name):
        self.name = name
        self.w = None
        self.r = {}


class Trk:
    SEM_ROLL = 20000

    def __init__(self, nc, stack, n_dma_sems=12):
        self.nc, self.stack = nc, stack
        self.eng = {"pe": nc.tensor, "act": nc.scalar, "dve": nc.vector,
                    "pool": nc.gpsimd, "sp": nc.sync}
        self.sem, self.cnt, self.seen = {}, {}, {}
        self.nsem = 0
        for e in self.eng:
            self.seen[e] = {}
        for e in ("pe", "act", "dve", "pool"):
            self._newsem(e)
        self.dma = {}
        for q in ("sp", "pool"):
            self.dma[q] = [[self._alloc(f"d_{q}{i}"), 0] for i in range(n_dma_sems)]
        self.dma_i = {"sp": 0, "pool": 0}
        self.ninst = 0

    def _alloc(self, name):
        self.nsem += 1
        return self.stack.enter_context(self.nc.semaphore(f"{name}_{self.nsem}"))

    def _newsem(self, e):
        self.sem[e] = self._alloc(f"c_{e}")
        self.cnt[e] = 0

    def _wait(self, e, deps, skip_same_pe=False):
        eng = self.eng[e]
        best = {}
        for d in deps:
            if d is None:
                continue
            sem, val, src = d
            if skip_same_pe and src == "pe" and e == "pe":
                continue
            k = id(sem)
            if k not in best or best[k][1] < val:
                best[k] = d
        for k, (sem, val, src) in best.items():
            if self.seen[e].get(k, 0) >= val:
                continue
            eng.wait_ge(sem, val)
            self.seen[e][k] = val

    def _deps(self, reads, writes):
        deps = []
        for b in reads:
            deps.append(b.w)
        for b in writes:
            deps.append(b.w)
            deps.extend(b.r.values())
        return deps

    def _record(self, dep, reads, writes):
        k = id(dep[0])
        for b in reads:
            b.r[k] = dep
        for b in writes:
            b.w = dep
            b.r = {}

    def op(self, e, fn, reads=(), writes=(), acc=False):
        self._wait(e, self._deps(reads, writes), skip_same_pe=acc)
        inst = fn(self.eng[e])
        if self.cnt[e] >= self.SEM_ROLL:
            self._newsem(e)
        self.cnt[e] += 1
        inst.then_inc(self.sem[e], 1)
        dep = (self.sem[e], self.cnt[e], e)
        self._record(dep, reads, writes)
        self.ninst += 1
        return dep

    def dma_op(self, q, out, in_, reads=(), writes=(), **kw):
        slots = self.dma[q]
        i = self.dma_i[q]
        self.dma_i[q] = (i + 1) % len(slots)
        slot = slots[i]
        deps = self._deps(reads, writes)
        if slot[1] > 0:
            deps.append((slot[0], slot[1], "dma"))
        self._wait(q, deps)
        slot[1] += 16
        self.eng[q].dma_start(out=out, in_=in_, **kw).then_inc(slot[0], 16)
        dep = (slot[0], slot[1], "dma")
        self._record(dep, reads, writes)
        self.ninst += 1
        return dep

    def finish(self, bufs):
        deps = [b.w for b in bufs]
        for e in ("pe", "act", "dve", "pool"):
            if self.cnt[e]:
                deps.append((self.sem[e], self.cnt[e], e))
        for q in self.dma:
            for s, v in self.dma[q]:
                if v:
                    deps.append((s, v, "dma"))
        self._wait("sp", deps)


D = 2048
KC = D // 128
EPS = 1e-6


def emit_rmsnorm(t, nc, st, xT, bx, nw, bnw, hT, bh, ones_bf, bones, L, tag):
    sq = st.enter_context(nc.sbuf_tensor(f"sq_{tag}", [128, 2, 512], BF16))
    rs = st.enter_context(nc.sbuf_tensor(f"rs_{tag}", [128, 512], F32))
    ps = st.enter_context(nc.psum_tensor(f"psn_{tag}", [128, 512], F32))
    bsq = [Buf("sq0"), Buf("sq1")]
    brs, bps = Buf("rs"), Buf("psn")
    for tb in range(L // 512):
        ts = slice(tb * 512, (tb + 1) * 512)
        for kc in range(KC):
            j = kc % 2
            t.op("act", lambda e, kc=kc, j=j: e.activation(out=sq[:, j, :], in_=xT[:, kc, ts], func=AF.Square),
                 reads=[bx], writes=[bsq[j]])
            t.op("pe", lambda e, kc=kc, j=j: e.matmul(ps[:], lhsT=ones_bf[:], rhs=sq[:, j, :],
                                                      start=(kc == 0), stop=(kc == KC - 1)),
                 reads=[bsq[j], bones], writes=[bps], acc=(kc > 0))
        t.op("dve", lambda e: e.tensor_scalar(out=rs[:], in0=ps[:], scalar1=1.0 / D, scalar2=EPS,
                                              op0=ALU.mult, op1=ALU.add), reads=[bps], writes=[brs])
        t.op("act", lambda e: e.activation(out=rs[:], in_=rs[:], func=AF.Sqrt), reads=[brs], writes=[brs])
        t.op("dve", lambda e: e.reciprocal(out=rs[:], in_=rs[:]), reads=[brs], writes=[brs])
        for kc in range(KC):
            t.op("dve", lambda e, kc=kc: e.scalar_tensor_tensor(out=hT[:, kc, ts], in0=xT[:, kc, ts],
                                                                scalar=nw[:, kc:kc + 1], in1=rs[:],
                                                                op0=ALU.mult, op1=ALU.mult),
                 reads=[bx, bnw, brs], writes=[bh])


I32 = mybir.dt.int32
T = 16
TWO_PI = 2.0 * math.pi


def build_S5(L):
    NCH = L // T
    NJ = int(math.log2(NCH))
    NS = [float(n) for n in range(17)] + [float(16 * 2 ** j) for j in range(1, NJ)]
    NP = len(NS)
    pidx = {int(n): i for i, n in enumerate(NS)}
    nc = bass.Bass("TRN2", target_bir_lowering=False)
    din = lambda n, s, d=F32: nc.dram_tensor(n, s, d, kind="ExternalInput").ap()
    d_u = din("u", [128, L], BF16)
    d_lr, d_li, d_ldt = din("lr", [128, 4]), din("li", [128, 4]), din("ldt", [128, 4])
    d_zb = [din("zb_re", [128, 4, 128]), din("zb_im", [128, 4, 128])]
    d_zc = [din("zc_re", [128, 4, 128]), din("zc_im", [128, 4, 128])]
    d_d = din("dskip", [128, 1])
    d_ns = din("ns", [128, NP])
    d_id = din("ident", [128, 128])
    d_y = nc.dram_tensor("y", [128, L], BF16, kind="ExternalOutput").ap()
    with ExitStack() as st:
        t = Trk(nc, st)
        B = {}

        def sb(n, s, d=F32):
            B[n] = Buf(n)
            return st.enter_context(nc.sbuf_tensor("s_" + n, s, d))

        def ps(n, s):
            B[n] = Buf(n)
            full = st.enter_context(nc.psum_tensor("p_" + n, [128, 512], F32))
            return full[:, :s[1]] if s[1] != 512 else full

        def dve(fn, r, w): t.op("dve", fn, reads=[B[x] for x in r], writes=[B[x] for x in w])
        def act(fn, r, w): t.op("act", fn, reads=[B[x] for x in r], writes=[B[x] for x in w])
        def pool(fn, r, w): t.op("pool", fn, reads=[B[x] for x in r], writes=[B[x] for x in w])
        def mm(o, on, l, ln, r, rn, start=True, stop=True, acc=False):
            t.op("pe", lambda e: e.matmul(o, lhsT=l, rhs=r, start=start, stop=stop), reads=[B[ln], B[rn]], writes=[B[on]], acc=acc)

        u32 = sb("u32", [128, L]); ub = sb("ub", [128, L], BF16)
        lr, li, ldt = sb("lr", [128, 4]), sb("li", [128, 4]), sb("ldt", [128, 4])
        zb = [sb("zb0", [128, 4, 128]), sb("zb1", [128, 4, 128])]
        zc = [sb("zc0", [128, 4, 128]), sb("zc1", [128, 4, 128])]
        dsk = sb("dsk", [128, 1]); ns = sb("ns", [128, NP]); ident = sb("ident", [128, 128])
        for tl, d, n in [(ub, d_u, "ub"), (lr, d_lr, "lr"), (li, d_li, "li"), (ldt, d_ldt, "ldt"), (zb[0], d_zb[0], "zb0"), (zb[1], d_zb[1], "zb1"),
                         (zc[0], d_zc[0], "zc0"), (zc[1], d_zc[1], "zc1"), (dsk, d_d, "dsk"), (ns, d_ns, "ns"), (ident, d_id, "ident")]:
            t.dma_op("sp", tl[:], d, writes=[B[n]])
        dve(lambda e: e.tensor_copy(out=u32[:], in_=ub[:]), ["ub"], ["u32"])

        dt_ = sb("dt", [128, 4]); lrdt = sb("lrdt", [128, 4]); th = sb("th", [128, 4])
        act(lambda e: e.activation(out=dt_[:], in_=ldt[:], func=AF.Exp), ["ldt"], ["dt"])
        dve(lambda e: e.tensor_tensor(out=lrdt[:], in0=lr[:], in1=dt_[:], op=ALU.mult), ["lr", "dt"], ["lrdt"])
        dve(lambda e: e.tensor_tensor(out=th[:], in0=li[:], in1=dt_[:], op=ALU.mult), ["li", "dt"], ["th"])
        pw = [sb("pw_re", [128, 4, NP]), sb("pw_im", [128, 4, NP])]
        mag = sb("mag", [128, 4, NP]); ph = sb("ph", [128, 4, NP]); kq = sb("kq", [128, 4, NP]); ki = sb("ki", [128, 4, NP], I32)
        rr = sb("rr", [128, 4, NP]); sn = sb("sn", [128, 4, NP]); cs = sb("cs", [128, 4, NP])
        for pr in range(4):
            dve(lambda e, pr=pr: e.tensor_scalar(out=mag[:, pr, :], in0=ns[:], scalar1=lrdt[:, pr:pr + 1], scalar2=None, op0=ALU.mult), ["ns", "lrdt", "mag"], ["mag"])
            dve(lambda e, pr=pr: e.tensor_scalar(out=ph[:, pr, :], in0=ns[:], scalar1=th[:, pr:pr + 1], scalar2=None, op0=ALU.mult), ["ns", "th", "ph"], ["ph"])
        act(lambda e: e.activation(out=mag[:], in_=mag[:], func=AF.Exp), ["mag"], ["mag"])
        dve(lambda e: e.tensor_scalar(out=kq[:], in0=ph[:], scalar1=1.0 / TWO_PI, scalar2=None, op0=ALU.mult), ["ph"], ["kq"])
        dve(lambda e: e.tensor_copy(out=ki[:], in_=kq[:]), ["kq"], ["ki"])
        dve(lambda e: e.tensor_copy(out=kq[:], in_=ki[:]), ["ki"], ["kq"])
        dve(lambda e: e.scalar_tensor_tensor(out=rr[:], in0=kq[:], scalar=-TWO_PI, in1=ph[:], op0=ALU.mult, op1=ALU.add), ["kq", "ph"], ["rr"])
        msk = sb("msk", [128, 4, NP])

        def wrap(dst, dn, src, srcn, shift):
            dve(lambda e: e.tensor_scalar(out=dst[:], in0=src[:], scalar1=shift, scalar2=None, op0=ALU.add), [srcn], [dn])
            dve(lambda e: e.tensor_scalar(out=msk[:], in0=dst[:], scalar1=math.pi, scalar2=None, op0=ALU.is_gt), [dn], ["msk"])
            dve(lambda e: e.scalar_tensor_tensor(out=dst[:], in0=msk[:], scalar=-TWO_PI, in1=dst[:], op0=ALU.mult, op1=ALU.add), ["msk", dn], [dn])
            dve(lambda e: e.tensor_scalar(out=msk[:], in0=dst[:], scalar1=-math.pi, scalar2=None, op0=ALU.is_lt), [dn], ["msk"])
            dve(lambda e: e.scalar_tensor_tensor(out=dst[:], in0=msk[:], scalar=TWO_PI, in1=dst[:], op0=ALU.mult, op1=ALU.add), ["msk", dn], [dn])

        wrap(sn, "sn", rr, "rr", 0.0)
        wrap(cs, "cs", rr, "rr", math.pi / 2)
        act(lambda e: e.activation(out=sn[:], in_=sn[:], func=AF.Sin), ["sn"], ["sn"])
        act(lambda e: e.activation(out=cs[:], in_=cs[:], func=AF.Sin), ["cs"], ["cs"])
        dve(lambda e: e.tensor_tensor(out=pw[0][:], in0=mag[:], in1=cs[:], op=ALU.mult), ["mag", "cs"], ["pw_re"])
        dve(lambda e: e.tensor_tensor(out=pw[1][:], in0=mag[:], in1=sn[:], op=ALU.mult), ["mag", "sn"], ["pw_im"])
        i1 = pidx[1]
        nr = sb("nr", [128, 4]); den = sb("den", [128, 4]); tmp = sb("tmp", [128, 4]); cre = sb("cre", [128, 4]); cim = sb("cim", [128, 4]); ncim = sb("ncim", [128, 4])
        dve(lambda e: e.tensor_scalar(out=nr[:], in0=pw[0][:, :, i1], scalar1=-1.0, scalar2=None, op0=ALU.add), ["pw_re"], ["nr"])
        dve(lambda e: e.tensor_tensor(out=den[:], in0=lr[:], in1=lr[:], op=ALU.mult), ["lr"], ["den"])
        dve(lambda e: e.tensor_tensor(out=tmp[:], in0=li[:], in1=li[:], op=ALU.mult), ["li"], ["tmp"])
        dve(lambda e: e.tensor_tensor(out=den[:], in0=den[:], in1=tmp[:], op=ALU.add), ["den", "tmp"], ["den"])
        dve(lambda e: e.reciprocal(out=den[:], in_=den[:]), ["den"], ["den"])
        dve(lambda e: e.tensor_tensor(out=cre[:], in0=nr[:], in1=lr[:], op=ALU.mult), ["nr", "lr"], ["cre"])
        dve(lambda e: e.tensor_tensor(out=tmp[:], in0=pw[1][:, :, i1], in1=li[:], op=ALU.mult), ["pw_im", "li"], ["tmp"])
        dve(lambda e: e.tensor_tensor(out=cre[:], in0=cre[:], in1=tmp[:], op=ALU.add), ["cre", "tmp"], ["cre"])
        dve(lambda e: e.tensor_tensor(out=cre[:], in0=cre[:], in1=den[:], op=ALU.mult), ["cre", "den"], ["cre"])
        dve(lambda e: e.tensor_tensor(out=cim[:], in0=pw[1][:, :, i1], in1=lr[:], op=ALU.mult), ["pw_im", "lr"], ["cim"])
        dve(lambda e: e.tensor_tensor(out=tmp[:], in0=nr[:], in1=li[:], op=ALU.mult), ["nr", "li"], ["tmp"])
        dve(lambda e: e.tensor_tensor(out=cim[:], in0=cim[:], in1=tmp[:], op=ALU.subtract), ["cim", "tmp"], ["cim"])
        dve(lambda e: e.tensor_tensor(out=cim[:], in0=cim[:], in1=den[:], op=ALU.mult), ["cim", "den"], ["cim"])
        dve(lambda e: e.tensor_scalar(out=ncim[:], in0=cim[:], scalar1=-1.0, scalar2=None, op0=ALU.mult), ["cim"], ["ncim"])
        bb = [sb("bb0", [128, 4, 128]), sb("bb1", [128, 4, 128])]
        for pr in range(4):
            s_ = slice(pr, pr + 1)
            dve(lambda e, pr=pr, s_=s_: e.tensor_scalar(out=bb[0][:, pr, :], in0=zb[0][:, pr, :], scalar1=cre[:, s_], scalar2=None, op0=ALU.mult), ["zb0", "cre", "bb0"], ["bb0"])
            dve(lambda e, pr=pr, s_=s_: e.scalar_tensor_tensor(out=bb[0][:, pr, :], in0=zb[1][:, pr, :], scalar=ncim[:, s_], in1=bb[0][:, pr, :], op0=ALU.mult, op1=ALU.add), ["zb1", "ncim", "bb0"], ["bb0"])
            dve(lambda e, pr=pr, s_=s_: e.tensor_scalar(out=bb[1][:, pr, :], in0=zb[1][:, pr, :], scalar1=cre[:, s_], scalar2=None, op0=ALU.mult), ["zb1", "cre", "bb1"], ["bb1"])
            dve(lambda e, pr=pr, s_=s_: e.scalar_tensor_tensor(out=bb[1][:, pr, :], in0=zb[0][:, pr, :], scalar=cim[:, s_], in1=bb[1][:, pr, :], op0=ALU.mult, op1=ALU.add), ["zb0", "cim", "bb1"], ["bb1"])
        npw_im = sb("npw_im", [128, 4, NP])
        dve(lambda e: e.tensor_scalar(out=npw_im[:], in0=pw[1][:], scalar1=-1.0, scalar2=None, op0=ALU.mult), ["pw_im"], ["npw_im"])
        nzc1 = sb("nzc1", [128, 4, 128])
        dve(lambda e: e.tensor_scalar(out=nzc1[:], in0=zc[1][:], scalar1=-1.0, scalar2=None, op0=ALU.mult), ["zc1"], ["nzc1"])

        Kt = sb("Kt", [128, T, 128], BF16)
        Wm = sb("Wm", [128, T, 4, 2, 128], BF16)
        Cm = sb("Cm", [128, T, 4, 2, 128], BF16)
        enb = [[sb("en%d_%d" % (q_, c), [128, 4, 128]) for c in range(2)] for q_ in range(2)]
        pkb = [ps("pk0", [128, 128]), ps("pk1", [128, 128])]; ptr = [ps("ptr0", [128, 128]), ps("ptr1", [128, 128])]
        for n in range(T):
            ix = pidx[n]
            q_ = n % 2
            en = enb[q_]; e0n, e1n = "en%d_0" % q_, "en%d_1" % q_
            pk = pkb[q_]; pkn = "pk%d" % q_
            for pr in range(4):
                a_re, a_im, na_im = pw[0][:, pr, ix:ix + 1], pw[1][:, pr, ix:ix + 1], npw_im[:, pr, ix:ix + 1]
                dve(lambda e, pr=pr, a_re=a_re, en=en: e.tensor_scalar(out=en[0][:, pr, :], in0=bb[0][:, pr, :], scalar1=a_re, scalar2=None, op0=ALU.mult), ["bb0", "pw_re", e0n], [e0n])
                dve(lambda e, pr=pr, na_im=na_im, en=en: e.scalar_tensor_tensor(out=en[0][:, pr, :], in0=bb[1][:, pr, :], scalar=na_im, in1=en[0][:, pr, :], op0=ALU.mult, op1=ALU.add), ["bb1", "npw_im", e0n], [e0n])
                dve(lambda e, pr=pr, a_re=a_re, en=en: e.tensor_scalar(out=en[1][:, pr, :], in0=bb[1][:, pr, :], scalar1=a_re, scalar2=None, op0=ALU.mult), ["bb1", "pw_re", e1n], [e1n])
                dve(lambda e, pr=pr, a_im=a_im, en=en: e.scalar_tensor_tensor(out=en[1][:, pr, :], in0=bb[0][:, pr, :], scalar=a_im, in1=en[1][:, pr, :], op0=ALU.mult, op1=ALU.add), ["bb0", "pw_im", e1n], [e1n])
            for pr in range(4):
                mm(pk, pkn, en[0][:, pr, :], e0n, zc[0][:, pr, :], "zc0", start=(pr == 0), stop=False, acc=(pr > 0))
                mm(pk, pkn, en[1][:, pr, :], e1n, nzc1[:, pr, :], "nzc1", start=False, stop=(pr == 3), acc=True)
            act(lambda e, n=n, pk=pk: e.copy(out=Kt[:, n, :], in_=pk), [pkn, "Kt"], ["Kt"])
            for pr in range(4):
                for c in range(2):
                    pn = "ptr%d" % c
                    ecn = e0n if c == 0 else e1n
                    t.op("pe", lambda e, pr=pr, c=c, en=en: e.transpose(ptr[c], en[c][:, pr, :], ident[:]), reads=[B[ecn], B["ident"]], writes=[B[pn]])
                    if c == 0:
                        act(lambda e, n=n, pr=pr: e.copy(out=Wm[:, n, pr, 0, :], in_=ptr[0]), [pn, "Wm"], ["Wm"])
                    else:
                        dve(lambda e, n=n, pr=pr: e.tensor_copy(out=Wm[:, n, pr, 1, :], in_=ptr[1]), [pn, "Wm"], ["Wm"])
        cnb = [sb("cn%d" % i_, [128, 128]) for i_ in range(4)]
        ci = 0
        for tt in range(T):
            ix = pidx[tt + 1]
            for pr in range(4):
                a_re, a_im, na_im = pw[0][:, pr, ix:ix + 1], pw[1][:, pr, ix:ix + 1], npw_im[:, pr, ix:ix + 1]
                for half in range(2):
                    cn = cnb[ci % 4]; cnn = "cn%d" % (ci % 4); ci += 1
                    if half == 0:
                        dve(lambda e, pr=pr, a_re=a_re, cn=cn: e.tensor_scalar(out=cn[:], in0=zc[0][:, pr, :], scalar1=a_re, scalar2=None, op0=ALU.mult), ["zc0", "pw_re", cnn], [cnn])
                        dve(lambda e, pr=pr, tt=tt, na_im=na_im, cn=cn: e.scalar_tensor_tensor(out=Cm[:, tt, pr, 0, :], in0=zc[1][:, pr, :], scalar=na_im, in1=cn[:], op0=ALU.mult, op1=ALU.add), ["zc1", "npw_im", cnn, "Cm"], ["Cm"])
                    else:
                        dve(lambda e, pr=pr, a_re=a_re, cn=cn: e.tensor_scalar(out=cn[:], in0=nzc1[:, pr, :], scalar1=a_re, scalar2=None, op0=ALU.mult), ["nzc1", "pw_re", cnn], [cnn])
                        dve(lambda e, pr=pr, tt=tt, na_im=na_im, cn=cn: e.scalar_tensor_tensor(out=Cm[:, tt, pr, 1, :], in0=zc[0][:, pr, :], scalar=na_im, in1=cn[:], op0=ALU.mult, op1=ALU.add), ["zc0", "npw_im", cnn, "Cm"], ["Cm"])

        X = [[sb("x%d_%d" % (k, c), [128, 4, NCH]) for c in range(2)] for k in range(2)]
        pw_ = [ps("pw0", [128, 512]), ps("pw1", [128, 512])]
        uv = ub[:].rearrange("p (n s) -> p n s", s=T)
        k = 0
        for pr in range(4):
            for c in range(2):
                for c0 in range(0, NCH, 512):
                    cw = min(512, NCH - c0)
                    pn = "pw%d" % (k % 2)
                    for s in range(T):
                        mm(pw_[k % 2][:, :cw], pn, Wm[:, T - 1 - s, pr, c, :], "Wm", uv[:, c0:c0 + cw, s], "ub", start=(s == 0), stop=(s == T - 1), acc=(s > 0))
                    if k % 2 == 0:
                        act(lambda e, pr=pr, c=c, c0=c0, cw=cw, k=k: e.copy(out=X[0][c][:, pr, c0:c0 + cw], in_=pw_[k % 2][:, :cw]), [pn, "x0_%d" % c], ["x0_%d" % c])
                    else:
                        dve(lambda e, pr=pr, c=c, c0=c0, cw=cw, k=k: e.tensor_copy(out=X[0][c][:, pr, c0:c0 + cw], in_=pw_[k % 2][:, :cw]), [pn, "x0_%d" % c], ["x0_%d" % c])
                    k += 1
        cur = 0
        for j in range(NJ):
            sh = 2 ** j
            ix = pidx[16 * sh]
            o, n_ = X[cur], X[1 - cur]
            on, nn = ["x%d_0" % cur, "x%d_1" % cur], ["x%d_0" % (1 - cur), "x%d_1" % (1 - cur)]
            for pr in range(4):
                a_re, a_im, na_im = pw[0][:, pr, ix:ix + 1], pw[1][:, pr, ix:ix + 1], npw_im[:, pr, ix:ix + 1]
                for c in range(2):
                    eng = dve if c == 0 else dve
                    eng(lambda e, pr=pr, c=c: e.tensor_copy(out=n_[c][:, pr, :sh], in_=o[c][:, pr, :sh]), [on[c], nn[c]], [nn[c]])
                dve(lambda e, pr=pr, a_re=a_re: e.scalar_tensor_tensor(out=n_[0][:, pr, sh:], in0=o[0][:, pr, :NCH - sh], scalar=a_re, in1=o[0][:, pr, sh:], op0=ALU.mult, op1=ALU.add), [on[0], "pw_re", nn[0]], [nn[0]])
                dve(lambda e, pr=pr, na_im=na_im: e.scalar_tensor_tensor(out=n_[0][:, pr, sh:], in0=o[1][:, pr, :NCH - sh], scalar=na_im, in1=n_[0][:, pr, sh:], op0=ALU.mult, op1=ALU.add), [on[1], "npw_im", nn[0]], [nn[0]])
                dve(lambda e, pr=pr, a_re=a_re: e.scalar_tensor_tensor(out=n_[1][:, pr, sh:], in0=o[1][:, pr, :NCH - sh], scalar=a_re, in1=o[1][:, pr, sh:], op0=ALU.mult, op1=ALU.add), [on[1], "pw_re", nn[1]], [nn[1]])
                dve(lambda e, pr=pr, a_im=a_im: e.scalar_tensor_tensor(out=n_[1][:, pr, sh:], in0=o[0][:, pr, :NCH - sh], scalar=a_im, in1=n_[1][:, pr, sh:], op0=ALU.mult, op1=ALU.add), [on[0], "pw_im", nn[1]], [nn[1]])
            cur = 1 - cur
        xb = [sb("xb0", [128, 4, NCH + 1], BF16), sb("xb1", [128, 4, NCH + 1], BF16)]
        for c in range(2):
            dve(lambda e, c=c: e.memset(xb[c][:, :, 0:1], 0.0), ["xb%d" % c], ["xb%d" % c])
            dve(lambda e, c=c: e.tensor_copy(out=xb[c][:, :, 1:], in_=X[cur][c][:]), ["x%d_%d" % (cur, c), "xb%d" % c], ["xb%d" % c])

        py = [ps("py0", [128, 512]), ps("py1", [128, 512])]
        ypre = sb("ypre", [128, 2, 512]); yo = sb("yo", [128, 2, 512], BF16)
        B["out"] = Buf("out")
        CPB = 512 // T
        for blk in range(L // 512):
            pi = blk % 2
            pn = "py%d" % pi
            pv = py[pi][:].rearrange("p (n s) -> p n s", s=T)
            ch0 = blk * CPB
            for tau in range(T):
                mm(pv[:, :, tau:T], pn, Kt[:, tau, :], "Kt", uv[:, ch0:ch0 + CPB, 0:T - tau], "ub", start=(tau == 0), stop=False, acc=(tau > 0))
            for tt in range(T):
                for pr in range(4):
                    for c in range(2):
                        last = (tt == T - 1 and pr == 3 and c == 1)
                        mm(pv[:, :, tt], pn, Cm[:, tt, pr, c, :], "Cm", xb[c][:, pr, ch0:ch0 + CPB], "xb%d" % c, start=False, stop=last, acc=True)
            ts = slice(blk * 512, (blk + 1) * 512)
            dve(lambda e, pi=pi, ts=ts: e.scalar_tensor_tensor(out=ypre[:, pi, :], in0=u32[:, ts], scalar=dsk[:, 0:1], in1=py[pi][:], op0=ALU.mult, op1=ALU.add), ["u32", "dsk", pn, "ypre"], ["ypre"])
            act(lambda e, pi=pi: e.activation(out=yo[:, pi, :], in_=ypre[:, pi, :], func=AF.Gelu), ["ypre", "yo"], ["yo"])
            t.dma_op("sp", d_y[:, ts], yo[:, pi, :], reads=[B["yo"]], writes=[B["out"]])
        t.finish([B["out"]])
    return nc, t, NS


class Ctx:
    def __init__(self):
        self.nc = bass.Bass("TRN2", target_bir_lowering=False)
        self.st = ExitStack()
        self.t = Trk(self.nc, self.st)
        self.B = {}

    def din(self, n, s, d=F32):
        return self.nc.dram_tensor(n, s, d, kind="ExternalInput").ap()

    def dout(self, n, s, d=F32):
        self.B["o_" + n] = Buf("o_" + n)
        return self.nc.dram_tensor(n, s, d, kind="ExternalOutput").ap()

    def sb(self, n, s, d=F32):
        self.B[n] = Buf(n)
        return self.st.enter_context(self.nc.sbuf_tensor("s_" + n, s, d))

    def ps(self, n, w=512):
        self.B[n] = Buf(n)
        full = self.st.enter_context(self.nc.psum_tensor("p_" + n, [128, 512], F32))
        return full[:] if w == 512 else full[:, :w]

    def _b(self, names):
        return [self.B[x] for x in names]

    _rec = None

    def rec_start(self):
        self._rec = []

    def rec_stop(self):
        r, self._rec = self._rec, None
        return r

    @staticmethod
    def interleave(*chains):
        for i in range(max(len(c) for c in chains)):
            for c in chains:
                if i < len(c):
                    c[i]()

    def _do(self, thunk):
        if self._rec is not None:
            self._rec.append(thunk)
        else:
            thunk()

    def dve(self, fn, r, w):
        R, W = self._b(r), self._b(w)
        self._do(lambda: self.t.op("dve", fn, reads=R, writes=W))

    def act(self, fn, r, w):
        R, W = self._b(r), self._b(w)
        self._do(lambda: self.t.op("act", fn, reads=R, writes=W))

    def pool(self, fn, r, w):
        R, W = self._b(r), self._b(w)
        self._do(lambda: self.t.op("pool", fn, reads=R, writes=W))

    def pe(self, fn, r, w):
        R, W = self._b(r), self._b(w)
        self._do(lambda: self.t.op("pe", fn, reads=R, writes=W))

    def mm(self, o, on, l, ln, r, rn, start=True, stop=True, acc=False):
        R, W = self._b([ln, rn]), self._b([on])
        self._do(lambda: self.t.op("pe", lambda e: e.matmul(o, lhsT=l, rhs=r, start=start, stop=stop), reads=R, writes=W, acc=acc))

    def load(self, q, tl, n, src, **kw):
        W = self._b([n])
        self._do(lambda: self.t.dma_op(q, tl, src, writes=W, **kw))

    def store(self, dst, on, tl, n):
        R, W = self._b([n]), self._b(["o_" + on])
        self._do(lambda: self.t.dma_op("sp", dst, tl, reads=R, writes=W))

    def done(self, outs):
        self.t.finish(self._b(["o_" + o for o in outs]))
        self.st.close()
        return self.nc


def col_groups(n):
    g, c = [], 0
    while c < n:
        w = 512 if n - c >= 512 else n - c
        assert w % 128 == 0
        g.append((c, w)); c += w
    return g


def emit_proj(cx, w_dram, K, NOUT, rhs_fn, rhs_names, L, evac, tag, wq="pool"):
    KCn = K // 128
    wb = cx.sb("wb_" + tag, [128, 2, KCn, 512], BF16)
    cx.B["wb0_" + tag], cx.B["wb1_" + tag] = Buf("wb0"), Buf("wb1")
    pss = [cx.ps("pp%d_%s" % (i, tag)) for i in range(3)]
    wv = w_dram.rearrange("(kc p) n -> p kc n", p=128)
    groups = col_groups(NOUT)

    def load_w(gi):
        c0, w = groups[gi]
        cx.t.dma_op(wq, wb[:, gi % 2, :, :w], wv[:, :, c0:c0 + w], writes=[cx.B["wb%d_%s" % (gi % 2, tag)]])

    load_w(0)
    k = 0
    for gi, (c0, w) in enumerate(groups):
        if gi + 1 < len(groups):
            load_w(gi + 1)
        for m in range(w // 128):
            for tb in range(L // 512):
                ts = slice(tb * 512, (tb + 1) * 512)
                pi = k % 3
                pn = "pp%d_%s" % (pi, tag)
                for kc in range(KCn):
                    cx.mm(pss[pi], pn, wb[:, gi % 2, kc, m * 128:(m + 1) * 128], "wb%d_%s" % (gi % 2, tag), rhs_fn(kc, ts), rhs_names[0] if len(rhs_names) == 1 else rhs_names[kc],
                          start=(kc == 0), stop=(kc == KCn - 1), acc=(kc > 0))
                evac(pss[pi], pn, c0 + m * 128, ts, k)
                k += 1


def build_A(L, NOUT):
    cx = Ctx()
    x = cx.din("xT", [D, L]); nwd = cx.din("nw", [128, KC]); w = cx.din("w", [D, NOUT])
    p = cx.dout("pT", [NOUT, L], BF16)
    xT = cx.sb("x", [128, KC, L]); hT = cx.sb("h", [128, KC, L], BF16); nw = cx.sb("nw", [128, KC]); ones = cx.sb("ones", [128, 128], BF16)
    ob = cx.sb("ob", [128, 3, 512], BF16)
    cx.load("sp", xT[:], "x", x.rearrange("(kc p) l -> p kc l", p=128))
    cx.load("sp", nw[:], "nw", nwd)
    cx.dve(lambda e: e.memset(ones[:], 1.0), [], ["ones"])
    emit_rmsnorm(cx.t, cx.nc, cx.st, xT, cx.B["x"], nw, cx.B["nw"], hT, cx.B["h"], ones, cx.B["ones"], L, "a")
    obn = ["ob0", "ob1", "ob2"]
    for n in obn:
        cx.B[n] = Buf(n)

    def evac(pst, pn, row0, ts, k):
        j = k % 3
        if k % 2 == 0:
            cx.act(lambda e: e.copy(out=ob[:, j, :], in_=pst), [pn], [obn[j]])
        else:
            cx.dve(lambda e: e.tensor_copy(out=ob[:, j, :], in_=pst), [pn], [obn[j]])
        cx.store(p[row0:row0 + 128, ts], "pT", ob[:, j, :], obn[j])

    emit_proj(cx, w, D, NOUT, lambda kc, ts: hT[:, kc, ts], ["h"], L, evac, "a")
    return cx.done(["pT"])


def build_DNF(L):
    NB = L // 128
    cx = Ctx()
    dq, dk, dv, dz = (cx.din(n, [128, L], BF16) for n in ("q", "k", "v", "z"))
    dbr, dar = cx.din("b_raw_bc", [128, L], BF16), cx.din("a_raw_bc", [128, L], BF16)
    darc = cx.din("a_raw_col", [128, NB], BF16)
    dcw = cx.din("convw", [128, 3, 4])
    dsc = cx.din("scal", [128, 2])
    oq, ok, ov = (cx.dout(n, [128, L]) for n in ("qn", "kn", "vs"))
    og, obt = cx.dout("g_bc", [128, L]), cx.dout("beta_bc", [128, L])
    ogc = cx.dout("g_col", [128, NB]); osz = cx.dout("sz", [128, L], BF16)
    raw = cx.sb("raw", [128, L], BF16); acc = cx.sb("acc", [128, L]); res = cx.sb("res", [128, L])
    cw = cx.sb("cw", [128, 3, 4]); sc = cx.sb("sc", [128, 2]); nea = cx.sb("nea", [128, 1])
    ones = cx.sb("ones", [128, 128]); sqa = cx.sb("sqa", [128, L]); rsa = cx.sb("rsa", [128, L]); epsb = cx.sb("epsb", [128, 1])
    pn2 = [cx.ps("pn0"), cx.ps("pn1")]
    cx.dve(lambda e: e.memset(epsb[:], EPS), [], ["epsb"])
    cx.load("sp", cw[:], "cw", dcw); cx.load("sp", sc[:], "sc", dsc)
    cx.dve(lambda e: e.memset(ones[:], 1.0), [], ["ones"])
    cx.act(lambda e: e.activation(out=nea[:], in_=sc[:, 0:1], func=AF.Exp), ["sc"], ["nea"])
    cx.dve(lambda e: e.tensor_scalar(out=nea[:], in0=nea[:], scalar1=-1.0, scalar2=None, op0=ALU.mult), ["nea"], ["nea"])
    for i, (src, dst, on) in enumerate([(dq, oq, "qn"), (dk, ok, "kn"), (dv, ov, "vs")]):
        cx.load("sp", raw[:], "raw", src)
        cx.dve(lambda e, i=i: e.tensor_scalar(out=acc[:], in0=raw[:], scalar1=cw[:, i, 3:4], scalar2=None, op0=ALU.mult), ["raw", "cw"], ["acc"])
        for s in (1, 2, 3):
            cx.dve(lambda e, i=i, s=s: e.scalar_tensor_tensor(out=acc[:, s:], in0=raw[:, :L - s], scalar=cw[:, i, 3 - s:4 - s], in1=acc[:, s:], op0=ALU.mult, op1=ALU.add), ["raw", "cw", "acc"], ["acc"])
        cx.act(lambda e: e.activation(out=res[:], in_=acc[:], func=AF.Silu), ["acc"], ["res"])
        if i < 2:
            cx.act(lambda e: e.activation(out=sqa[:], in_=res[:], func=AF.Square), ["res"], ["sqa"])
            for tb in range(L // 512):
                ts = slice(tb * 512, (tb + 1) * 512)
                pj = pn2[tb % 2]
                cx.mm(pj, "pn%d" % (tb % 2), ones[:], "ones", sqa[:, ts], "sqa")
                if tb % 2 == 0:
                    cx.dve(lambda e, ts=ts, pj=pj: e.tensor_scalar(out=rsa[:, ts], in0=pj, scalar1=EPS, scalar2=None, op0=ALU.add), ["pn0", "rsa"], ["rsa"])
                else:
                    cx.act(lambda e, ts=ts, pj=pj: e.activation(out=rsa[:, ts], in_=pj, func=AF.Identity, bias=epsb[:, 0:1]), ["pn1", "rsa", "epsb"], ["rsa"])
            cx.act(lambda e: e.activation(out=rsa[:], in_=rsa[:], func=AF.Sqrt), ["rsa"], ["rsa"])
            cx.dve(lambda e: e.reciprocal(out=rsa[:], in_=rsa[:]), ["rsa"], ["rsa"])
            if i == 0:
                cx.dve(lambda e: e.scalar_tensor_tensor(out=res[:], in0=res[:], scalar=128.0 ** -0.5, in1=rsa[:], op0=ALU.mult, op1=ALU.mult), ["res", "rsa"], ["res"])
            else:
                cx.dve(lambda e: e.tensor_tensor(out=res[:], in0=res[:], in1=rsa[:], op=ALU.mult), ["res", "rsa"], ["res"])
        cx.store(dst, on, res[:], "res")
    szt = cx.sb("szt", [128, L], BF16)
    cx.load("sp", raw[:], "raw", dz)
    cx.act(lambda e: e.activation(out=szt[:], in_=raw[:], func=AF.Silu), ["raw"], ["szt"])
    cx.store(osz, "sz", szt[:], "szt")
    cx.load("sp", raw[:], "raw", dbr)
    cx.act(lambda e: e.activation(out=res[:], in_=raw[:], func=AF.Sigmoid), ["raw"], ["res"])
    cx.store(obt, "beta_bc", res[:], "res")
    cx.load("sp", raw[:], "raw", dar)
    cx.act(lambda e: e.activation(out=acc[:], in_=raw[:], func=AF.Exp, bias=sc[:, 1:2]), ["raw", "sc"], ["acc"])
    cx.act(lambda e: e.activation(out=acc[:], in_=acc[:], func=AF.Ln, bias=1.0), ["acc"], ["acc"])
    cx.dve(lambda e: e.tensor_scalar(out=res[:], in0=acc[:], scalar1=nea[:, 0:1], scalar2=None, op0=ALU.mult), ["acc", "nea"], ["res"])
    cx.store(og, "g_bc", res[:], "res")
    rc = cx.sb("rc", [128, NB], BF16); gcl = cx.sb("gcl", [128, NB])
    cx.load("sp", rc[:], "rc", darc)
    cx.act(lambda e: e.activation(out=gcl[:], in_=rc[:], func=AF.Exp, bias=sc[:, 1:2]), ["rc", "sc"], ["gcl"])
    cx.act(lambda e: e.activation(out=gcl[:], in_=gcl[:], func=AF.Ln, bias=1.0), ["gcl"], ["gcl"])
    cx.dve(lambda e: e.tensor_scalar(out=gcl[:], in0=gcl[:], scalar1=nea[:, 0:1], scalar2=None, op0=ALU.mult), ["gcl", "nea"], ["gcl"])
    cx.store(ogc, "g_col", gcl[:], "gcl")
    return cx.done(["qn", "kn", "vs", "g_bc", "beta_bc", "g_col", "sz"])


NEG = -30000.0


def build_DNC(L, cx=None, io=None):
    NB = L // 128
    cx = cx or Ctx()
    if io is not None:
        cx.begin(io)
    dq, dk, dv, dg, db = (cx.din(n, [128, L]) for n in ("qn", "kn", "vs", "g_bc", "beta_bc"))
    dgc = cx.din("g_col", [128, NB]); dsz = cx.din("sz", [128, L], BF16); dnw = cx.din("dn_nw", [128, 1])
    dmu, dmui, dml, dtri, did = (cx.din(n, [128, 128]) for n in ("maskU", "maskUi", "maskL", "tri", "ident"))
    dy = cx.dout("y_dn", [128, L], BF16)
    gcol = cx.sb("gcol", [128, NB]); gccol = cx.sb("gccol", [128, NB]); ngccol = cx.sb("ngccol", [128, NB]); nw = cx.sb("nw", [128, 1])
    mU, mUi, mL, tri, ident = (cx.sb(n, [128, 128]) for n in ("mU", "mUi", "mL", "tri", "id"))
    ones1 = cx.sb("ones1", [128, 128])
    for tl, d, n in [(gcol, dgc, "gcol"), (nw, dnw, "nw"), (mU, dmu, "mU"), (mUi, dmui, "mUi"), (mL, dml, "mL"), (tri, dtri, "tri"), (ident, did, "id")]:
        cx.load("sp", tl[:], n, d)
    cx.dve(lambda e: e.memset(ones1[:], 1.0), [], ["ones1"])
    PS = [{n: cx.ps("%s%d" % (n, c), 128) for n in ("pA", "pB", "pC", "pD")} for c in range(2)]
    pcol = PS[1]["pD"][:, :NB]
    cx.mm(pcol, "pD1", tri[:], "tri", gcol[:], "gcol")
    cx.dve(lambda e: e.tensor_copy(out=gccol[:], in_=pcol), ["pD1"], ["gccol"])
    cx.dve(lambda e: e.tensor_scalar(out=ngccol[:], in0=pcol, scalar1=-1.0, scalar2=None, op0=ALU.mult), ["pD1"], ["ngccol"])
    inb = [[cx.sb("in%d_%d" % (p, i), [128, 128]) for i in range(5)] for p in range(2)]
    szb = [cx.sb("sz%d" % p, [128, 128], BF16) for p in range(2)]
    names = ["arg", "argL", "argI", "DT", "DTi", "D", "M", "Lm", "M2", "L2", "R", "Rt", "attnT", "vb", "kbg", "wT", "vnew", "kdec", "egl", "bcol", "kb", "gc", "egc", "qg", "osb", "osq", "rs"]
    TS = [{n: cx.sb("t%d_%s" % (c, n), [128, 128]) for n in names} for c in range(2)]
    yb = [cx.sb("yb%d" % p, [128, 128], BF16) for p in range(2)]
    S = cx.sb("t_S", [128, 128])
    cx.dve(lambda e: e.memset(S[:], 0.0), [], ["t_S"])
    srcs = [dq, dk, dv, dg, db]

    def load_in(b):
        p = b % 2
        bs = slice(b * 128, (b + 1) * 128)
        for i in range(5):
            cx.load("sp", inb[p][i][:], "in%d_%d" % (p, i), srcs[i][:, bs])

    def load_sz(b):
        p = b % 2
        cx.load("sp", szb[p][:], "sz%d" % p, dsz[:, b * 128:(b + 1) * 128])

    def phase_P(b):
        c = b % 2
        T_, P_ = TS[c], PS[c]
        tn = lambda n: "t%d_%s" % (c, n)
        pA, pB, pC, pD = P_["pA"], P_["pB"], P_["pC"], P_["pD"]
        nA, nB, nC, nD = "pA%d" % c, "pB%d" % c, "pC%d" % c, "pD%d" % c
        qT, kT, vT, gb, bb = (inb[c][i] for i in range(5))
        nq, nk, nv, ng, nb_ = ("in%d_%d" % (c, i) for i in range(5))
        gcc, ngcc = gccol[:, b:b + 1], ngccol[:, b:b + 1]
        kb, gc, egc, qg = T_["kb"], T_["gc"], T_["egc"], T_["qg"]
        cx.dve(lambda e: e.tensor_tensor(out=kb[:], in0=kT[:], in1=bb[:], op=ALU.mult), [nk, nb_], [tn("kb")])
        cx.dve(lambda e: e.tensor_tensor_scan(out=gc[:], data0=ones1[:], data1=gb[:], initial=0.0, op0=ALU.mult, op1=ALU.add), [ng, "ones1"], [tn("gc")])
        cx.act(lambda e: e.activation(out=egc[:], in_=gc[:], func=AF.Exp), [tn("gc")], [tn("egc")])
        cx.dve(lambda e: e.tensor_tensor(out=qg[:], in0=qT[:], in1=egc[:], op=ALU.mult), [nq, tn("egc")], [tn("qg")])
        cx.mm(pA, nA, kT[:], nk, kb[:], tn("kb"))
        cx.dve(lambda e: e.tensor_tensor(out=T_["arg"][:], in0=gc[:], in1=mU[:], op=ALU.add), [tn("gc"), "mU"], [tn("arg")])
        cx.act(lambda e: e.activation(out=T_["DT"][:], in_=T_["arg"][:], func=AF.Exp, bias=ngcc), [tn("arg"), "ngccol"], [tn("DT")])
        cx.dve(lambda e: e.tensor_tensor(out=T_["M"][:], in0=pA, in1=T_["DT"][:], op=ALU.mult), [nA, tn("DT")], [tn("M")])
        cx.mm(pB, nB, kb[:], tn("kb"), kT[:], nk)
        cx.dve(lambda e: e.tensor_tensor(out=T_["argL"][:], in0=mL[:], in1=gc[:], op=ALU.subtract), [tn("gc"), "mL"], [tn("argL")])
        cx.act(lambda e: e.activation(out=T_["D"][:], in_=T_["argL"][:], func=AF.Exp, bias=gcc), [tn("argL"), "gccol"], [tn("D")])
        cx.dve(lambda e: e.tensor_tensor(out=T_["Lm"][:], in0=pB, in1=T_["D"][:], op=ALU.mult), [nB, tn("D")], [tn("Lm")])
        cx.mm(pC, nC, kT[:], nk, qT[:], nq)
        cx.dve(lambda e: e.tensor_tensor(out=T_["argI"][:], in0=gc[:], in1=mUi[:], op=ALU.add), [tn("gc"), "mUi"], [tn("argI")])
        cx.act(lambda e: e.activation(out=T_["DTi"][:], in_=T_["argI"][:], func=AF.Exp, bias=ngcc), [tn("argI"), "ngccol"], [tn("DTi")])
        cx.dve(lambda e: e.tensor_tensor(out=T_["attnT"][:], in0=pC, in1=T_["DTi"][:], op=ALU.mult), [nC, tn("DTi")], [tn("attnT")])
        cx.dve(lambda e: e.tensor_tensor(out=T_["R"][:], in0=ident[:], in1=T_["M"][:], op=ALU.subtract), ["id", tn("M")], [tn("R")])
        cx.mm(pA, nA, T_["Lm"][:], tn("Lm"), T_["M"][:], tn("M"))
        cx.mm(pB, nB, T_["M"][:], tn("M"), T_["Lm"][:], tn("Lm"))
        cx.act(lambda e: e.copy(out=T_["M2"][:], in_=pA), [nA], [tn("M2")])
        cx.dve(lambda e: e.tensor_copy(out=T_["L2"][:], in_=pB), [nB], [tn("L2")])
        Qn, Qtn, Qo, Qto = "M2", "L2", "M", "Lm"
        for k_ in range(1, 7):
            cx.mm(pC, nC, T_[Qtn][:], tn(Qtn), T_["R"][:], tn("R"))
            if k_ < 6:
                cx.mm(pA, nA, T_[Qtn][:], tn(Qtn), T_[Qn][:], tn(Qn))
                cx.mm(pB, nB, T_[Qn][:], tn(Qn), T_[Qtn][:], tn(Qtn))
            cx.dve(lambda e: e.tensor_tensor(out=T_["R"][:], in0=T_["R"][:], in1=pC, op=ALU.add), [tn("R"), nC], [tn("R")])
            if k_ < 6:
                cx.act(lambda e, Qo=Qo: e.copy(out=T_[Qo][:], in_=pA), [nA], [tn(Qo)])
                cx.act(lambda e, Qto=Qto: e.copy(out=T_[Qto][:], in_=pB), [nB], [tn(Qto)])
                Qn, Qtn, Qo, Qto = Qo, Qto, Qn, Qtn
        cx.pe(lambda e: e.transpose(pA, vT[:], ident[:]), [nv, "id"], [nA])
        cx.pe(lambda e: e.transpose(pB, kT[:], ident[:]), [nk, "id"], [nB])
        cx.pe(lambda e: e.transpose(pC, bb[:], ident[:]), [nb_, "id"], [nC])
        cx.dve(lambda e: e.tensor_copy(out=T_["bcol"][:], in_=pC), [nC], [tn("bcol")])
        cx.dve(lambda e: e.tensor_tensor(out=T_["vb"][:], in0=pA, in1=T_["bcol"][:], op=ALU.mult), [nA, tn("bcol")], [tn("vb")])
        cx.act(lambda e: e.activation(out=T_["egl"][:, 0:1], in_=gcc, func=AF.Exp), ["gccol", tn("egl")], [tn("egl")])
        cx.dve(lambda e: e.tensor_tensor(out=T_["kbg"][:], in0=pB, in1=T_["bcol"][:], op=ALU.mult), [nB, tn("bcol")], [tn("kbg")])
        cx.dve(lambda e: e.tensor_scalar(out=T_["kbg"][:], in0=T_["kbg"][:], scalar1=T_["egl"][:, 0:1], scalar2=None, op0=ALU.mult), [tn("kbg"), tn("egl")], [tn("kbg")])
        cx.dve(lambda e: e.tensor_tensor(out=T_["egl"][:, 1:2], in0=gc[:, 127:128], in1=gcc, op=ALU.subtract), [tn("gc"), "gccol", tn("egl")], [tn("egl")])
        cx.act(lambda e: e.activation(out=T_["egl"][:, 1:2], in_=T_["egl"][:, 1:2], func=AF.Exp), [tn("egl")], [tn("egl")])
        cx.act(lambda e: e.activation(out=T_["egl"][:, 2:3], in_=gc[:, 127:128], func=AF.Exp), [tn("gc"), tn("egl")], [tn("egl")])
        cx.dve(lambda e: e.tensor_scalar(out=T_["kdec"][:], in0=pB, scalar1=T_["egl"][:, 1:2], scalar2=None, op0=ALU.mult), [nB, tn("egl")], [tn("kdec")])
        cx.mm(pD, nD, T_["kbg"][:], tn("kbg"), T_["R"][:], tn("R"))
        cx.dve(lambda e: e.tensor_scalar(out=T_["wT"][:], in0=pD, scalar1=-1.0, scalar2=None, op0=ALU.mult), [nD], [tn("wT")])

    def phase_Q(b):
        c = b % 2
        T_, P_ = TS[c], PS[c]
        tn = lambda n: "t%d_%s" % (c, n)
        pA, pC, pD = P_["pA"], P_["pC"], P_["pD"]
        nA, nC, nD = "pA%d" % c, "pC%d" % c, "pD%d" % c
        qg = T_["qg"]
        cx.mm(pA, nA, T_["R"][:], tn("R"), T_["vb"][:], tn("vb"), start=True, stop=False)
        cx.mm(pA, nA, T_["wT"][:], tn("wT"), S[:], "t_S", start=False, stop=True, acc=True)
        cx.act(lambda e: e.copy(out=T_["vnew"][:], in_=pA), [nA], [tn("vnew")])
        cx.mm(pC, nC, S[:], "t_S", qg[:], tn("qg"), start=True, stop=False)
        cx.mm(pC, nC, T_["vnew"][:], tn("vnew"), T_["attnT"][:], tn("attnT"), start=False, stop=True, acc=True)
        cx.dve(lambda e: e.tensor_copy(out=T_["osb"][:], in_=pC), [nC], [tn("osb")])
        cx.mm(pD, nD, T_["kdec"][:], tn("kdec"), T_["vnew"][:], tn("vnew"))
        cx.dve(lambda e: e.scalar_tensor_tensor(out=S[:], in0=S[:], scalar=T_["egl"][:, 2:3], in1=pD, op0=ALU.mult, op1=ALU.add), ["t_S", tn("egl"), nD], ["t_S"])
        cx.act(lambda e: e.activation(out=T_["osq"][:], in_=T_["osb"][:], func=AF.Square), [tn("osb")], [tn("osq")])
        cx.mm(pA, nA, ones1[:], "ones1", T_["osq"][:], tn("osq"))
        cx.dve(lambda e: e.tensor_scalar(out=T_["rs"][:], in0=pA, scalar1=1.0 / 128, scalar2=EPS, op0=ALU.mult, op1=ALU.add), [nA], [tn("rs")])
        cx.act(lambda e: e.activation(out=T_["rs"][:], in_=T_["rs"][:], func=AF.Sqrt), [tn("rs")], [tn("rs")])
        cx.dve(lambda e: e.reciprocal(out=T_["rs"][:], in_=T_["rs"][:]), [tn("rs")], [tn("rs")])
        cx.dve(lambda e: e.scalar_tensor_tensor(out=T_["osb"][:], in0=T_["osb"][:], scalar=nw[:, 0:1], in1=T_["rs"][:], op0=ALU.mult, op1=ALU.mult), [tn("osb"), "nw", tn("rs")], [tn("osb")])
        cx.dve(lambda e: e.tensor_tensor(out=yb[c][:], in0=T_["osb"][:], in1=szb[c][:], op=ALU.mult), [tn("osb"), "sz%d" % c, "yb%d" % c], ["yb%d" % c])
        cx.store(dy[:, b * 128:(b + 1) * 128], "y_dn", yb[c][:], "yb%d" % c)

    def rec(fn, b):
        cx.rec_start(); fn(b); return cx.rec_stop()

    assert NB % 2 == 0
    for b0 in (0, 1):
        load_in(b0); load_sz(b0)
    for b in range(0, NB, 2):
        cx.interleave(rec(phase_P, b), rec(phase_P, b + 1))
        if b + 2 < NB:
            load_in(b + 2); load_in(b + 3)
        phase_Q(b)
        phase_Q(b + 1)
        if b + 2 < NB:
            load_sz(b + 2); load_sz(b + 3)
    return cx.done(["y_dn"])


def dn_consts():
    i = np.arange(128)
    f = lambda a: np.ascontiguousarray(a, dtype=np.float32)
    return {"maskU": f(np.where(i[:, None] < i[None, :], 0.0, NEG)), "maskUi": f(np.where(i[:, None] <= i[None, :], 0.0, NEG)),
            "maskL": f(np.where(i[None, :] < i[:, None], 0.0, NEG)), "tri": f(i[:, None] <= i[None, :]), "ident": f(np.eye(128))}


def emit_proj2(cx, w_dram, K, NOUT, rhs_fn, rhs_name, tslices, evac, tag, gw=512, blocked=False):
    KCn = K // 128
    wb = cx.sb("wb_" + tag, [128, 2, KCn, gw], BF16)
    cx.B["wb0_" + tag], cx.B["wb1_" + tag] = Buf("wb0"), Buf("wb1")
    pss = [cx.ps("pp%d_%s" % (i, tag)) for i in range(2)]
    wv = None if blocked else w_dram.rearrange("(kc p) n -> p kc n", p=128)
    groups = [(c, gw) for c in range(0, NOUT, gw)]
    assert NOUT % gw == 0

    def load_w(gi):
        c0, w = groups[gi]
        src = w_dram[gi] if blocked else wv[:, :, c0:c0 + w]
        cx.t.dma_op("pool", wb[:, gi % 2, :, :], src, writes=[cx.B["wb%d_%s" % (gi % 2, tag)]])

    load_w(0)
    k = 0
    for gi, (c0, w) in enumerate(groups):
        if gi + 1 < len(groups):
            load_w(gi + 1)
        for m in range(w // 128):
            for (a, b_) in tslices:
                pi = k % 2
                pn = "pp%d_%s" % (pi, tag)
                for kc in range(KCn):
                    cx.mm(pss[pi][:, :b_ - a], pn, wb[:, gi % 2, kc, m * 128:(m + 1) * 128], "wb%d_%s" % (gi % 2, tag), rhs_fn(kc, slice(a, b_)), rhs_name,
                          start=(kc == 0), stop=(kc == KCn - 1), acc=(kc > 0))
                evac(pss[pi][:, :b_ - a], pn, (c0 + m * 128) // 128, (a, b_), k)
                k += 1


def build_C1(Lh):
    cx = Ctx()
    dx = cx.din("xT", [D, Lh]); dys = cx.din("ys", [1024, Lh], BF16); dyd = cx.din("yd", [1024, Lh], BF16)
    dgs = cx.din("gs", [D, Lh], BF16); dgd = cx.din("gd", [D, Lh], BF16)
    dglu = cx.din("glu_w", [1024, 4096]); ddp = cx.din("dn_proj", [1024, D]); dwo = cx.din("w_out", [D, D])
    ox = cx.dout("xo", [D, Lh])
    x = cx.sb("x", [128, KC, Lh]); ys = cx.sb("ys", [128, 8, Lh], BF16); yd = cx.sb("yd", [128, 8, Lh], BF16)
    gs = cx.sb("gs", [128, KC, Lh], BF16); gd = cx.sb("gd", [128, KC, Lh], BF16)
    sigb = cx.sb("sigb", [128, KC, Lh], BF16); mg = cx.sb("mg", [128, KC, Lh]); mgb = cx.sb("mgb", [128, KC, Lh], BF16); tmp = cx.sb("tmp", [128, Lh])
    cx.load("sp", x[:], "x", dx.rearrange("(kc p) l -> p kc l", p=128))
    cx.load("sp", ys[:], "ys", dys.rearrange("(kc p) l -> p kc l", p=128)); cx.load("sp", yd[:], "yd", dyd.rearrange("(kc p) l -> p kc l", p=128))
    cx.load("sp", gs[:], "gs", dgs.rearrange("(kc p) l -> p kc l", p=128)); cx.load("sp", gd[:], "gd", dgd.rearrange("(kc p) l -> p kc l", p=128))
    cx.act(lambda e: e.activation(out=gs[:], in_=gs[:], func=AF.Sigmoid), ["gs"], ["gs"])
    cx.act(lambda e: e.activation(out=gd[:], in_=gd[:], func=AF.Sigmoid), ["gd"], ["gd"])
    ts = [(0, Lh)]

    def ev_glu(pst, pn, mt, tsl, k):
        if mt < 16:
            cx.act(lambda e: e.activation(out=sigb[:, mt, :], in_=pst, func=AF.Sigmoid), [pn, "sigb"], ["sigb"])
        else:
            j = mt - 16
            cx.dve(lambda e: e.tensor_tensor(out=tmp[:], in0=pst, in1=sigb[:, j, :], op=ALU.mult), [pn, "sigb"], ["tmp"])
            cx.dve(lambda e: e.tensor_tensor(out=mg[:, j, :], in0=tmp[:], in1=gs[:, j, :], op=ALU.mult), ["tmp", "gs", "mg"], ["mg"])

    emit_proj2(cx, dglu, 1024, 4096, lambda kc, s: ys[:, kc, s], "ys", ts, ev_glu, "glu")

    def ev_dn(pst, pn, mt, tsl, k):
        cx.dve(lambda e: e.tensor_tensor(out=tmp[:], in0=pst, in1=gd[:, mt, :], op=ALU.mult), [pn, "gd"], ["tmp"])
        cx.dve(lambda e: e.tensor_tensor(out=mgb[:, mt, :], in0=tmp[:], in1=mg[:, mt, :], op=ALU.add), ["tmp", "mg", "mgb"], ["mgb"])

    emit_proj2(cx, ddp, 1024, D, lambda kc, s: yd[:, kc, s], "yd", ts, ev_dn, "dnp")

    def ev_out(pst, pn, mt, tsl, k):
        cx.dve(lambda e: e.tensor_tensor(out=x[:, mt, :], in0=x[:, mt, :], in1=pst, op=ALU.add), [pn, "x"], ["x"])
        cx.store(ox[mt * 128:(mt + 1) * 128, :], "xo", x[:, mt, :], "x")

    emit_proj2(cx, dwo, D, D, lambda kc, s: mgb[:, kc, s], "mgb", ts, ev_out, "wo", gw=256)
    return cx.done(["xo"])


FF = 5632


def build_C2(Lh, final):
    W = Lh + 2
    cx = Ctx()
    dx = cx.din("xT", [D, W]); dnw = cx.din("nw", [128, KC]); dup = cx.din("ffn_up", [D, 2 * FF]); dcw = cx.din("convw", [128, 88, 3]); ddn = cx.din("ffn_down", [D // 128, 128, FF // 128, 128])
    dfw = cx.din("fnw", [128, KC])
    ox = cx.dout("xo", [D, Lh])
    x = cx.sb("x", [128, KC, W]); h = cx.sb("h", [128, KC, W], BF16); nw = cx.sb("nw", [128, KC]); fw = cx.sb("fw", [128, KC]); cw = cx.sb("cw", [128, 88, 3])
    ones = cx.sb("ones", [128, 128], BF16); rs = cx.sb("rs", [128, W])
    inter = cx.sb("inter", [128, 44, Lh], BF16); actb = cx.sb("actb", [128, 44, Lh], BF16); upp = cx.sb("upp", [128, W]); cv = cx.sb("cv", [128, Lh])
    cx.load("sp", x[:], "x", dx.rearrange("(kc p) l -> p kc l", p=128)); cx.load("sp", nw[:], "nw", dnw); cx.load("sp", fw[:], "fw", dfw); cx.load("sp", cw[:], "cw", dcw)
    cx.dve(lambda e: e.memset(ones[:], 1.0), [], ["ones"])
    pn_ = cx.ps("pnrm")
    tsl = [(0, 2), (2, W)]

    def rmsnorm(src, sname, wt, wname, dst, dname, slices):
        for (a, b_) in slices:
            cx.act(lambda e: e.activation(out=h[:, :, a:b_], in_=src[:, :, a:b_], func=AF.Square), [sname, "h"], ["h"])
            for kc in range(KC):
                cx.mm(pn_[:, :b_ - a], "pnrm", ones[:], "ones", h[:, kc, a:b_], "h", start=(kc == 0), stop=(kc == KC - 1), acc=(kc > 0))
            cx.dve(lambda e: e.tensor_scalar(out=rs[:, a:b_], in0=pn_[:, :b_ - a], scalar1=1.0 / D, scalar2=EPS, op0=ALU.mult, op1=ALU.add), ["pnrm", "rs"], ["rs"])
            cx.act(lambda e: e.activation(out=rs[:, a:b_], in_=rs[:, a:b_], func=AF.Sqrt), ["rs"], ["rs"])
            cx.dve(lambda e: e.reciprocal(out=rs[:, a:b_], in_=rs[:, a:b_]), ["rs"], ["rs"])
            for kc in range(KC):
                cx.dve(lambda e, kc=kc: e.scalar_tensor_tensor(out=dst[:, kc, a:b_], in0=src[:, kc, a:b_], scalar=wt[:, kc:kc + 1], in1=rs[:, a:b_], op0=ALU.mult, op1=ALU.mult), [sname, wname, "rs", dname], [dname])

    rmsnorm(x, "x", nw, "nw", h, "h", tsl)

    def ev_up(pst, pn, mt, sl, k):
        a, b_ = sl
        if a == 0:
            cx.act(lambda e: e.copy(out=upp[:, 0:2], in_=pst), [pn, "upp"], ["upp"])
            return
        cx.act(lambda e: e.copy(out=upp[:, 2:W], in_=pst), [pn, "upp"], ["upp"])
        cx.dve(lambda e: e.tensor_scalar(out=cv[:], in0=upp[:, 2:W], scalar1=cw[:, mt, 2:3], scalar2=None, op0=ALU.mult), ["upp", "cw"], ["cv"])
        cx.dve(lambda e: e.scalar_tensor_tensor(out=cv[:], in0=upp[:, 1:W - 1], scalar=cw[:, mt, 1:2], in1=cv[:], op0=ALU.mult, op1=ALU.add), ["upp", "cw", "cv"], ["cv"])
        cx.dve(lambda e: e.scalar_tensor_tensor(out=cv[:], in0=upp[:, 0:W - 2], scalar=cw[:, mt, 0:1], in1=cv[:], op0=ALU.mult, op1=ALU.add), ["upp", "cw", "cv"], ["cv"])
        if mt < 44:
            cx.act(lambda e: e.activation(out=actb[:, mt, :], in_=cv[:], func=AF.Silu), ["cv", "actb"], ["actb"])
        else:
            cx.dve(lambda e: e.tensor_tensor(out=inter[:, mt - 44, :], in0=cv[:], in1=actb[:, mt - 44, :], op=ALU.mult), ["cv", "actb", "inter"], ["inter"])

    emit_proj2(cx, dup, D, 2 * FF, lambda kc, s: h[:, kc, s], "h", tsl, ev_up, "up")

    def ev_dn(pst, pn, mt, sl, k):
        cx.dve(lambda e: e.tensor_tensor(out=x[:, mt, 2:W], in0=x[:, mt, 2:W], in1=pst, op=ALU.add), [pn, "x"], ["x"])
        if not final:
            cx.store(ox[mt * 128:(mt + 1) * 128, :], "xo", x[:, mt, 2:W], "x")

    emit_proj2(cx, ddn, FF, D, lambda kc, s: inter[:, kc, s], "inter", [(0, Lh)], ev_dn, "dn", gw=128, blocked=True)
    if final:
        rmsnorm(x, "x", fw, "fw", x, "x", [(2, W)])
        for mt in range(KC):
            cx.store(ox[mt * 128:(mt + 1) * 128, :], "xo", x[:, mt, 2:W], "x")
    return cx.done(["xo"])


_PROG = {}
NCORE = 8
SEQ = 8192
LC = SEQ // NCORE
NA = 9216 + 128


def _prog(key, fn):
    if key not in _PROG:
        _PROG[key] = fn()
    return _PROG[key]


def _run(nc, in_maps):
    res = run_bass_kernel_spmd(nc, in_maps, core_ids=list(range(NCORE)))
    return res.results


def _c(a, dt=None):
    return np.ascontiguousarray(a if dt is None else a.astype(dt))


def _pcol(v):
    return _c(v.reshape(-1, 128).T)


def kernel(x, mix_norm_w, w_in, s5_log_dt, s5_a_re, s5_a_im, s5_b_re, s5_b_im, s5_c_re, s5_c_im, s5_d, s5_glu_w,
           dn_conv_w, dn_a_log, dn_dt_bias, dn_norm_w, dn_proj_w, w_out, ffn_norm_w, ffn_up, ffn_conv_w, ffn_down, final_norm_w):
    f32 = np.float32
    XT = _c(np.asarray(x, f32)[0].T)
    depth = w_in.shape[0]
    cst = dn_consts()
    ncA = _prog("A", lambda: build_A(LC, NA))
    s5b = _prog("S5", lambda: build_S5(SEQ))
    ncS5, NS = s5b[0], s5b[2]
    ncF = _prog("DNF", lambda: build_DNF(SEQ)); ncDC = _prog("DNC", lambda: build_DNC(SEQ))
    ncC1 = _prog("C1", lambda: build_C1(512))
    ns_t = _c(np.tile(np.array(NS, f32), (128, 1))); ident = cst["ident"]
    for l in range(depth):
        w = np.asarray(w_in[l], f32)
        wre = np.zeros((D, NA), f32)
        wre[:, 0:5120] = w[:, 0:5120]; wre[:, 5120:9216] = w[:, 5136:9232]; wre[:, 9216:9232] = w[:, 5120:5136]
        nw = _pcol(np.asarray(mix_norm_w[l], f32))
        r = _run(ncA, [{"xT": _c(XT[:, c * LC:(c + 1) * LC]), "nw": nw, "w": wre} for c in range(NCORE)])
        P = np.concatenate([np.asarray(r[c]["pT"]) for c in range(NCORE)], axis=1)
        ins = []
        for c in range(NCORE):
            g0 = 8 * c
            lay = lambda a: _c(np.asarray(a, f32)[g0:g0 + 8].reshape(4, 128).T)
            def zpad(mm_):
                z = np.zeros((128, 4, 128), f32)
                for g in range(8):
                    z[(g % 2) * 64:(g % 2) * 64 + 64, g // 2, g * 16:(g + 1) * 16] = mm_[g]
                return z
            ins.append({"u": _c(P[128 * c:128 * c + 128]), "lr": lay(s5_a_re[l]), "li": lay(s5_a_im[l]),
                        "ldt": lay(np.repeat(np.asarray(s5_log_dt[l], f32)[:, None], 64, 1)),
                        "zb_re": zpad(np.asarray(s5_b_re[l], f32)[g0:g0 + 8]), "zb_im": zpad(np.asarray(s5_b_im[l], f32)[g0:g0 + 8]),
                        "zc_re": zpad(np.asarray(s5_c_re[l], f32)[g0:g0 + 8].transpose(0, 2, 1)), "zc_im": zpad(np.asarray(s5_c_im[l], f32)[g0:g0 + 8].transpose(0, 2, 1)),
                        "dskip": _c(np.asarray(s5_d[l], f32)[128 * c:128 * c + 128, None]), "ns": ns_t, "ident": ident})
        r = _run(ncS5, ins)
        YS = np.concatenate([np.asarray(r[c]["y"]) for c in range(NCORE)], axis=0)
        cwl = np.asarray(dn_conv_w[l], f32)
        ins = []
        for c in range(NCORE):
            b_raw, a_raw = P[9216 + c], P[9216 + 8 + c]
            cw3 = np.stack([cwl[:, 128 * c:128 * c + 128], cwl[:, 1024 + 128 * c:1024 + 128 * c + 128], cwl[:, 2048 + 128 * c:2048 + 128 * c + 128]], 1)
            ins.append({"q": _c(P[1024 + 128 * c:1152 + 128 * c]), "k": _c(P[2048 + 128 * c:2176 + 128 * c]), "v": _c(P[3072 + 128 * c:3200 + 128 * c]),
                        "z": _c(P[4096 + 128 * c:4224 + 128 * c]), "b_raw_bc": _c(np.tile(b_raw, (128, 1))), "a_raw_bc": _c(np.tile(a_raw, (128, 1))),
                        "a_raw_col": _c(a_raw.reshape(SEQ // 128, 128).T), "convw": _c(cw3.transpose(2, 1, 0)),
                        "scal": _c(np.tile(np.array([dn_a_log[l][c], dn_dt_bias[l][c]], f32), (128, 1)))})
        r1 = _run(ncF, ins)
        ins = []
        for c in range(NCORE):
            d = {n: np.asarray(r1[c][n]) for n in ("qn", "kn", "vs", "g_bc", "beta_bc", "g_col", "sz")}
            d["dn_nw"] = _c(np.asarray(dn_norm_w[l], f32)[:, None]); d.update(cst); ins.append(d)
        r = _run(ncDC, ins)
        YD = np.concatenate([np.asarray(r[c]["y_dn"]) for c in range(NCORE)], axis=0)
        glu = np.asarray(s5_glu_w[l], f32)
        glu_ba = _c(np.concatenate([glu[:, 2048:], glu[:, :2048]], 1))
        dpw, wo = _c(np.asarray(dn_proj_w[l], f32)), _c(np.asarray(w_out[l], f32))
        Xmid = np.empty_like(XT)
        for hh in range(2):
            sl = [slice(c * LC + hh * 512, c * LC + hh * 512 + 512) for c in range(NCORE)]
            r = _run(ncC1, [{"xT": _c(XT[:, s]), "ys": _c(YS[:, s]), "yd": _c(YD[:, s]), "gs": _c(P[5120:7168, s]), "gd": _c(P[7168:9216, s]),
                             "glu_w": glu_ba, "dn_proj": dpw, "w_out": wo} for s in sl])
            for c, s in enumerate(sl):
                Xmid[:, s] = np.asarray(r[c]["xo"])
        final = (l == depth - 1)
        ncC2 = _prog("C2f" if final else "C2", lambda: build_C2(512, final))
        fcw = np.asarray(ffn_conv_w[l], f32)
        cw = _c(fcw.reshape(3, 88, 128).transpose(2, 1, 0))
        upw = _c(np.asarray(ffn_up[l], f32))
        dnw_ = _c(np.asarray(ffn_down[l], f32).reshape(FF // 128, 128, D // 128, 128).transpose(2, 1, 0, 3))
        fnw, nw2 = _pcol(np.asarray(final_norm_w, f32)), _pcol(np.asarray(ffn_norm_w[l], f32))
        Xn = np.empty_like(XT)
        for hh in range(2):
            ins, sl = [], []
            for c in range(NCORE):
                t0 = c * LC + hh * 512
                xh = np.zeros((D, 514), f32)
                if t0 >= 2:
                    xh[:, :] = Xmid[:, t0 - 2:t0 + 512]
                else:
                    xh[:, 2:] = Xmid[:, t0:t0 + 512]
                ins.append({"xT": xh, "nw": nw2, "ffn_up": upw, "convw": cw, "ffn_down": dnw_, "fnw": fnw}); sl.append(slice(t0, t0 + 512))
            r = _run(ncC2, ins)
            for c, s in enumerate(sl):
                Xn[:, s] = np.asarray(r[c]["xo"])
        XT = Xn
    return _c(XT.T[None].astype(f32))
```
